# Optimizing a Trainium2 kernel written in Bass

```python
import math
import jax
import jax.numpy as jnp
from jax import lax
import numpy as np

D_MODEL = 1024
BATCH = 2
SEQ = 8192
DEPTH = 4

GRID_W = 64
CTX_LEN = 256
D_MIX = D_MODEL
W_GROUP = D_MIX // 4
EPS = 1e-6
CONV_W = 3
ROPE_BASE = 10000.0

ML_HEADS = 4
ML_DH = W_GROUP // ML_HEADS
ML_CHUNK = 64

MLA_HEADS = 4
MLA_NOPE = 64
MLA_ROPE = 32
MLA_V = W_GROUP // MLA_HEADS
MLA_Q_LORA = 256
MLA_KV_LORA = 128
ATTN_BLOCK = 128

SSD_HEADDIM = 64
SSD_HEADS = W_GROUP // SSD_HEADDIM
SSD_GROUPS = 2
SSD_STATE = 128
SSD_CHUNK = 64

S5_GROUP = 16
S5_NGROUPS = W_GROUP // S5_GROUP
S5_STATE = 64

D_FF = 2816

ML_COLS = 4 * W_GROUP + 4 * ML_HEADS
MLA_COLS = MLA_Q_LORA + MLA_KV_LORA + MLA_ROPE
SSD_COLS = 2 * W_GROUP + 2 * SSD_GROUPS * SSD_STATE + 2 * SSD_HEADS
S5_COLS = W_GROUP
P_IN = ML_COLS + MLA_COLS + SSD_COLS + S5_COLS
SPLIT_IN = (ML_COLS, ML_COLS + MLA_COLS, ML_COLS + MLA_COLS + SSD_COLS)

kernel_name = 'hybrid_parallel_groups_dit_trunk'


def rmsnorm(x, g):
    x32 = x.astype(jnp.float32)
    y = x32 * lax.rsqrt(jnp.mean(x32 * x32, axis=-1, keepdims=True) + EPS)
    return (y * g.astype(jnp.float32)).astype(x.dtype)


def modulate(h, shift, scale):
    return h * (1 + scale) + shift


def dwconv(x, w, b=None):
    pad = (CONV_W - 1) // 2
    y = lax.conv_general_dilated(x, w[:, None, :].astype(x.dtype), window_strides=(1,),
                                 padding=((pad, pad),), dimension_numbers=('NWC', 'WIO', 'NWC'),
                                 feature_group_count=x.shape[-1])
    return y if b is None else y + b


def axial_rope_tables(n_tokens, dim):
    rows = n_tokens // GRID_W
    row = jnp.broadcast_to(jnp.arange(rows)[:, None], (rows, GRID_W)).reshape(-1).astype(jnp.float32)
    col = jnp.broadcast_to(jnp.arange(GRID_W)[None, :], (rows, GRID_W)).reshape(-1).astype(jnp.float32)
    n_freq = dim // 4
    inv = ROPE_BASE ** (-jnp.arange(n_freq, dtype=jnp.float32) / n_freq)
    ang = jnp.concatenate([row[:, None] * inv, col[:, None] * inv], axis=-1)
    return jnp.cos(ang), jnp.sin(ang)


def apply_rope(x, cos, sin):
    x1, x2 = jnp.split(x.astype(jnp.float32), 2, axis=-1)
    cs, sn = cos[:, None, :], sin[:, None, :]
    return jnp.concatenate([x1 * cs - x2 * sn, x1 * sn + x2 * cs], axis=-1).astype(x.dtype)


def mlstm_chunked(q, k, v, ig, lf, state):
    bsz, nh, T, dh = q.shape
    nc = T // ML_CHUNK

    def to_chunks(a):
        return jnp.moveaxis(a.reshape(a.shape[:2] + (nc, ML_CHUNK) + a.shape[3:]), 2, 0)

    mask = jnp.tril(jnp.ones((ML_CHUNK, ML_CHUNK), dtype=bool))

    def step(carry, blk):
        C, n, m = carry
        qb, kb, vb, ib, fb = blk
        b = jnp.cumsum(fb, axis=-1)
        dmat = jnp.where(mask, b[..., :, None] - b[..., None, :] + ib[..., None, :], -jnp.inf)
        inter = b + m[..., None]
        m_t = jnp.maximum(inter, jnp.max(dmat, axis=-1))
        w_inter = jnp.exp(inter - m_t)
        s = jnp.einsum('bhtd,bhsd->bhts', qb, kb) * jnp.exp(dmat - m_t[..., None])
        num = jnp.einsum('bhts,bhse->bhte', s, vb) + w_inter[..., None] * jnp.einsum('bhtd,bhde->bhte', qb, C)
        den = jnp.sum(s, axis=-1) + w_inter * jnp.einsum('bhtd,bhd->bht', qb, n)
        h = num / jnp.maximum(jnp.abs(den), jnp.exp(-m_t))[..., None]
        b_last = b[..., -1]
        w_log = b_last[..., None] - b + ib
        m_new = jnp.maximum(b_last + m, jnp.max(w_log, axis=-1))
        decay = jnp.exp(b_last + m - m_new)
        w_s = jnp.exp(w_log - m_new[..., None])
        C = decay[..., None, None] * C + jnp.einsum('bhs,bhsd,bhse->bhde', w_s, kb, vb)
        n = decay[..., None] * n + jnp.einsum('bhs,bhsd->bhd', w_s, kb)
        return (C, n, m_new), h

    state, h = lax.scan(step, state, tuple(to_chunks(a) for a in (q, k, v, ig, lf)))
    return jnp.moveaxis(h, 0, 2).reshape(bsz, nh, T, dh), state


def mlstm_mixer(pc, pl, gate_bias, norm_g):
    def prep(p):
        q, k, v, o, g = jnp.split(p, [W_GROUP, 2 * W_GROUP, 3 * W_GROUP, 4 * W_GROUP], axis=-1)
        bsz, T = p.shape[:2]

        def heads(a):
            return jnp.moveaxis(a.astype(jnp.float32).reshape(bsz, T, ML_HEADS, ML_DH), 2, 1)

        g = jnp.moveaxis(g.astype(jnp.float32).reshape(bsz, T, 4, ML_HEADS) + gate_bias, 1, 3)
        return heads(q), heads(k) * ML_DH ** -0.5, heads(v), o, g

    qc, kc, vc, oc, gc = prep(pc)
    ql, kl, vl, ol, gl = prep(pl)
    bsz = pl.shape[0]
    zero = (jnp.zeros((bsz, ML_HEADS, ML_DH, ML_DH), jnp.float32),
            jnp.zeros((bsz, ML_HEADS, ML_DH), jnp.float32),
            jnp.zeros((bsz, ML_HEADS), jnp.float32))
    flip = lambda a: jnp.flip(a, axis=2)
    hc_sum, hl_sum = 0.0, 0.0
    for d in range(2):
        ctx_args = (qc, kc, vc, gc[:, 2 * d], jax.nn.log_sigmoid(gc[:, 2 * d + 1]))
        lat_args = (ql, kl, vl, gl[:, 2 * d], jax.nn.log_sigmoid(gl[:, 2 * d + 1]))
        if d == 1:
            ctx_args = tuple(flip(a) for a in ctx_args)
            lat_args = tuple(flip(a) for a in lat_args)
        hc, st = mlstm_chunked(*ctx_args, zero)
        hl, _ = mlstm_chunked(*lat_args, st)
        if d == 1:
            hc, hl = flip(hc), flip(hl)
        hc_sum = hc_sum + hc
        hl_sum = hl_sum + hl

    def out(h, o):
        bsz_, _, T, _ = h.shape
        h = rmsnorm(jnp.moveaxis(h, 1, 2), norm_g.reshape(ML_HEADS, ML_DH)).reshape(bsz_, T, W_GROUP)
        return (h * jax.nn.sigmoid(o.astype(jnp.float32))).astype(o.dtype)

    return out(hc_sum, oc), out(hl_sum, ol)


def mla_mixer(pc, pl, q_norm, kv_norm, w_uq, w_ukv, cos, sin):
    scale = (MLA_NOPE + MLA_ROPE) ** -0.5

    def project(p, rotate):
        cq, ckv, kr = jnp.split(p, [MLA_Q_LORA, MLA_Q_LORA + MLA_KV_LORA], axis=-1)
        bsz, T = p.shape[:2]
        q = (rmsnorm(cq, q_norm) @ w_uq).reshape(bsz, T, MLA_HEADS, MLA_NOPE + MLA_ROPE)
        kv = (rmsnorm(ckv, kv_norm) @ w_ukv).reshape(bsz, T, MLA_HEADS, MLA_NOPE + MLA_V)
        q_nope, q_rope = jnp.split(q, [MLA_NOPE], axis=-1)
        k_nope, v = jnp.split(kv, [MLA_NOPE], axis=-1)
        kr = kr[:, :, None, :]
        if rotate:
            q_rope = apply_rope(q_rope, cos, sin)
            kr = apply_rope(kr, cos, sin)
        q = jnp.concatenate([q_nope, q_rope], axis=-1)
        k = jnp.concatenate([k_nope, jnp.broadcast_to(kr, k_nope.shape[:-1] + (MLA_ROPE,))], axis=-1)
        return q, k, v

    def attend(q, k, v):
        s = jnp.einsum('bqhd,bkhd->bhqk', q, k).astype(jnp.float32) * scale
        p = jax.nn.softmax(s, axis=-1).astype(v.dtype)
        return jnp.einsum('bhqk,bkhd->bqhd', p, v)

    qc, kc, vc = project(pc, False)
    ql, kl, vl = project(pl, True)
    bsz, Tc = pc.shape[:2]
    T = pl.shape[1]
    yc = attend(qc, kc, vc).reshape(bsz, Tc, W_GROUP)
    k_all = jnp.concatenate([kl, kc], axis=1)
    v_all = jnp.concatenate([vl, vc], axis=1)
    nb = T // ATTN_BLOCK
    qb = jnp.moveaxis(ql.reshape(bsz, nb, ATTN_BLOCK, MLA_HEADS, MLA_NOPE + MLA_ROPE), 1, 0)
    yl = lax.map(lambda qq: attend(qq, k_all, v_all), qb)
    yl = jnp.moveaxis(yl, 0, 1).reshape(bsz, T, W_GROUP)
    return yc, yl


def ssd_chunked(q, k, v, la, S):
    bsz, nh, T, _ = q.shape
    nc = T // SSD_CHUNK

    def to_chunks(a):
        return jnp.moveaxis(a.reshape(a.shape[:2] + (nc, SSD_CHUNK) + a.shape[3:]), 2, 0)

    mask = jnp.tril(jnp.ones((SSD_CHUNK, SSD_CHUNK), dtype=bool))

    def step(S, blk):
        qb, kb, vb, lb = blk
        cs = jnp.cumsum(lb, axis=-1)
        decay = jnp.exp(jnp.where(mask, cs[..., :, None] - cs[..., None, :], -jnp.inf))
        y = jnp.einsum('bhts,bhsp->bhtp', jnp.einsum('bhtn,bhsn->bhts', qb, kb) * decay, vb) \
            + jnp.exp(cs)[..., None] * jnp.einsum('bhtn,bhnp->bhtp', qb, S)
        w_s = jnp.exp(cs[..., -1:] - cs)
        S = jnp.exp(cs[..., -1])[..., None, None] * S + jnp.einsum('bhs,bhsn,bhsp->bhnp', w_s, kb, vb)
        return S, y

    S, y = lax.scan(step, S, tuple(to_chunks(a) for a in (q, k, v, la)))
    return jnp.moveaxis(y, 0, 2).reshape(bsz, nh, T, v.shape[-1]), S


def ssd_mixer(pc, pl, conv_w, conv_b, a_log, dt_bias, d_skip, norm_g):
    gn = SSD_GROUPS * SSD_STATE
    rep = SSD_HEADS // SSD_GROUPS

    def prep(p):
        z, xbc, dt = jnp.split(p, [W_GROUP, 2 * W_GROUP + 2 * gn], axis=-1)
        xbc = jax.nn.silu(dwconv(xbc, conv_w, conv_b)).astype(jnp.float32)
        xs, bm, cm = jnp.split(xbc, [W_GROUP, W_GROUP + gn], axis=-1)
        bsz, T = p.shape[:2]
        xs = jnp.moveaxis(xs.reshape(bsz, T, SSD_HEADS, SSD_HEADDIM), 2, 1)
        bm = jnp.repeat(jnp.moveaxis(bm.reshape(bsz, T, SSD_GROUPS, SSD_STATE), 2, 1), rep, axis=1)
        cm = jnp.repeat(jnp.moveaxis(cm.reshape(bsz, T, SSD_GROUPS, SSD_STATE), 2, 1), rep, axis=1)
        dt = jax.nn.softplus(dt.astype(jnp.float32).reshape(bsz, T, 2, SSD_HEADS) + dt_bias)
        return z, xs, bm, cm, jnp.moveaxis(dt, 1, 3)

    zc, xc, bc, cc, dtc = prep(pc)
    zl, xl, bl, cl, dtl = prep(pl)
    bsz = pl.shape[0]
    zero = jnp.zeros((bsz, SSD_HEADS, SSD_STATE, SSD_HEADDIM), jnp.float32)
    flip = lambda a: jnp.flip(a, axis=2)
    yc = d_skip[:, None, None] * xc
    yl = d_skip[:, None, None] * xl
    for d in range(2):
        A = -jnp.exp(a_log[d])[:, None]
        ctx_args = (cc, bc * dtc[:, d, ..., None], xc, dtc[:, d] * A)
        lat_args = (cl, bl * dtl[:, d, ..., None], xl, dtl[:, d] * A)
        if d == 1:
            ctx_args = tuple(flip(a) for a in ctx_args)
            lat_args = tuple(flip(a) for a in lat_args)
        hc, S = ssd_chunked(*ctx_args, zero)
        hl, _ = ssd_chunked(*lat_args, S)
        if d == 1:
            hc, hl = flip(hc), flip(hl)
        yc = yc + hc
        yl = yl + hl

    def out(y, z):
        bsz_, _, T, _ = y.shape
        y = jnp.moveaxis(y, 1, 2).reshape(bsz_, T, W_GROUP)
        return rmsnorm(y * jax.nn.silu(z.astype(jnp.float32)), norm_g).astype(z.dtype)

    return out(yc, zc), out(yl, zl)


def diag_scan(ab_re, ab_im, bu_re, bu_im, x0_re, x0_im):
    bu_re = bu_re.at[:, 0].add(ab_re * x0_re - ab_im * x0_im)
    bu_im = bu_im.at[:, 0].add(ab_re * x0_im + ab_im * x0_re)
    a_re = jnp.broadcast_to(ab_re, bu_re.shape)
    a_im = jnp.broadcast_to(ab_im, bu_im.shape)

    def op(e1, e2):
        a1r, a1i, b1r, b1i = e1
        a2r, a2i, b2r, b2i = e2
        return (a2r * a1r - a2i * a1i, a2r * a1i + a2i * a1r,
                a2r * b1r - a2i * b1i + b2r, a2r * b1i + a2i * b1r + b2i)

    _, _, xr, xi = lax.associative_scan(op, (a_re, a_im, bu_re, bu_im), axis=1)
    return xr, xi


def s5_mixer(uc, ul, a_re, a_im, log_dt, b_re, b_im, c_re, c_im, d_skip, w_glu):
    def groups(u):
        return u.astype(jnp.float32).reshape(u.shape[0], u.shape[1], S5_NGROUPS, S5_GROUP)

    def readout(s_re, s_im):
        return jnp.einsum('gjn,btgn->btgj', c_re, s_re) - jnp.einsum('gjn,btgn->btgj', c_im, s_im)

    ugc, ugl = groups(uc), groups(ul)
    zero = jnp.zeros((uc.shape[0], S5_NGROUPS, S5_STATE), jnp.float32)
    dg = d_skip.reshape(S5_NGROUPS, S5_GROUP)
    yc = dg * ugc
    yl = dg * ugl
    for d in range(2):
        lam_re = jnp.minimum(a_re[d], -1e-4)
        lam_im = a_im[d]
        dt = jnp.exp(log_dt[d])[:, None]
        mag = jnp.exp(lam_re * dt)
        ab_re, ab_im = mag * jnp.cos(lam_im * dt), mag * jnp.sin(lam_im * dt)
        den = lam_re * lam_re + lam_im * lam_im
        f_re = ((ab_re - 1) * lam_re + ab_im * lam_im) / den
        f_im = (ab_im * lam_re - (ab_re - 1) * lam_im) / den
        bb_re = f_re[..., None] * b_re - f_im[..., None] * b_im
        bb_im = f_re[..., None] * b_im + f_im[..., None] * b_re

        def drive(ug):
            return (jnp.einsum('gnj,btgj->btgn', bb_re, ug), jnp.einsum('gnj,btgj->btgn', bb_im, ug))

        uc_d, ul_d = (ugc, ugl) if d == 0 else (jnp.flip(ugc, 1), jnp.flip(ugl, 1))
        sc_re, sc_im = diag_scan(ab_re, ab_im, *drive(uc_d), zero, zero)
        sl_re, sl_im = diag_scan(ab_re, ab_im, *drive(ul_d), sc_re[:, -1], sc_im[:, -1])
        rc, rl = readout(sc_re, sc_im), readout(sl_re, sl_im)
        if d == 1:
            rc, rl = jnp.flip(rc, 1), jnp.flip(rl, 1)
        yc = yc + rc
        yl = yl + rl

    def glu(y, like):
        y = jax.nn.gelu(y.reshape(y.shape[0], y.shape[1], W_GROUP))
        return (y * jax.nn.sigmoid(y @ w_glu.astype(jnp.float32))).astype(like.dtype)

    return glu(yc, uc), glu(yl, ul)


def conv_ffn(h, w_up, conv_w, w_down):
    u, g = jnp.split(h @ w_up, 2, axis=-1)
    return (jax.nn.silu(dwconv(g, conv_w)) * u) @ w_down


def setup_inputs(seed: int = 0) -> dict:
    key = jax.random.key(seed)
    ks = iter(jax.random.split(key, 48))
    f32 = jnp.float32
    L, D = DEPTH, D_MODEL

    def nrm(shape, scale):
        return scale * jax.random.normal(next(ks), shape, f32)

    def gain(shape):
        return 1.0 + 0.02 * jax.random.normal(next(ks), shape, f32)

    def unif(shape, lo, hi):
        return jax.random.uniform(next(ks), shape, f32, lo, hi)

    ssd_conv_ch = W_GROUP + 2 * SSD_GROUPS * SSD_STATE
    fgate = jnp.linspace(3.0, 6.0, ML_HEADS, dtype=f32)
    igate = jnp.zeros((ML_HEADS,), f32)
    ml_gate_bias = jnp.stack([igate, fgate, igate, fgate])[None] + nrm((L, 4, ML_HEADS), 0.1)
    dt0 = jnp.exp(unif((L, 2, SSD_HEADS), math.log(1e-3), math.log(1e-1)))
    return {
        'x': nrm((BATCH, SEQ, D), 1.0),
        'c': nrm((BATCH, D), 1.0),
        'ctx': nrm((BATCH, CTX_LEN, D), 1.0),
        'c_ctx': nrm((D,), 1.0),
        'w_mod': nrm((L, D, 6 * D), 0.5 * D ** -0.5),
        'b_mod': nrm((L, 6 * D), 0.01),
        'norm1': gain((L, D)),
        'norm2': gain((L, D)),
        'w_in': nrm((L, D, P_IN), D ** -0.5),
        'ml_gate_bias': ml_gate_bias,
        'ml_norm': gain((L, W_GROUP)),
        'mla_q_norm': gain((L, MLA_Q_LORA)),
        'mla_kv_norm': gain((L, MLA_KV_LORA)),
        'mla_w_uq': nrm((L, MLA_Q_LORA, MLA_HEADS * (MLA_NOPE + MLA_ROPE)), MLA_Q_LORA ** -0.5),
        'mla_w_ukv': nrm((L, MLA_KV_LORA, MLA_HEADS * (MLA_NOPE + MLA_V)), MLA_KV_LORA ** -0.5),
        'ssd_conv_w': nrm((L, CONV_W, ssd_conv_ch), CONV_W ** -0.5),
        'ssd_conv_b': nrm((L, ssd_conv_ch), 0.01),
        'ssd_a_log': jnp.log(unif((L, 2, SSD_HEADS), 1.0, 16.0)),
        'ssd_dt_bias': dt0 + jnp.log(-jnp.expm1(-dt0)),
        'ssd_d': gain((L, SSD_HEADS)),
        'ssd_norm': gain((L, W_GROUP)),
        's5_a_re': -0.5 + nrm((L, 2, S5_NGROUPS, S5_STATE), 0.01),
        's5_a_im': jnp.broadcast_to(math.pi * jnp.arange(S5_STATE, dtype=f32), (L, 2, S5_NGROUPS, S5_STATE)),
        's5_log_dt': unif((L, 2, S5_NGROUPS), math.log(1e-3), math.log(1e-1)),
        's5_b_re': nrm((L, S5_NGROUPS, S5_STATE, S5_GROUP), (2 * S5_GROUP) ** -0.5),
        's5_b_im': nrm((L, S5_NGROUPS, S5_STATE, S5_GROUP), (2 * S5_GROUP) ** -0.5),
        's5_c_re': nrm((L, S5_NGROUPS, S5_GROUP, S5_STATE), S5_STATE ** -0.5),
        's5_c_im': nrm((L, S5_NGROUPS, S5_GROUP, S5_STATE), S5_STATE ** -0.5),
        's5_d': gain((L, W_GROUP)),
        's5_w_glu': nrm((L, W_GROUP, W_GROUP), W_GROUP ** -0.5),
        'w_out': nrm((L, D_MIX, D), D_MIX ** -0.5),
        'ffn_w_up': nrm((L, D, 2 * D_FF), D ** -0.5),
        'ffn_conv_w': nrm((L, CONV_W, D_FF), CONV_W ** -0.5),
        'ffn_w_down': nrm((L, D_FF, D), D_FF ** -0.5),
        'final_norm': gain((D,)),
    }


def reference(x, c, ctx, c_ctx, w_mod, b_mod, norm1, norm2, w_in, ml_gate_bias, ml_norm,
              mla_q_norm, mla_kv_norm, mla_w_uq, mla_w_ukv, ssd_conv_w, ssd_conv_b, ssd_a_log,
              ssd_dt_bias, ssd_d, ssd_norm, s5_a_re, s5_a_im, s5_log_dt, s5_b_re, s5_b_im,
              s5_c_re, s5_c_im, s5_d, s5_w_glu, w_out, ffn_w_up, ffn_conv_w, ffn_w_down, final_norm):
    cos, sin = axial_rope_tables(x.shape[1], MLA_ROPE)
    s_lat = jax.nn.silu(c)
    s_ctx = jax.nn.silu(c_ctx)
    for l in range(DEPTH):
        ml = jnp.split(s_lat @ w_mod[l] + b_mod[l], 6, axis=-1)
        mc = jnp.split(s_ctx @ w_mod[l] + b_mod[l], 6, axis=-1)
        hl = modulate(rmsnorm(x, norm1[l]), ml[0][:, None], ml[1][:, None])
        hc = modulate(rmsnorm(ctx, norm1[l]), mc[0], mc[1])
        pl = jnp.split(hl @ w_in[l], SPLIT_IN, axis=-1)
        pc = jnp.split(hc @ w_in[l], SPLIT_IN, axis=-1)
        ya_c, ya_l = mlstm_mixer(pc[0], pl[0], ml_gate_bias[l], ml_norm[l])
        yb_c, yb_l = mla_mixer(pc[1], pl[1], mla_q_norm[l], mla_kv_norm[l], mla_w_uq[l], mla_w_ukv[l], cos, sin)
        yc_c, yc_l = ssd_mixer(pc[2], pl[2], ssd_conv_w[l], ssd_conv_b[l], ssd_a_log[l], ssd_dt_bias[l],
                               ssd_d[l], ssd_norm[l])
        yd_c, yd_l = s5_mixer(pc[3], pl[3], s5_a_re[l], s5_a_im[l], s5_log_dt[l], s5_b_re[l], s5_b_im[l],
                              s5_c_re[l], s5_c_im[l], s5_d[l], s5_w_glu[l])
        y_lat = jnp.concatenate([ya_l, yb_l, yc_l, yd_l], axis=-1) @ w_out[l]
        x = x + ml[2][:, None] * y_lat
        h2 = modulate(rmsnorm(x, norm2[l]), ml[3][:, None], ml[4][:, None])
        x = x + ml[5][:, None] * conv_ffn(h2, ffn_w_up[l], ffn_conv_w[l], ffn_w_down[l])
        if l < DEPTH - 1:
            y_ctx = jnp.concatenate([ya_c, yb_c, yc_c, yd_c], axis=-1) @ w_out[l]
            ctx = ctx + mc[2] * y_ctx
            hc2 = modulate(rmsnorm(ctx, norm2[l]), mc[3], mc[4])
            ctx = ctx + mc[5] * conv_ffn(hc2, ffn_w_up[l], ffn_conv_w[l], ffn_w_down[l])
    return rmsnorm(x, final_norm)
```

```python
import numpy as np
import concourse.bass as bass
import concourse.mybir as mybir
from concourse.bass_utils import run_bass_kernel_spmd
from contextlib import ExitStack

F32 = mybir.dt.float32
BF16 = mybir.dt.bfloat16
I32 = mybir.dt.int32
AF = mybir.ActivationFunctionType
ALU = mybir.AluOpType
AX = mybir.AxisListType

NDMASEM = 8
NCCSEM = 16
EPS = 1e-6


class _Op:
    __slots__ = ("eng", "fn", "deps", "id", "ticket", "dma", "has_dep", "pre", "sem", "cc", "info")


class Prog:
    ENGS = ("pe", "act", "dve", "pool", "sp")

    def __init__(self, nc):
        self.nc = nc
        self.st = ExitStack()
        self.ops = []
        self.state = {}
        self.ntile = 0
        self.evi = 0
        self.bar = set()
        self.arena = None
        self.aoff = 0
        self.last_by_eng = {}
        self.dma_recent = {}

    def sb(self, shape, dtype=F32, name=None):
        self.ntile += 1
        name = name or "t"
        return self.st.enter_context(self.nc.sbuf_tensor(f"{name}_{self.ntile}", list(shape), dtype))

    def ps(self, shape, dtype=F32, name=None):
        self.ntile += 1
        name = name or "p"
        return self.st.enter_context(self.nc.psum_tensor(f"{name}_{self.ntile}", list(shape), dtype))

    def arena_init(self, nbytes):
        self.arena = self.sb([128, nbytes // 4], F32, "arena")
        self.acap = nbytes
        self.aoff = 0

    def al(self, shape, dtype=F32, name=None):
        esz = 4 if dtype in (F32, I32) else 2
        nfree = 1
        for d_ in shape[1:]:
            nfree *= d_
        nb = (nfree * esz + 3) // 4 * 4
        assert self.aoff + nb <= self.acap, f"arena overflow {self.aoff + nb} > {self.acap}"
        v = self.arena[0:shape[0], self.aoff // 4:(self.aoff + nb) // 4]
        self.aoff += nb
        if esz == 2:
            v = v.bitcast(dtype)
        if len(shape) == 3:
            v = v.rearrange("p (a b) -> p a b", b=shape[2])
        elif len(shape) == 4:
            v = v.rearrange("p (a b c) -> p a b c", b=shape[2], c=shape[3])
        return v

    def mark(self):
        return self.aoff

    def release(self, mark):
        self.aoff = mark
        bar = set(self.last_by_eng.values())
        for lst in self.dma_recent.values():
            bar.update(lst)
        self.bar = bar

    def add(self, eng, fn, reads=(), writes=(), dma=False, cc=False):
        op = _Op()
        op.eng = eng
        op.fn = fn
        op.id = len(self.ops)
        op.dma = dma
        op.cc = cc
        op.info = (tuple(reads), tuple(writes))
        op.has_dep = False
        op.ticket = None
        op.pre = None
        op.sem = None
        deps = set()
        for k in reads:
            s = self.state.setdefault(k, [None, []])
            if s[0] is not None:
                deps.add(s[0])
        for k in writes:
            s = self.state.setdefault(k, [None, []])
            if s[0] is not None:
                deps.add(s[0])
            deps.update(s[1])
        for k in reads:
            self.state[k][1].append(op.id)
        for k in writes:
            s = self.state[k]
            s[0] = op.id
            s[1] = []
        deps.update(self.bar)
        deps.discard(op.id)
        op.deps = deps
        self.ops.append(op)
        if cc:
            self.dma_recent.setdefault("cc", []).append(op.id)
        elif dma:
            lst = self.dma_recent.setdefault(eng, [])
            lst.append(op.id)
            if len(lst) > NDMASEM:
                lst.pop(0)
        else:
            self.last_by_eng[eng] = op.id
        return op

    def mm(self, out, lhsT, rhs, start=True, stop=True, reads=(), writes=(), **kw):
        return self.add("pe", lambda e: e.matmul(out, lhsT, rhs, start=start, stop=stop, **kw), reads, writes)

    def tr(self, out, in_, ident, reads=(), writes=()):
        return self.add("pe", lambda e: e.transpose(out, in_, ident), reads, writes)

    def actf(self, out, in_, func, bias=None, scale=1.0, accum_out=None, reads=(), writes=()):
        kw = {}
        if bias is not None:
            kw["bias"] = bias
        if accum_out is not None:
            kw["accum_out"] = accum_out
        return self.add("act", lambda e: e.activation(out, in_, func, scale=scale, **kw), reads, writes)

    def dma(self, out, in_, reads=(), writes=(), eng="sp", **kw):
        return self.add(eng, lambda e: e.dma_start(out=out, in_=in_, **kw), reads, writes, dma=True)

    def v(self, eng, name, *args, reads=(), writes=(), **kw):
        return self.add(eng, lambda e: getattr(e, name)(*args, **kw), reads, writes)

    def cast(self, out, in_, reads=(), writes=()):
        self.cvi = getattr(self, "cvi", 0) + 1
        m = self.cvi % 6
        if m in (0, 2, 4):
            return self.actf(out, in_, AF.Identity, reads=reads, writes=writes)
        if m in (1, 3):
            return self.v("dve", "tensor_copy", out, in_, reads=reads, writes=writes)
        return self.v("pool", "tensor_copy", out, in_, reads=reads, writes=writes)

    def evac(self, out, in_, scale=None, reads=(), writes=()):
        self.evi += 1
        if self.evi % 2 == 0:
            return self.actf(out, in_, AF.Identity, scale=(1.0 if scale is None else scale), reads=reads, writes=writes)
        if scale is None:
            return self.v("dve", "tensor_copy", out, in_, reads=reads, writes=writes)
        return self.v("dve", "tensor_scalar_mul", out, in_, scale, reads=reads, writes=writes)

    def emit(self):
        nc = self.nc
        ops = self.ops
        for op in ops:
            for d in op.deps:
                p = ops[d]
                if p.eng == "pe" and op.eng == "pe" and not p.dma and not op.dma:
                    continue
                p.has_dep = True
        cnt = {e: 0 for e in self.ENGS}
        dcnt = {e: 0 for e in self.ENGS}
        ncc = 0
        for op in ops:
            if op.cc:
                op.sem = ("cc", ncc % NCCSEM)
                op.ticket = ncc // NCCSEM + 1
                op.pre = (op.sem, op.ticket - 1) if op.ticket > 1 else None
                ncc += 1
            elif op.dma:
                i = dcnt[op.eng]
                dcnt[op.eng] += 1
                slot = i % NDMASEM
                val = 16 * (i // NDMASEM + 1)
                op.sem = ("d", op.eng, slot)
                op.ticket = val
                op.pre = (op.sem, val - 16) if val > 16 else None
            elif op.has_dep:
                cnt[op.eng] += 1
                op.sem = ("c", op.eng)
                op.ticket = cnt[op.eng]
        sems = {}
        for e in self.ENGS:
            sems[("c", e)] = self.st.enter_context(nc.semaphore(f"c_{e}"))
            if dcnt[e]:
                for s in range(NDMASEM):
                    sems[("d", e, s)] = self.st.enter_context(nc.semaphore(f"d_{e}_{s}"))
        for i in range(min(ncc, NCCSEM)):
            sems[("cc", i)] = self.st.enter_context(nc.semaphore(f"cc_{i}"))
        by_eng = {e: [op for op in ops if op.eng == e] for e in self.ENGS}
        self.stats = {e: len(by_eng[e]) for e in self.ENGS}

        def run(E, e):
            known = {}

            def wait(sk, val):
                if known.get(sk, 0) >= val:
                    return
                e.wait_ge(sems[sk], val)
                known[sk] = val

            last_dma = {}
            for op in by_eng[E]:
                if op.pre is not None:
                    wait(*op.pre)
                for d in sorted(op.deps):
                    p = ops[d]
                    if p.ticket is None:
                        continue
                    if (not p.dma) and (not op.dma) and p.eng == "pe" and E == "pe":
                        continue
                    wait(p.sem, p.ticket)
                try:
                    ins = op.fn(e)
                except Exception:
                    print("EMIT FAILED at op", op.id, op.eng, op.info, flush=True)
                    raise
                if op.cc:
                    ins.then_inc(sems[op.sem], 1)
                    last_dma[op.sem] = op.ticket
                elif op.dma:
                    ins.then_inc(sems[op.sem], 16)
                    last_dma[op.sem] = op.ticket
                elif op.ticket is not None:
                    ins.then_inc(sems[op.sem], 1)
            for sk, val in last_dma.items():
                wait(sk, val)

        with nc.Block() as block:
            @block.tensor
            def _(e):
                run("pe", e)

            @block.scalar
            def _(e):
                run("act", e)

            @block.vector
            def _(e):
                run("dve", e)

            @block.gpsimd
            def _(e):
                run("pool", e)

            @block.sync
            def _(e):
                run("sp", e)
        self.st.close()


D = 1024
KC = 8
B_ = 2
SEQ = 8192
CTX = 256
NLAT = 2048
NCTX = 64
NT = NLAT + NCTX
P_IN = 2744
DFF = 2816
NFF = 22
R_ML = 0
R_SS = 1040
R_Q = 2328
R_KN = 2712
R_V = 2968
R_KR = 3224
NPT = 3256


def _dram(nc, name, shape, kind="ExternalInput", dtype=F32):
    return nc.dram_tensor(name, list(shape), dtype, kind=kind).ap()


def _rms_rstd(P, ps_ss, n, nfeat, rstd, key_ps, key_out, tmp, key_tmp):
    P.v("dve", "tensor_scalar", tmp[:, :n], ps_ss[:, :n], 1.0 / nfeat, EPS, ALU.mult, ALU.add,
        reads=[key_ps], writes=[key_tmp])
    P.actf(tmp[:, :n], tmp[:, :n], AF.Sqrt, reads=[key_tmp], writes=[key_tmp])
    P.v("dve", "reciprocal", rstd[:, :n], tmp[:, :n], reads=[key_tmp], writes=[key_out])


def _mod_vectors(P, wmod, bmod_sb, cs, parts, wst, wst_keys, psb, psb_key, modv):
    for j, part in enumerate(parts):
        for k in range(KC):
            buf = k % 2
            P.dma(wst[buf][:, 0:1024], wmod[k * 128:(k + 1) * 128, part * 1024:(part + 1) * 1024],
                  writes=[wst_keys[buf]])
            for dc in range(KC):
                P.mm(psb[:, 2 * dc:2 * dc + 2], wst[buf][:, dc * 128:(dc + 1) * 128], cs[:, k, :],
                     start=(k == 0 and dc == 0), stop=(k == KC - 1), skip_group_check=True,
                     reads=[wst_keys[buf], "cs"], writes=[psb_key])
        for dc in range(KC):
            P.v("dve", "tensor_scalar", modv[:, j, dc, :], psb[:, 2 * dc:2 * dc + 2],
                bmod_sb[:, part * 8 + dc:part * 8 + dc + 1], None, ALU.add,
                reads=[psb_key, "bmod"], writes=["modv"])


def phase_A(P, E, l):
    m0 = P.mark()
    banks = E["banks"]
    xT = E["XS"][l % 2]
    wmod, bmod, norm1 = E["wmodA"][l], E["bmodA"][l], E["norm1"][l]
    win, winsw, qn, kvn = E["win"][l], E["winsw"][l], E["qn"][l], E["kvn"][l]
    wuq, wuqsw, wukv = E["wuq"][l], E["wuqsw"][l], E["wukv"][l]
    ropeC, ropeS = E["ropeC"], E["ropeS"]
    EXf = _flat(E["EXA"])
    PTl = EXf[A_LAT_OFF:A_LAT_OFF + NPT * NLAT].rearrange("(r w) -> r w", w=NLAT)
    PTc = EXf[A_CTX_OFF:A_CTX_OFF + NPT * NCTX].rearrange("(r w) -> r w", w=NCTX)

    def PTdst(r0, r1, c0, n, seg):
        return PTl[r0:r1, c0:c0 + n] if seg == 0 else PTc[r0:r1, 0:n]
    ones_b, cs = E["ones_b"], E["cs"]
    bmod_sb = P.al([128, 16], F32)
    P.dma(bmod_sb[:], bmod, writes=["bmod"])
    n1 = P.al([128, KC], F32)
    P.dma(n1[:], norm1, writes=["n1"])
    qn_sb = P.al([128, 2], F32)
    P.dma(qn_sb[:], qn, writes=["qn"])
    kvn_sb = P.al([128, 1], F32)
    P.dma(kvn_sb[:], kvn, writes=["kvn"])
    tabC = P.al([96, NT], F32)
    tabS = P.al([96, NT], F32)
    P.dma(tabC[64:96, :], ropeC, writes=["tab"])
    P.dma(tabS[64:96, :], ropeS, writes=["tab"])
    P.dma(tabC[0:32, :], ropeC, writes=["tab"])
    P.dma(tabS[0:32, :], ropeS, writes=["tab"])

    wst = [P.al([128, P_IN], F32, f"wst{i}") for i in range(2)]
    wst_keys = ["wst0", "wst1"]
    psb = banks[6]
    modv = P.al([128, 2, KC, 2], F32, "modv")
    _mod_vectors(P, wmod, bmod_sb, cs, [0, 1], wst, wst_keys, psb, "bank6", modv)
    Amod = P.al([128, KC, 2], F32, "Amod")
    P.v("dve", "tensor_scalar_add", Amod[:], modv[:, 1, :, :], 1.0, reads=["modv"], writes=["Amod"])
    for s_ in range(2):
        P.v("dve", "tensor_tensor", Amod[:, :, s_], Amod[:, :, s_], n1[:], ALU.mult,
            reads=["Amod", "n1"], writes=["Amod"])

    win_b = P.al([128, KC, P_IN], BF16, "winb")
    winsw_b = P.al([128, KC, 32], BF16, "winswb")
    for k in range(KC):
        buf = k % 2
        P.dma(wst[buf][:, :], win[k * 128:(k + 1) * 128, :], writes=[wst_keys[buf]])
        P.cast(win_b[:, k, :], wst[buf][:, :], reads=[wst_keys[buf]], writes=["winb"])
    sm = P.al([128, KC, 32], F32, "smallst")
    P.dma(sm[:], winsw.rearrange("(k p) c -> p k c", p=128), writes=["smallst"])
    P.v("pool", "tensor_copy", winsw_b[:], sm[:], reads=["smallst"], writes=["winb"])
    wuq_b = P.al([128, 2, 384], BF16, "wuqb")
    wuqsw_b = P.al([128, 2, 384], BF16, "wuqswb")
    wukv_b = P.al([128, 512], BF16, "wukvb")
    st2 = P.al([128, 2, 384], F32, "st2")
    P.dma(st2[:], wuq.rearrange("(k p) c -> p k c", p=128), writes=["st2"])
    P.v("pool", "tensor_copy", wuq_b[:], st2[:], reads=["st2"], writes=["wmla"])
    P.dma(st2[:], wuqsw.rearrange("(k p) c -> p k c", p=128), writes=["st2"])
    P.v("pool", "tensor_copy", wuqsw_b[:], st2[:], reads=["st2"], writes=["wmla"])
    P.dma(st2[:, 0, :], wukv[:, 0:384], writes=["st2"])
    P.dma(st2[:, 1, 0:128], wukv[:, 384:512], writes=["st2"])
    P.v("pool", "tensor_copy", wukv_b[:, 0:384], st2[:, 0, :], reads=["st2"], writes=["wmla"])
    P.v("pool", "tensor_copy", wukv_b[:, 384:512], st2[:, 1, 0:128], reads=["st2"], writes=["wmla"])

    xt = [P.al([128, KC, 512], F32, f"xt{i}") for i in range(2)]
    xsq = P.al([128, KC, 512], BF16, "xsq")
    hT = P.al([128, KC, 512], BF16, "hT")
    tmpn = P.al([128, 512], F32, "tmpn")
    rstd = P.al([128, 512], F32, "rstd")
    tmpx = [P.al([128, 512], F32, f"tmpx{i}") for i in range(2)]
    stage = [P.al([128, 512], F32, f"stg{i}") for i in range(4)]
    cq = P.al([128, 2, 512], F32, "cq")
    ckv = P.al([128, 512], F32, "ckv")
    krr = P.al([32, 2, 512], F32, "krr")
    cqn = P.al([128, 2, 512], BF16, "cqn")
    ckvn = P.al([128, 512], BF16, "ckvn")
    sq2 = P.al([128, 2, 512], BF16, "sq2")
    rt1 = P.al([96, 512], F32, "rt1")
    rt2 = P.al([96, 512], F32, "rt2")
    pss = banks[7]
    psm = banks[0:4]
    psq = banks[4:6]

    tiles = [(i * 512, 512, 0) for i in range(4)] + [(NLAT, NCTX, 1)]
    nstage = 0
    nps = 0
    for ti, (c0, n, seg) in enumerate(tiles):
        xb = xt[ti % 2]
        xk = f"xt{ti % 2}"
        P.dma(xb[:, :, :n], xT[:, c0:c0 + n].rearrange("(k p) n -> p k n", p=128), reads=[f"XS{l % 2}"], writes=[xk])
        P.actf(xsq[:, :, :n], xb[:, :, :n], AF.Square, reads=[xk], writes=["xsq"])
        for k in range(KC):
            P.mm(pss[:, :n], ones_b[:], xsq[:, k, :n], start=(k == 0), stop=(k == KC - 1),
                 reads=["ones", "xsq"], writes=["bank7"])
        _rms_rstd(P, pss, n, D, rstd, "bank7", "rstd", tmpn, "tmpn")
        for k in range(KC):
            tb = tmpx[k % 2]
            tk = f"tmpx{k % 2}"
            P.v("dve", "tensor_tensor", tb[:, :n], xb[:, k, :n], rstd[:, :n], ALU.mult,
                reads=[xk, "rstd"], writes=[tk])
            P.actf(hT[:, k, :n], tb[:, :n], AF.Identity, bias=modv[:, 0, k, seg:seg + 1],
                   scale=Amod[:, k, seg:seg + 1], reads=[tk, "Amod", "modv"], writes=["hT"])

        def inproj(col0, m, wsrc=None):
            nonlocal nps
            ps = psm[nps % 4]
            pk = f"bank{nps % 4}"
            nps += 1
            for k in range(KC):
                lhs = win_b[:, k, col0:col0 + m] if wsrc is None else wsrc[:, k, 0:m]
                P.mm(ps[:m, :n], lhs, hT[:, k, :n], start=(k == 0), stop=(k == KC - 1),
                     reads=["winb", "hT"], writes=[pk])
            return ps, pk

        def out_chunk(ps, pk, m, row0, scale=None):
            nonlocal nstage
            sg = stage[nstage % 4]
            sk = f"stg{nstage % 4}"
            nstage += 1
            P.evac(sg[:m, :n], ps[:m, :n], scale=scale, reads=[pk], writes=[sk])
            P.dma(PTdst(row0, row0 + m, c0, n, seg), sg[:m, :n], reads=[sk], writes=["PT"])

        for i in range(9):
            col0 = i * 128
            m = 128 if i < 8 else 16
            ps, pk = inproj(col0, m)
            out_chunk(ps, pk, m, R_ML + col0, scale=(0.125 if i in (2, 3) else None))
        for i in range(11):
            col0 = 1456 + i * 128
            m = 128 if i < 10 else 8
            ps, pk = inproj(col0, m)
            out_chunk(ps, pk, m, R_SS + i * 128)
        for j in range(2):
            ps, pk = inproj(1040 + j * 128, 128)
            P.evac(cq[:, j, :n], ps[:, :n], reads=[pk], writes=["cq"])
        ps, pk = inproj(1296, 128)
        P.evac(ckv[:, :n], ps[:, :n], reads=[pk], writes=["ckv"])
        ps, pk = inproj(1424, 32)
        P.evac(krr[:, 0, :n], ps[:32, :n], reads=[pk], writes=["krr"])
        ps, pk = inproj(0, 32, wsrc=winsw_b)
        P.evac(krr[:, 1, :n], ps[:32, :n], reads=[pk], writes=["krr"])
        P.actf(sq2[:, :, :n], cq[:, :, :n], AF.Square, reads=["cq"], writes=["sq2"])
        for j in range(2):
            P.mm(pss[:, :n], ones_b[:], sq2[:, j, :n], start=(j == 0), stop=(j == 1),
                 reads=["ones", "sq2"], writes=["bank7"])
        _rms_rstd(P, pss, n, 256, rstd, "bank7", "rstd", tmpn, "tmpn")
        for j in range(2):
            tb = tmpx[j % 2]
            tk = f"tmpx{j % 2}"
            P.v("dve", "tensor_tensor", tb[:, :n], cq[:, j, :n], rstd[:, :n], ALU.mult,
                reads=["cq", "rstd"], writes=[tk])
            P.actf(cqn[:, j, :n], tb[:, :n], AF.Identity, scale=qn_sb[:, j:j + 1], reads=[tk, "qn"], writes=["cqn"])
        P.actf(sq2[:, 0, :n], ckv[:, :n], AF.Square, reads=["ckv"], writes=["sq2"])
        P.mm(pss[:, :n], ones_b[:], sq2[:, 0, :n], reads=["ones", "sq2"], writes=["bank7"])
        _rms_rstd(P, pss, n, 128, rstd, "bank7", "rstd", tmpn, "tmpn")
        P.v("dve", "tensor_tensor", tmpx[0][:, :n], ckv[:, :n], rstd[:, :n], ALU.mult,
            reads=["ckv", "rstd"], writes=["tmpx0"])
        P.actf(ckvn[:, :n], tmpx[0][:, :n], AF.Identity, scale=kvn_sb[:, 0:1], reads=["tmpx0", "kvn"], writes=["ckvn"])
        for h in range(4):
            pa, pb = psq[0], psq[1]
            for j in range(2):
                P.mm(pa[:96, :n], wuq_b[:, j, h * 96:(h + 1) * 96], cqn[:, j, :n], start=(j == 0), stop=(j == 1),
                     reads=["wmla", "cqn"], writes=["bank4"])
            for j in range(2):
                P.mm(pb[:96, :n], wuqsw_b[:, j, h * 96:(h + 1) * 96], cqn[:, j, :n], start=(j == 0), stop=(j == 1),
                     reads=["wmla", "cqn"], writes=["bank5"])
            sg = stage[nstage % 4]
            sk = f"stg{nstage % 4}"
            nstage += 1
            P.actf(sg[0:64, :n], pa[0:64, :n], AF.Identity, reads=["bank4"], writes=[sk])
            P.v("dve", "tensor_tensor", rt1[64:96, :n], pa[64:96, :n], tabC[64:96, c0:c0 + n], ALU.mult,
                reads=["bank4", "tab"], writes=["rt1"])
            P.v("dve", "tensor_tensor", rt2[64:96, :n], pb[64:96, :n], tabS[64:96, c0:c0 + n], ALU.mult,
                reads=["bank5", "tab"], writes=["rt2"])
            P.v("pool", "tensor_tensor", sg[64:96, :n], rt1[64:96, :n], rt2[64:96, :n], ALU.add,
                reads=["rt1", "rt2"], writes=[sk])
            P.dma(PTdst(R_Q + h * 96, R_Q + (h + 1) * 96, c0, n, seg), sg[:96, :n], reads=[sk], writes=["PT"])
        for c in range(2):
            for which, row0 in ((0, R_KN), (1, R_V)):
                ps = psm[nps % 4]
                pk = f"bank{nps % 4}"
                nps += 1
                P.mm(ps[:, :n], wukv_b[:, which * 256 + c * 128:which * 256 + (c + 1) * 128], ckvn[:, :n],
                     reads=["wmla", "ckvn"], writes=[pk])
                out_chunk(ps, pk, 128, row0 + c * 128)
        sg = stage[nstage % 4]
        sk = f"stg{nstage % 4}"
        nstage += 1
        P.v("dve", "tensor_tensor", rt1[0:32, :n], krr[:, 0, :n], tabC[0:32, c0:c0 + n], ALU.mult,
            reads=["krr", "tab"], writes=["rt1"])
        P.v("dve", "tensor_tensor", rt2[0:32, :n], krr[:, 1, :n], tabS[0:32, c0:c0 + n], ALU.mult,
            reads=["krr", "tab"], writes=["rt2"])
        P.v("pool", "tensor_tensor", sg[0:32, :n], rt1[0:32, :n], rt2[0:32, :n], ALU.add,
            reads=["rt1", "rt2"], writes=[sk])
        P.dma(PTdst(R_KR, R_KR + 32, c0, n, seg), sg[:32, :n], reads=[sk], writes=["PT"])
    P.release(m0)


TALL = CTX + SEQ
NCH = TALL // 64
NJ = TALL // 128
CT = 512
EXT = NLAT + 2 + NCTX + 2
LAT0, LAT1 = 0, NLAT + 2
CX0, CX1 = NLAT + 2, EXT


def _col_tiles(a, b, w=CT):
    out = []
    c = a
    while c < b:
        out.append((c, min(w, b - c)))
        c += w
    return out


CHA = 262144
NCHA = 27
RA_PAD = NCHA * CHA
A_LAT_OFF, A_CTX_OFF = 0, NPT * NLAT
Y_PIECES = [(0, 512), (512, 512), (1024, 2), (1024, 512), (1536, 512), (2048, 2), (2050, 66)]
CHY = 15 * 16896
Y_OFF = []
_o = 0
for (_c, _w) in Y_PIECES:
    _o = -(-_o // 16896) * 16896
    Y_OFF.append(_o)
    _o += D * _w
NCHY = -(-_o // CHY)
RY_PAD = NCHY * CHY
assert NPT * NT <= RA_PAD and A_CTX_OFF % NCTX == 0 and CHA % NLAT == 0


def _gaddr(f, rank, ch):
    return (f // ch) * 4 * ch + rank * ch + (f % ch)


def _flat(ap):
    return ap.rearrange("a b -> (a b)")


SSR = R_SS
OP_ROWS = {
    "m_QT": [(R_Q, R_Q + 384)], "m_KT": [(R_KN, R_KN + 256), (R_KR, R_KR + 32)], "m_V": [(R_V, R_V + 256)],
    "l_q": [(0, 256)], "l_k": [(256, 512)], "l_v": [(512, 768)], "l_o": [(768, 1024)],
    "l_g": [(1024, 1040)], "l_grow1": [(1024, 1040)], "l_grow3": [(1024, 1040)],
    "s_x": [(SSR + 256, SSR + 512)], "s_B": [(SSR + 512, SSR + 768)], "s_C": [(SSR + 768, SSR + 1024)], "s_z": [(SSR, SSR + 256)],
    "s_dt": [(SSR + 1024, SSR + 1032)], "s_dtrow0": [(SSR + 1024, SSR + 1032)], "s_dtrow1": [(SSR + 1024, SSR + 1032)],
    "s5_u0": [(SSR + 1032, SSR + 1288)], "s5_u1": [(SSR + 1032, SSR + 1288)],
}


def _op_chunks(op):
    cs = set()
    for (a, b) in OP_ROWS[op]:
        for r in (a, b - 1):
            pass
        cs.update(range((a * NLAT) // CHA, ((b - 1) * NLAT) // CHA + 1))
        cs.update(range((A_CTX_OFF + a * NCTX) // CHA, (A_CTX_OFF + (b - 1) * NCTX) // CHA + 1))
    return sorted(cs)


OPS_B = ["m_QT", "m_KT", "m_V", "l_q", "l_k", "l_v", "l_o", "l_g", "l_grow1", "l_grow3",
         "s_x", "s_B", "s_C", "s_z", "s_dt", "s_dtrow0", "s_dtrow1", "s5_u0", "s5_u1"]
OPIDX = {n: i for i, n in enumerate(OPS_B)}
NIB = 8 * len(OPS_B)


def gather_fm(P, E, dst, op, nr, key):
    it = E["idxB"]
    Gf = _flat(E["GEXA"])
    for qp in range(4):
        for pc, (o0, o1, w, eoff) in enumerate(((CTX + NLAT * qp, CTX + NLAT * (qp + 1), NLAT, 0),
                                                (NCTX * qp, NCTX * (qp + 1), NCTX, 0))):
            c = (OPIDX[op] * 4 + qp) * 2 + pc
            src = Gf.rearrange("(r w) -> r w", w=w)
            P.add("pool", (lambda c, o0, o1, src, eoff: (lambda e: e.indirect_dma_start(
                out=dst[0:nr, o0:o1], out_offset=None, in_=src,
                in_offset=bass.IndirectOffsetOnAxis(ap=it[0:nr, c:c + 1], axis=0), element_offset=eoff)))(c, o0, o1, src, eoff),
                reads=[f"GEXA{c_}" for c_ in _op_chunks(op)] + ["idxB"], writes=[key], dma=True)


def fm_to_tok(P, E, src, r, dst, key_src, key_dst):
    B, ident = E["banks"], E["ident"]
    for g in range(0, NJ, 4):
        ps, pk = B[6 + (g // 4) % 2], f"bank{6 + (g // 4) % 2}"
        nj = min(4, NJ - g)
        for jj in range(nj):
            j = g + jj
            P.tr(ps[:, 128 * jj:128 * jj + r], src[0:r, 128 * j:128 * j + 128], ident[0:r, 0:r], reads=[key_src, "ident"], writes=[pk])
        view = ps[:, 0:128 * nj].rearrange("p (j e) -> p j e", e=128)[:, :, 0:r]
        P.evac(dst[:, g:g + nj, 0:r], view, reads=[pk], writes=[key_dst])


def tok_to_fm(P, E, src, dst, key_src, key_dst):
    B, ident = E["banks"], E["ident"]
    for g in range(0, NJ, 4):
        ps, pk = B[6 + (g // 4) % 2], f"bank{6 + (g // 4) % 2}"
        nj = min(4, NJ - g)
        for jj in range(nj):
            P.tr(ps[0:64, 128 * jj:128 * jj + 128], src[:, g + jj, :], ident[:, :], reads=[key_src, "ident"], writes=[pk])
        P.evac(dst[0:64, 128 * g:128 * (g + nj)], ps[0:64, 0:128 * nj], reads=[pk], writes=[key_dst])


def store_cols(P, E, row0, nr, src_fn, col0, n, key):
    Yf = _flat(E["EXY"])
    a, b = col0, col0 + n
    for q in range(4):
        segs = []
        lo, hi = max(a, CTX + NLAT * q - 1, CTX), min(b, CTX + NLAT * (q + 1) + 1, TALL)
        if lo < hi:
            segs.append((lo, hi, lo - (CTX + NLAT * q - 1)))
        lo, hi = max(a, NCTX * q - 1, 0), min(b, NCTX * (q + 1) + 1, CTX)
        if lo < hi:
            segs.append((lo, hi, CX0 + lo - (NCTX * q - 1)))
        for (lo, hi, e0) in segs:
            for pi, (pc0, pw) in enumerate(Y_PIECES):
                x0, x1 = max(e0, pc0), min(e0 + (hi - lo), pc0 + pw)
                if x0 < x1:
                    piece = Yf[Y_OFF[pi]:Y_OFF[pi] + D * pw].rearrange("(r w) -> r w", w=pw)
                    P.dma(piece[q * 256 + row0:q * 256 + row0 + nr, x0 - pc0:x1 - pc0], src_fn(lo + (x0 - e0), lo + (x1 - e0)),
                          reads=[key], writes=["EXY"], allow_slow_non_contiguous=True)


def b_mla(P, E, l, C, after_first_gather=None):
    m0 = P.mark()
    QT = P.al([97, TALL], BF16)
    KT = P.al([97, TALL], BF16)
    Vt = P.al([128, NJ, 65], BF16)
    kst = P.al([96, TALL], F32)
    sq = P.al([96, 2112], BF16)
    kmx = P.al([128, 20], F32)
    nkm = P.al([128, 1], F32)
    tmpq = P.al([128, 512], F32)
    pT = [P.al([128, 512], BF16) for _ in range(4)]
    drow = P.al([65, 512], F32)
    rden = P.al([64, 512], F32)
    ost = [P.al([64, 512], F32) for _ in range(2)]
    ones_b, ones_f = C["ones_b"], C["ones_f"]
    B = C["banks"]
    scale = 96.0 ** -0.5
    P.v("pool", "memset", KT[96:97, :], 1.0, writes=["KT"])
    P.v("pool", "memset", kmx[:], 0.0, writes=["kmx"])
    ti = 0
    gather_fm(P, E, kst, "m_KT", 96, "mkst")
    if after_first_gather is not None:
        after_first_gather()
    for ci in range(4):
        c0 = ci * 2112
        sb_, sk = kst[:, c0:c0 + 2112], "mkst"
        P.v("dve", "tensor_copy", KT[0:96, c0:c0 + 2112], sb_[0:96, :], reads=[sk], writes=["KT"])
        P.actf(sq[:, :], sb_[0:96, :], AF.Square, reads=[sk], writes=["msq"])
        for (t0, n) in _col_tiles(0, 2112):
            ps, pk = B[6 + ti % 2], f"bank{6 + ti % 2}"
            P.mm(ps[:, :n], ones_b[0:96, :], sq[:, t0:t0 + n], reads=["ones", "msq"], writes=[pk])
            P.v("dve", "reduce_max", kmx[:, ti:ti + 1], ps[:, :n], AX.X, reads=[pk], writes=["kmx"])
            ti += 1
    P.v("dve", "reduce_max", nkm[:], kmx[:], AX.X, reads=["kmx"], writes=["nkm"])
    P.actf(nkm[:], nkm[:], AF.Sqrt, reads=["nkm"], writes=["nkm"])
    P.v("dve", "tensor_scalar_mul", nkm[:], nkm[:], -1.0, reads=["nkm"], writes=["nkm"])
    ti = 0
    gather_fm(P, E, kst, "m_QT", 96, "mkst")
    for ci in range(4):
        c0 = ci * 2112
        sb_, sk = kst[:, c0:c0 + 2112], "mkst"
        P.v("dve", "tensor_copy", QT[0:96, c0:c0 + 2112], sb_[0:96, :], reads=[sk], writes=["QT"])
        P.actf(sq[:, :], sb_[0:96, :], AF.Square, reads=[sk], writes=["msq"])
        for (t0, n) in _col_tiles(0, 2112):
            ps, pk = B[6 + ti % 2], f"bank{6 + ti % 2}"
            P.mm(ps[:, :n], ones_b[0:96, :], sq[:, t0:t0 + n], reads=["ones", "msq"], writes=[pk])
            P.actf(tmpq[96:97, :n], ps[96:97, :n], AF.Sqrt, reads=[pk], writes=["tmpq"])
            P.v("dve", "tensor_scalar_mul", QT[96:97, c0 + t0:c0 + t0 + n], tmpq[96:97, :n], nkm[96:97, 0:1],
                reads=["tmpq", "nkm"], writes=["QT"])
            ti += 1
    gather_fm(P, E, kst, "m_V", 64, "mkst")
    fm_to_tok(P, E, kst, 64, Vt, "mkst", "Vt")
    P.v("pool", "memset", Vt[:, :, 64:65], 1.0, writes=["Vt"])
    qtiles = [(0, 256, [0, 1])] + [(c0, n, list(range(NJ))) for (c0, n) in _col_tiles(256, TALL)]
    it = 0
    LOOK = 2
    for qi, (q0, n, kbs) in enumerate(qtiles):
        po, pok = B[4 + qi % 2], f"bank{4 + qi % 2}"
        slots = []

        def score(kb):
            nonlocal it
            ps, pk = B[it % 4], f"bank{it % 4}"
            pt, ptk = pT[it % 4], f"pT{it % 4}"
            it += 1
            P.mm(ps[:, :n], KT[:, kb * 128:(kb + 1) * 128], QT[:, q0:q0 + n], reads=["KT", "QT"], writes=[pk])
            P.actf(pt[:, :n], ps[:, :n], AF.Exp, scale=scale, reads=[pk], writes=[ptk])
            slots.append((pt, ptk))

        for ki in range(min(LOOK, len(kbs))):
            score(kbs[ki])
        for ki, kb in enumerate(kbs):
            if ki + LOOK < len(kbs):
                score(kbs[ki + LOOK])
            pt, ptk = slots[ki]
            P.mm(po[0:65, :n], Vt[:, kb, :], pt[:, :n], start=(ki == 0), stop=(ki == len(kbs) - 1),
                 reads=["Vt", ptk], writes=[pok])
        P.v("dve", "tensor_copy", drow[64:65, :n], po[64:65, :n], reads=[pok], writes=["drow"])
        pb, pbk = B[6 + qi % 2], f"bank{6 + qi % 2}"
        P.mm(pb[0:64, :n], ones_f[64:65, 0:64], drow[64:65, :n], reads=["onesf", "drow"], writes=[pbk])
        P.v("dve", "reciprocal", rden[:, :n], pb[0:64, :n], reads=[pbk], writes=["rden"])
        o, ok_ = ost[qi % 2], f"most{qi % 2}"
        P.v("dve", "tensor_tensor", o[:, :n], po[0:64, :n], rden[:, :n], ALU.mult, reads=[pok, "rden"], writes=[ok_])
        store_cols(P, E, 64, 64, (lambda a, b, o=o, q0=q0: o[:, a - q0:b - q0]), q0, n, ok_)
    P.release(m0)


def b_consts(P):
    C = {}
    C["banks"] = [P.ps([128, 512], F32, f"bank{i}") for i in range(8)]
    ones_b = P.sb([128, 128], BF16, "ones_b")
    ones_f = P.sb([128, 128], F32, "ones_f")
    ident = P.sb([128, 128], F32, "ident")
    triF = P.sb([128, 128], F32, "triF")
    triB = P.sb([128, 128], F32, "triB")
    mTF = P.sb([128, 64], F32, "mTF")
    mTB = P.sb([128, 64], F32, "mTB")
    mRF = P.sb([128, 512], F32, "mRF")
    mRB = P.sb([128, 512], F32, "mRB")
    P.v("pool", "memset", ones_b[:], 1.0, writes=["ones"])
    P.v("pool", "memset", ones_f[:], 1.0, writes=["onesf"])
    P.v("pool", "memset", ident[:], 0.0, writes=["ident"])
    P.add("pool", lambda e: e.affine_select(out=ident[:], in_=ident[:], pattern=[[-1, 128]], compare_op=ALU.not_equal,
                                            fill=1.0, base=0, channel_multiplier=1), reads=["ident"], writes=["ident"])
    for t, key, cm, st in ((triF, "triF", -1, 1), (triB, "triB", 1, -1)):
        P.v("pool", "memset", t[:], 0.0, writes=[key])
        for h in range(2):
            blk = t[64 * h:64 * h + 64, 64 * h:64 * h + 64]
            P.v("pool", "memset", blk, 1.0, reads=[key], writes=[key])
            P.add("pool", (lambda blk, cm, st: (lambda e: e.affine_select(out=blk, in_=blk, pattern=[[st, 64]],
                  compare_op=ALU.is_ge, fill=0.0, base=0, channel_multiplier=cm)))(blk, cm, st), reads=[key], writes=[key])
    for t, key, cm, st in ((mTF, "mTF", -1, 1), (mTB, "mTB", 1, -1)):
        P.v("pool", "memset", t[:], 0.0, writes=[key])
        for h in range(2):
            blk = t[64 * h:64 * h + 64, :]
            P.add("pool", (lambda blk, cm, st: (lambda e: e.affine_select(out=blk, in_=blk, pattern=[[st, 64]],
                  compare_op=ALU.is_ge, fill=-30000.0, base=0, channel_multiplier=cm)))(blk, cm, st), reads=[key], writes=[key])
    P.v("pool", "memset", mRF[:], 1.0, writes=["mRF"])
    P.v("pool", "memset", mRF[:, 0::64], 0.0, reads=["mRF"], writes=["mRF"])
    P.v("pool", "memset", mRB[:], 1.0, writes=["mRB"])
    P.v("pool", "memset", mRB[:, 63::64], 0.0, reads=["mRB"], writes=["mRB"])
    C.update(ones_b=ones_b, ones_f=ones_f, ident=ident, triF=triF, triB=triB, mTF=mTF, mTB=mTB, mRF=mRF, mRB=mRB)
    return C


def dla(P, C, dv1, qT, kT, ktok, get_vb, make_gates, finish_dir, tag, NUM, kNUM, accum=False):
    B = C["banks"]
    dk = 128
    brow = P.al([128, TALL], F32)
    lftok = P.al([128, NJ], F32)
    igtok = P.al([128, NJ], F32)
    negb = P.al([128, NJ], F32)
    wtok = P.al([128, NJ], F32)
    dec = P.al([128, NCH], F32)
    qTb = P.al([dk, TALL], BF16)
    Crun = [P.al([dk, dv1], F32) for _ in range(4)]
    Dt = [P.al([128, 64], F32) for _ in range(4)]
    Dm = [P.al([128, 64], F32) for _ in range(4)]
    pTt = [P.al([128, 64], BF16) for _ in range(4)]
    etmp = [P.al([128, 512], F32) for _ in range(2)]
    K = lambda s: f"{tag}_{s}"
    mdir = P.mark()
    for d in range(2):
        rev = d == 1
        tri = C["triB"] if rev else C["triF"]
        mT = C["mTB"] if rev else C["mTF"]
        mR = C["mRB"] if rev else C["mRF"]
        lfrow = P.al([128, TALL], F32)
        has_ig = make_gates(d, lfrow, lftok, igtok, K("lfrow"), K("lftok"), K("igtok"))
        vb, vbk = get_vb(d)
        for ti, (c0, n) in enumerate(_col_tiles(0, TALL)):
            o_, a_, b_ = brow[:, c0:c0 + n], mR[:, :n], lfrow[:, c0:c0 + n]
            if rev:
                o_, a_, b_ = o_[:, ::-1], a_[:, ::-1], b_[:, ::-1]
            P.v("dve", "tensor_tensor_scan", o_, a_, b_, 0.0, ALU.mult, ALU.add,
                reads=[K("lfrow"), "mR"], writes=[K("brow")])
            et, ek = etmp[ti % 2], K(f"etmp{ti % 2}")
            P.actf(et[:, :n], brow[:, c0:c0 + n], AF.Exp, reads=[K("brow")], writes=[ek])
            P.v("dve", "tensor_tensor", qTb[:, c0:c0 + n], qT[:, c0:c0 + n], et[:, :n], ALU.mult,
                reads=[K("qT"), ek], writes=[K("qTb")])
        P.release(mdir)
        kw = P.al([128, NJ, dk], BF16)
        Call = P.al([dk, NCH, dv1], BF16)
        P.actf(dec[:, :], brow[:, (0 if rev else 63)::64], AF.Exp, reads=[K("brow")], writes=[K("dec")])
        pm, pmk = B[6], "bank6"
        P.mm(pm[:, 0:NJ], tri[:, :], lftok[:, :], reads=["tri", K("lftok")], writes=[pmk])
        if has_ig:
            P.v("dve", "tensor_tensor", negb[:, :], igtok[:, :], pm[:, 0:NJ], ALU.subtract, reads=[pmk, K("igtok")], writes=[K("negb")])
        else:
            P.v("dve", "tensor_scalar_mul", negb[:, :], pm[:, 0:NJ], -1.0, reads=[pmk], writes=[K("negb")])
        for h in range(2):
            off = 64 * h + (0 if rev else 63)
            P.v("dve", "tensor_tensor", wtok[64 * h:64 * h + 64, :], negb[64 * h:64 * h + 64, :],
                brow[64 * h:64 * h + 64, off::128], ALU.add, reads=[K("negb"), K("brow")], writes=[K("wtok")])
        P.actf(wtok[:, :], wtok[:, :], AF.Exp, reads=[K("wtok")], writes=[K("wtok")])
        P.v("dve", "tensor_tensor", kw[:, :, :], ktok[:, :, :], wtok[:, :].unsqueeze(2).to_broadcast([128, NJ, dk]), ALU.mult,
            reads=[K("ktok"), K("wtok")], writes=[K("kw")])
        P.v("pool", "memset", Crun[0][:, :], 0.0, writes=[K("Crun0")])
        order = list(range(NCH)) if not rev else [3, 2, 1, 0] + list(range(NCH - 1, 3, -1))

        def slots(it, c):
            pb = 64 * (c % 2)
            sl = (it // 2) % 7
            s8 = (it // 2) % 8
            par = 3 * (c % 2)
            psU = B[par + 2][0:dk, 65 * sl:65 * sl + dv1]
            psN = B[par + 1][pb:pb + 64, 65 * sl:65 * sl + dv1]
            psS = B[par + 0][pb:pb + 64, 64 * s8:64 * s8 + 64]
            return pb, c // 2, slice(64 * c, 64 * c + 64), it % 4, psU, psN, psS, f"psU{par}_{sl}", f"psN{par}_{sl}", f"psS{par}_{s8}"

        def stage1(it, c):
            pb, j, cols, r4, psU, psN, psS, kU, kN, kS = slots(it, c)
            P.mm(psU, kw[pb:pb + 64, j, :], vb[pb:pb + 64, j, :], reads=[K("kw"), vbk], writes=[kU])
            P.mm(psS, kT[:, cols], qT[:, cols], reads=[K("kT"), K("qT")], writes=[kS])
            P.v("pool", "tensor_tensor", Dm[r4][pb:pb + 64, :], brow[pb:pb + 64, cols], mT[pb:pb + 64, :], ALU.add,
                reads=[K("brow"), "mT"], writes=[K(f"Dm{r4}")])
            P.actf(Dt[r4][pb:pb + 64, :], Dm[r4][pb:pb + 64, :], AF.Exp, bias=negb[pb:pb + 64, j:j + 1],
                   reads=[K(f"Dm{r4}"), K("negb")], writes=[K(f"Dt{r4}")])
            P.v("dve", "tensor_tensor", pTt[r4][pb:pb + 64, :], psS, Dt[r4][pb:pb + 64, :], ALU.mult,
                reads=[kS, K(f"Dt{r4}")], writes=[K(f"pTt{r4}")])

        def stage2(it, c):
            pb, j, cols, r4, psU, psN, psS, kU, kN, kS = slots(it, c)
            a, b = it % 4, (it + 1) % 4
            P.actf(Call[:, c, :], Crun[a][:, :], AF.Identity, reads=[K(f"Crun{a}")], writes=[K(f"Call{c % 16}")])
            P.v("dve", "scalar_tensor_tensor", Crun[b][:, :], Crun[a][:, :], dec[:, c:c + 1], psU, ALU.mult, ALU.add,
                reads=[K(f"Crun{a}"), K("dec"), kU], writes=[K(f"Crun{b}")])

        def stage3(it, c):
            pb, j, cols, r4, psU, psN, psS, kU, kN, kS = slots(it, c)
            P.mm(psN, pTt[r4][pb:pb + 64, :], vb[pb:pb + 64, j, :], start=True, stop=False,
                 reads=[K(f"pTt{r4}"), vbk], writes=[kN])
            P.mm(psN, qTb[:, cols], Call[:, c, :], start=False, stop=True,
                 reads=[K("qTb"), K(f"Call{c % 16}")], writes=[kN])
            if accum and d == 1:
                P.v("dve", "tensor_tensor", NUM[pb:pb + 64, j, :], NUM[pb:pb + 64, j, :], psN, ALU.add, reads=[kN, kNUM], writes=[kNUM])
            else:
                P.evac(NUM[pb:pb + 64, j, :], psN, reads=[kN], writes=[kNUM])

        LOOK = 2
        for it in range(min(LOOK, len(order))):
            stage1(it, order[it])
        for it, c in enumerate(order):
            if it + LOOK < len(order):
                stage1(it + LOOK, order[it + LOOK])
            stage2(it, c)
            stage3(it, c)
        finish_dir(d, NUM, kNUM)
        P.release(mdir)


def _softplus(P, ap, bias_ap, key, deps=()):
    P.actf(ap, ap, AF.Exp, bias=bias_ap, reads=[key] + list(deps), writes=[key])
    P.actf(ap, ap, AF.Ln, bias=1.0, reads=[key], writes=[key])


def b_mlstm(P, E, l, C):
    m0 = P.mark()
    qT = P.al([128, TALL], BF16)
    kT = P.al([128, TALL], BF16)
    ktok = P.al([128, NJ, 128], BF16)
    vb = P.al([128, NJ, 65], BF16)
    gtok = P.al([128, NJ, 4], F32)
    gb = P.al([128, 4], F32)
    ngb = P.al([128, 4], F32)
    H = P.al([128, NJ, 64], F32)
    rd = P.al([128, NJ], F32)
    m1 = P.mark()
    stg = P.al([128, TALL], F32)
    P.dma(gb[:, :], E["l_gbias"][l].partition_broadcast(128), writes=["l_gb"])
    P.v("dve", "tensor_scalar_mul", ngb[:, :], gb[:, :], -1.0, reads=["l_gb"], writes=["l_ngb"])
    for op, dst, key in (("l_q", qT, "l_qT"), ("l_k", kT, "l_kT")):
        P.v("pool", "memset", dst[64:128, :], 0.0, writes=[key])
        gather_fm(P, E, stg, op, 64, "l_stg")
        P.v("dve", "tensor_copy", dst[0:64, :], stg[0:64, :], reads=["l_stg"], writes=[key])
        if op == "l_k":
            P.v("pool", "memset", ktok[:, :, 64:128], 0.0, writes=["l_ktok"])
            fm_to_tok(P, E, stg, 64, ktok, "l_stg", "l_ktok")
    gather_fm(P, E, stg, "l_v", 64, "l_stg")
    fm_to_tok(P, E, stg, 64, vb, "l_stg", "l_vb")
    P.v("pool", "memset", vb[:, :, 64:65], 1.0, writes=["l_vb"])
    gather_fm(P, E, stg, "l_g", 4, "l_stg")
    fm_to_tok(P, E, stg, 4, gtok, "l_stg", "l_gtok")

    def make_gates(d, lfrow, lftok, igtok, klr, klt, kit):
        gather_fm(P, E, lfrow, f"l_grow{2 * d + 1}", 128, klr)
        P.actf(lfrow[:, :], lfrow[:, :], AF.Exp, bias=ngb[:, 2 * d + 1:2 * d + 2], scale=-1.0, reads=[klr, "l_ngb"], writes=[klr])
        P.actf(lfrow[:, :], lfrow[:, :], AF.Ln, bias=1.0, reads=[klr], writes=[klr])
        P.v("dve", "tensor_scalar_mul", lfrow[:, :], lfrow[:, :], -1.0, reads=[klr], writes=[klr])
        P.actf(lftok[:, :], gtok[:, :, 2 * d + 1], AF.Exp, bias=ngb[:, 2 * d + 1:2 * d + 2], scale=-1.0, reads=["l_gtok", "l_ngb"], writes=[klt])
        P.actf(lftok[:, :], lftok[:, :], AF.Ln, bias=1.0, reads=[klt], writes=[klt])
        P.v("dve", "tensor_scalar_mul", lftok[:, :], lftok[:, :], -1.0, reads=[klt], writes=[klt])
        P.v("dve", "tensor_scalar", igtok[:, :], gtok[:, :, 2 * d], gb[:, 2 * d:2 * d + 1], None, ALU.add, reads=["l_gtok", "l_gb"], writes=[kit])
        return True

    def finish_dir(d, NUM, kn):
        P.actf(rd[:, :], NUM[:, :, 64], AF.Abs, reads=[kn], writes=["l_rd"])
        P.v("dve", "tensor_scalar_max", rd[:, :], rd[:, :], 1.0, reads=["l_rd"], writes=["l_rd"])
        P.v("dve", "reciprocal", rd[:, :], rd[:, :], reads=["l_rd"], writes=["l_rd"])
        rb = rd[:, :].unsqueeze(2).to_broadcast([128, NJ, 64])
        if d == 0:
            P.v("dve", "tensor_tensor", H[:, :, :], NUM[:, :, 0:64], rb, ALU.mult, reads=[kn, "l_rd"], writes=["l_H"])
        else:
            P.v("dve", "tensor_tensor", NUM[:, :, 0:64], NUM[:, :, 0:64], rb, ALU.mult, reads=[kn, "l_rd"], writes=[kn])
            P.v("dve", "tensor_tensor", H[:, :, :], H[:, :, :], NUM[:, :, 0:64], ALU.add, reads=[kn, "l_H"], writes=["l_H"])

    P.release(m1)
    NUM = P.al([128, NJ, 65], F32)
    dla(P, C, 65, qT, kT, ktok, lambda d: (vb, "l_vb"), make_gates, finish_dir, "l", NUM, "l_NUM")
    P.release(m1)
    sqh = P.al([128, NJ, 64], F32)
    ss = P.al([128, NJ], F32)
    otok = P.al([128, NJ, 64], F32)
    gn = P.al([128, 64], F32)
    ostg = P.al([64, TALL], F32)
    P.dma(gn[:, :], E["l_norm"][l].partition_broadcast(128), writes=["l_gn"])
    gather_fm(P, E, ostg, "l_o", 64, "l_ostg")
    fm_to_tok(P, E, ostg, 64, otok, "l_ostg", "l_otok")
    P.v("dve", "tensor_tensor", sqh[:, :, :], H[:, :, :], H[:, :, :], ALU.mult, reads=["l_H"], writes=["l_sqh"])
    P.v("dve", "reduce_sum", ss[:, :], sqh[:, :, :], AX.X, reads=["l_sqh"], writes=["l_ss"])
    P.v("dve", "tensor_scalar", ss[:, :], ss[:, :], 1.0 / 64, EPS, ALU.mult, ALU.add, reads=["l_ss"], writes=["l_ss"])
    P.actf(ss[:, :], ss[:, :], AF.Sqrt, reads=["l_ss"], writes=["l_ss"])
    P.v("dve", "reciprocal", ss[:, :], ss[:, :], reads=["l_ss"], writes=["l_ss"])
    P.v("dve", "tensor_tensor", H[:, :, :], H[:, :, :], ss[:, :].unsqueeze(2).to_broadcast([128, NJ, 64]), ALU.mult,
        reads=["l_H", "l_ss"], writes=["l_H"])
    P.v("dve", "tensor_tensor", H[:, :, :], H[:, :, :], gn[:, :].unsqueeze(1).to_broadcast([128, NJ, 64]), ALU.mult,
        reads=["l_H", "l_gn"], writes=["l_H"])
    P.actf(otok[:, :, :], otok[:, :, :], AF.Sigmoid, reads=["l_otok"], writes=["l_otok"])
    P.v("dve", "tensor_tensor", H[:, :, :], H[:, :, :], otok[:, :, :], ALU.mult, reads=["l_H", "l_otok"], writes=["l_H"])
    tok_to_fm(P, E, H, ostg, "l_H", "l_ostg")
    store_cols(P, E, 0, 64, (lambda a, b: ostg[0:64, a:b]), 0, TALL, "l_ostg")
    P.release(m0)


def b_ssd(P, E, l, C):
    B = C["banks"]
    ident = C["ident"]
    m0 = P.mark()
    qT = P.al([128, TALL], BF16)
    kT = P.al([128, TALL], BF16)
    ktok = P.al([128, NJ, 128], BF16)
    vtok = P.al([128, NJ, 64], F32)
    vbd = P.al([128, NJ, 64], BF16)
    dttok = P.al([128, NJ, 2], F32)
    Y = P.al([128, NJ, 64], F32)
    cw = P.al([128, 3, 4], F32)
    dtb = P.al([128, 2], F32)
    Ad = P.al([128, 2], F32)
    dsk = P.al([128, 1], F32)
    P.dma(cw[:, :, :], E["s_convw"][l], writes=["s_cw"])
    P.dma(dtb[:, :], E["s_dtbias"][l].partition_broadcast(128), writes=["s_dtb"])
    P.dma(Ad[:, :], E["s_alog"][l].partition_broadcast(128), writes=["s_Ad"])
    P.actf(Ad[:, :], Ad[:, :], AF.Exp, reads=["s_Ad"], writes=["s_Ad"])
    P.v("dve", "tensor_scalar_mul", Ad[:, :], Ad[:, :], -1.0, reads=["s_Ad"], writes=["s_Ad"])
    P.dma(dsk[:, :], E["s_dskip"][l].partition_broadcast(128), writes=["s_dsk"])
    m1 = P.mark()
    raw = P.al([128, TALL], F32)
    acc = P.al([128, TALL], F32)
    gather_fm(P, E, raw, "s_dt", 2, "s_raw")
    fm_to_tok(P, E, raw, 2, dttok, "s_raw", "s_dttok")
    for d in range(2):
        _softplus(P, dttok[:, :, d], dtb[:, d:d + 1], "s_dttok", deps=["s_dtb"])
    for blk, (r0, nr) in enumerate(((0, 64), (64, 128), (192, 128))):
        gather_fm(P, E, raw, ("s_x", "s_B", "s_C")[blk], nr, "s_raw")
        P.actf(acc[0:nr, :], raw[0:nr, :], AF.Identity, bias=cw[0:nr, blk, 3:4], scale=cw[0:nr, blk, 1:2],
               reads=["s_raw", "s_cw"], writes=["s_acc"])
        for (a, b) in ((0, CTX), (CTX, TALL)):
            P.v("dve", "scalar_tensor_tensor", acc[0:nr, a + 1:b], raw[0:nr, a:b - 1], cw[0:nr, blk, 0:1], acc[0:nr, a + 1:b],
                ALU.mult, ALU.add, reads=["s_raw", "s_cw", "s_acc"], writes=["s_acc"])
            P.v("dve", "scalar_tensor_tensor", acc[0:nr, a:b - 1], raw[0:nr, a + 1:b], cw[0:nr, blk, 2:3], acc[0:nr, a:b - 1],
                ALU.mult, ALU.add, reads=["s_raw", "s_cw", "s_acc"], writes=["s_acc"])
        P.actf(acc[0:nr, :], acc[0:nr, :], AF.Silu, reads=["s_acc"], writes=["s_acc"])
        if blk == 2:
            P.v("dve", "tensor_copy", qT[:, :], acc[:, :], reads=["s_acc"], writes=["s_qT"])
            continue
        if blk == 1:
            P.v("dve", "tensor_copy", kT[:, :], acc[:, :], reads=["s_acc"], writes=["s_kT"])
        for g in range(0, NJ, 4):
            ps, pk = B[6 + (g // 4) % 2], f"bank{6 + (g // 4) % 2}"
            nj = min(4, NJ - g)
            for jj in range(nj):
                j = g + jj
                P.tr(ps[:, 128 * jj:128 * jj + nr], acc[0:nr, 128 * j:128 * j + 128], ident[0:nr, 0:nr],
                     reads=["s_acc", "ident"], writes=[pk])
            src = ps[:, 0:128 * nj].rearrange("p (j e) -> p j e", e=128)[:, :, 0:nr]
            if blk == 0:
                P.evac(vtok[:, g:g + nj, :], src, reads=[pk], writes=["s_vtok"])
            else:
                P.evac(ktok[:, g:g + nj, :], src, reads=[pk], writes=["s_ktok"])
    P.release(m1)

    def make_gates(d, lfrow, lftok, igtok, klr, klt, kit):
        gather_fm(P, E, lfrow, f"s_dtrow{d}", 128, klr)
        _softplus(P, lfrow[:, :], dtb[:, d:d + 1], klr, deps=["s_dtb"])
        P.actf(lfrow[:, :], lfrow[:, :], AF.Identity, scale=Ad[:, d:d + 1], reads=[klr, "s_Ad"], writes=[klr])
        P.v("dve", "tensor_scalar", lftok[:, :], dttok[:, :, d], Ad[:, d:d + 1], None, ALU.mult, reads=["s_dttok", "s_Ad"], writes=[klt])
        return False

    def get_vb(d):
        P.v("dve", "tensor_tensor", vbd[:, :, :], vtok[:, :, :], dttok[:, :, d:d + 1].to_broadcast([128, NJ, 64]), ALU.mult,
            reads=["s_vtok", "s_dttok"], writes=["s_vbd"])
        return vbd, "s_vbd"

    def finish_dir(d, NUM, kn):
        pass

    dla(P, C, 64, qT, kT, ktok, get_vb, make_gates, finish_dir, "s", Y, "s_Y", accum=True)
    P.release(m1)
    ztok = P.al([128, NJ, 64], F32)
    zst = P.al([64, TALL], F32)
    gather_fm(P, E, zst, "s_z", 64, "s_zst")
    fm_to_tok(P, E, zst, 64, ztok, "s_zst", "s_ztok")
    P.v("dve", "scalar_tensor_tensor", Y[:, :, :], vtok[:, :, :], dsk[:, 0:1], Y[:, :, :], ALU.mult, ALU.add,
        reads=["s_vtok", "s_dsk", "s_Y"], writes=["s_Y"])
    P.actf(ztok[:, :, :], ztok[:, :, :], AF.Silu, reads=["s_ztok"], writes=["s_ztok"])
    P.v("dve", "tensor_tensor", Y[:, :, :], Y[:, :, :], ztok[:, :, :], ALU.mult, reads=["s_Y", "s_ztok"], writes=["s_Y"])
    tok_to_fm(P, E, Y, zst, "s_Y", "s_zst")
    store_cols(P, E, 128, 64, (lambda a, b: zst[0:64, a:b]), 0, TALL, "s_zst")
    P.release(m0)


MAGIC = 12582912.0
TWO_PI = 6.283185307179586


def _sin_rr(P, out, in_, tmp, key_in, key_out, key_tmp, shift=0.0):
    P.v("dve", "tensor_scalar", tmp, in_, 1.0 / TWO_PI, shift / TWO_PI + MAGIC, ALU.mult, ALU.add, reads=[key_in], writes=[key_tmp])
    P.v("dve", "tensor_scalar", tmp, tmp, MAGIC, -TWO_PI, ALU.subtract, ALU.mult, reads=[key_tmp], writes=[key_tmp])
    P.v("dve", "scalar_tensor_tensor", tmp, in_, 1.0, tmp, ALU.mult, ALU.add, reads=[key_in, key_tmp], writes=[key_tmp])
    P.actf(out, tmp, AF.Sin, bias=shift, reads=[key_tmp], writes=[key_out])


def b_s5(P, E, l, C):
    B = C["banks"]
    ident = C["ident"]
    m0 = P.mark()
    pr = {k: P.al([128, 4], F32) for k in ("are", "aim", "ldt", "dt", "lr", "mag", "th", "sn", "cs", "abr", "abi",
                                            "den", "fre", "fim", "t1", "t2", "t3")}
    bre = P.al([128, 2, 32], F32)
    bim = P.al([128, 2, 32], F32)
    cre = P.al([128, 2, 32], F32)
    cim = P.al([128, 2, 32], F32)
    cTr = P.al([128, 2, 32], BF16)
    cTi = P.al([128, 2, 32], BF16)
    bbT = P.al([32, 8, 128], BF16)
    bbtmp = P.al([128, 2, 32], F32)
    dsk = P.al([32, 2], F32)
    pw = P.al([128, 15, 2], F32)
    npw = P.al([128, 15], F32)
    for k, nm in (("are", "s5_are"), ("aim", "s5_aim"), ("ldt", "s5_ldt")):
        P.dma(pr[k][:, :], E[nm][l], writes=["5p_" + k])
    for t, nm in ((bre, "s5_bre"), (bim, "s5_bim"), (cre, "s5_cre"), (cim, "s5_cim")):
        P.dma(t[:, :, :], E[nm][l], writes=["5p_" + nm])
    P.dma(dsk[:, :], E["s5_d"][l], writes=["5p_dsk"])
    P.v("dve", "tensor_copy", cTr[:, :, :], cre[:, :, :], reads=["5p_s5_cre"], writes=["5p_cT"])
    P.v("dve", "tensor_scalar_mul", cTi[:, :, :], cim[:, :, :], -1.0, reads=["5p_s5_cim"], writes=["5p_cT"])
    kk = "5p_small"
    a = lambda k: pr[k][:, :]
    P.actf(a("dt"), a("ldt"), AF.Exp, reads=["5p_ldt"], writes=[kk])
    P.v("dve", "tensor_scalar_min", a("lr"), a("are"), -1e-4, reads=["5p_are"], writes=[kk])
    P.v("dve", "tensor_tensor", a("t1"), a("lr"), a("dt"), ALU.mult, reads=[kk], writes=[kk])
    P.actf(a("mag"), a("t1"), AF.Exp, reads=[kk], writes=[kk])
    P.v("dve", "tensor_tensor", a("th"), a("aim"), a("dt"), ALU.mult, reads=[kk, "5p_aim"], writes=[kk])
    _sin_rr(P, a("sn"), a("th"), a("t2"), kk, kk, kk)
    P.v("dve", "tensor_scalar_add", a("t3"), a("th"), TWO_PI / 4, reads=[kk], writes=[kk])
    _sin_rr(P, a("cs"), a("t3"), a("t2"), kk, kk, kk)
    P.v("dve", "tensor_tensor", a("abr"), a("mag"), a("cs"), ALU.mult, reads=[kk], writes=[kk])
    P.v("dve", "tensor_tensor", a("abi"), a("mag"), a("sn"), ALU.mult, reads=[kk], writes=[kk])
    P.v("dve", "tensor_tensor", a("den"), a("lr"), a("lr"), ALU.mult, reads=[kk], writes=[kk])
    P.v("dve", "tensor_tensor", a("t1"), a("aim"), a("aim"), ALU.mult, reads=[kk, "5p_aim"], writes=[kk])
    P.v("dve", "tensor_tensor", a("den"), a("den"), a("t1"), ALU.add, reads=[kk], writes=[kk])
    P.v("dve", "reciprocal", a("den"), a("den"), reads=[kk], writes=[kk])
    P.v("dve", "tensor_scalar_add", a("t1"), a("abr"), -1.0, reads=[kk], writes=[kk])
    P.v("dve", "tensor_tensor", a("t2"), a("t1"), a("lr"), ALU.mult, reads=[kk], writes=[kk])
    P.v("dve", "tensor_tensor", a("t3"), a("abi"), a("aim"), ALU.mult, reads=[kk, "5p_aim"], writes=[kk])
    P.v("dve", "tensor_tensor", a("t2"), a("t2"), a("t3"), ALU.add, reads=[kk], writes=[kk])
    P.v("dve", "tensor_tensor", a("fre"), a("t2"), a("den"), ALU.mult, reads=[kk], writes=[kk])
    P.v("dve", "tensor_tensor", a("t2"), a("abi"), a("lr"), ALU.mult, reads=[kk], writes=[kk])
    P.v("dve", "tensor_tensor", a("t3"), a("t1"), a("aim"), ALU.mult, reads=[kk, "5p_aim"], writes=[kk])
    P.v("dve", "tensor_tensor", a("t2"), a("t2"), a("t3"), ALU.subtract, reads=[kk], writes=[kk])
    P.v("dve", "tensor_tensor", a("fim"), a("t2"), a("den"), ALU.mult, reads=[kk], writes=[kk])
    P.v("dve", "tensor_scalar_mul", a("t1"), a("fim"), -1.0, reads=[kk], writes=[kk])
    for gp in range(2):
        for d in range(2):
            ix = gp * 2 + d
            for comp in range(2):
                if comp == 0:
                    P.v("dve", "tensor_scalar", bbtmp[:, 0, :], bre[:, gp, :], pr["fre"][:, ix:ix + 1], None, ALU.mult,
                        reads=[kk, "5p_s5_bre"], writes=["5p_bbtmp"])
                    P.v("dve", "scalar_tensor_tensor", bbtmp[:, 0, :], bim[:, gp, :], pr["t1"][:, ix:ix + 1], bbtmp[:, 0, :],
                        ALU.mult, ALU.add, reads=[kk, "5p_s5_bim", "5p_bbtmp"], writes=["5p_bbtmp"])
                else:
                    P.v("dve", "tensor_scalar", bbtmp[:, 0, :], bim[:, gp, :], pr["fre"][:, ix:ix + 1], None, ALU.mult,
                        reads=[kk, "5p_s5_bim"], writes=["5p_bbtmp"])
                    P.v("dve", "scalar_tensor_tensor", bbtmp[:, 0, :], bre[:, gp, :], pr["fim"][:, ix:ix + 1], bbtmp[:, 0, :],
                        ALU.mult, ALU.add, reads=[kk, "5p_s5_bre", "5p_bbtmp"], writes=["5p_bbtmp"])
                P.tr(B[6][0:32, 0:128], bbtmp[:, 0, :], ident[:, :], reads=["5p_bbtmp", "ident"], writes=["bank6"])
                P.v("dve", "tensor_copy", bbT[:, gp * 4 + d * 2 + comp, :], B[6][0:32, 0:128], reads=["bank6"], writes=["5p_bbT"])
    Er = P.al([128, TALL], F32)
    Ei = P.al([128, TALL], F32)
    DEC = P.al([128, 512], F32)
    ust = P.al([32, TALL], F32)
    ub = P.al([32, TALL], BF16)
    Y = P.al([32, TALL], F32)
    tt = {k: [P.al([128, 512], F32) for _ in range(2)] for k in ("m1", "m2", "m3", "m4", "vr", "vi", "zr", "zi")}
    for k in ("n1", "n2"):
        nbuf = P.al([128, 512], F32)
        tt[k] = [nbuf, nbuf]
    xr = [P.al([128, 512], BF16) for _ in range(2)]
    xi = [P.al([128, 512], BF16) for _ in range(2)]
    for gp in range(2):
        gather_fm(P, E, ust, f"s5_u{gp}", 32, "5_ust")
        P.cast(ub[:, :], ust[:, :], reads=["5_ust"], writes=["5_ub"])
        for d in range(2):
            ix = gp * 2 + d
            rev = d == 1
            P.v("dve", "tensor_copy", pw[:, 0, 0:1], pr["cs"][:, ix:ix + 1], reads=[kk], writes=["5_pw"])
            P.v("dve", "tensor_copy", pw[:, 0, 1:2], pr["sn"][:, ix:ix + 1], reads=[kk], writes=["5_pw"])
            for k in range(14):
                c_, s_ = pw[:, k, 0:1], pw[:, k, 1:2]
                P.v("dve", "tensor_tensor", pr["t2"][:, 0:1], c_, c_, ALU.mult, reads=["5_pw"], writes=["5_t"])
                P.v("dve", "tensor_tensor", pr["t2"][:, 1:2], s_, s_, ALU.mult, reads=["5_pw"], writes=["5_t"])
                P.v("dve", "tensor_tensor", pw[:, k + 1, 0:1], pr["t2"][:, 0:1], pr["t2"][:, 1:2], ALU.subtract, reads=["5_t"], writes=["5_pw"])
                P.v("dve", "scalar_tensor_tensor", pw[:, k + 1, 1:2], c_, 2.0, s_, ALU.mult, ALU.mult, reads=["5_pw"], writes=["5_pw"])
            P.v("dve", "tensor_scalar_mul", npw[:, :], pw[:, :, 1], -1.0, reads=["5_pw"], writes=["5_npw"])
            P.v("pool", "memset", Er[:, 0:1], 1.0, writes=["5_E"])
            P.v("pool", "memset", Ei[:, 0:1], 0.0, writes=["5_E"])
            for k in range(14):
                L = 1 << k
                n = min(L, TALL - L)
                c_, s_, ns_ = pw[:, k, 0:1], pw[:, k, 1:2], npw[:, k:k + 1]
                P.v("dve", "tensor_scalar", Er[:, L:L + n], Er[:, 0:n], c_, None, ALU.mult, reads=["5_E", "5_pw"], writes=["5_E"])
                P.v("dve", "scalar_tensor_tensor", Er[:, L:L + n], Ei[:, 0:n], ns_, Er[:, L:L + n], ALU.mult, ALU.add,
                    reads=["5_E", "5_npw"], writes=["5_E"])
                P.v("dve", "tensor_scalar", Ei[:, L:L + n], Ei[:, 0:n], c_, None, ALU.mult, reads=["5_E", "5_pw"], writes=["5_E"])
                P.v("dve", "scalar_tensor_tensor", Ei[:, L:L + n], Er[:, 0:n], s_, Ei[:, L:L + n], ALU.mult, ALU.add,
                    reads=["5_E", "5_pw"], writes=["5_E"])
            P.v("pool", "memset", DEC[:, :], 1.0, writes=["5_DEC"])
            P.v("dve", "tensor_scalar", DEC[:, :], DEC[:, :], pr["mag"][:, ix:ix + 1], None, ALU.mult, reads=["5_DEC", kk], writes=["5_DEC"])
            lat = _col_tiles(CTX, TALL)
            tiles = [(0, CTX)] + (lat if not rev else lat[::-1])
            prev = [None]

            def views(ti):
                c0, n = tiles[ti]
                r = ti % 2
                if not rev:
                    EV = lambda E_, c0=c0, n=n: E_[:, c0:c0 + n]
                else:
                    seg_hi = (CTX - 1) if c0 < CTX else (TALL - 1 + CTX)
                    lo, hi = seg_hi - (c0 + n - 1), seg_hi - c0 + 1
                    EV = lambda E_, lo=lo, hi=hi: E_[:, lo:hi][:, ::-1]
                T_ = lambda k, r=r, n=n: tt[k][r][:, :n]
                Kk = lambda k, r=r: (f"5_{k}" if k in ("n1", "n2") else f"5_{k}{r}")
                return c0, n, r, slice(c0, c0 + n), EV, T_, Kk, B[r], B[2 + r], f"bank{r}", f"bank{2 + r}"

            def pre(ti):
                c0, n, r, cols, EV, T_, Kk, pR, pI, kR, kI = views(ti)
                P.mm(pR[:, :n], bbT[:, gp * 4 + d * 2 + 0, :], ub[:, cols], reads=["5p_bbT", "5_ub"], writes=[kR])
                P.mm(pI[:, :n], bbT[:, gp * 4 + d * 2 + 1, :], ub[:, cols], reads=["5p_bbT", "5_ub"], writes=[kI])
                P.v("dve", "tensor_tensor", T_("m1"), pR[:, :n], EV(Er), ALU.mult, reads=[kR, "5_E"], writes=[Kk("m1")])
                P.v("dve", "tensor_tensor", T_("m2"), pI[:, :n], EV(Ei), ALU.mult, reads=[kI, "5_E"], writes=[Kk("m2")])
                P.v("pool", "tensor_tensor", T_("vr"), T_("m1"), T_("m2"), ALU.add, reads=[Kk("m1"), Kk("m2")], writes=[Kk("vr")])
                P.v("dve", "tensor_tensor", T_("m3"), pI[:, :n], EV(Er), ALU.mult, reads=[kI, "5_E"], writes=[Kk("m3")])
                P.v("dve", "tensor_tensor", T_("m4"), pR[:, :n], EV(Ei), ALU.mult, reads=[kR, "5_E"], writes=[Kk("m4")])
                P.v("pool", "tensor_tensor", T_("vi"), T_("m3"), T_("m4"), ALU.subtract, reads=[Kk("m3"), Kk("m4")], writes=[Kk("vi")])

            def scanpost(ti):
                c0, n, r, cols, EV, T_, Kk, pR, pI, kR, kI = views(ti)
                for comp, vk in (("zr", "vr"), ("zi", "vi")):
                    o_, d1 = T_(comp), T_(vk)
                    if rev:
                        o_, d1 = o_[:, ::-1], d1[:, ::-1]
                    if prev[0] is None:
                        init, ik = 0.0, []
                    else:
                        pr_r, pn = prev[0]
                        init = tt[comp][pr_r][:, 0:1] if rev else tt[comp][pr_r][:, pn - 1:pn]
                        ik = [f"5_{comp}{pr_r}"]
                    P.v("dve", "tensor_tensor_scan", o_, DEC[:, :n], d1, init, ALU.mult, ALU.add,
                        reads=["5_DEC", Kk(vk)] + ik, writes=[Kk(comp)])
                prev[0] = (r, n)
                P.v("dve", "tensor_tensor", T_("n1"), T_("zr"), EV(Er), ALU.mult, reads=[Kk("zr"), "5_E"], writes=[Kk("n1")])
                P.v("pool", "tensor_tensor", T_("n2"), T_("zi"), EV(Ei), ALU.mult, reads=[Kk("zi"), "5_E"], writes=[Kk("n2")])
                P.v("dve", "tensor_tensor", xr[r][:, :n], T_("n1"), T_("n2"), ALU.subtract, reads=[Kk("n1"), Kk("n2")], writes=[f"5_xr{r}"])
                P.v("dve", "tensor_tensor", T_("n1"), T_("zr"), EV(Ei), ALU.mult, reads=[Kk("zr"), "5_E", f"5_xr{r}"], writes=[Kk("n1")])
                P.v("dve", "tensor_tensor", T_("n2"), T_("zi"), EV(Er), ALU.mult, reads=[Kk("zi"), "5_E", f"5_xr{r}"], writes=[Kk("n2")])
                P.v("dve", "tensor_tensor", xi[r][:, :n], T_("n1"), T_("n2"), ALU.add, reads=[Kk("n1"), Kk("n2")], writes=[f"5_xi{r}"])
                pY, kY = B[4 + r], f"bank{4 + r}"
                P.mm(pY[0:32, :n], cTr[:, gp, :], xr[r][:, :n], start=True, stop=False, reads=["5p_cT", f"5_xr{r}"], writes=[kY])
                P.mm(pY[0:32, :n], cTi[:, gp, :], xi[r][:, :n], start=False, stop=True, reads=["5p_cT", f"5_xi{r}"], writes=[kY])
                if d == 0:
                    P.v("dve", "scalar_tensor_tensor", Y[:, cols], ust[:, cols], dsk[:, gp:gp + 1], pY[0:32, :n], ALU.mult, ALU.add,
                        reads=["5_ust", "5p_dsk", kY], writes=["5_Y"])
                else:
                    P.v("dve", "tensor_tensor", Y[:, cols], Y[:, cols], pY[0:32, :n], ALU.add, reads=["5_Y", kY], writes=["5_Y"])

            pre(0)
            for ti in range(len(tiles)):
                if ti + 1 < len(tiles):
                    pre(ti + 1)
                scanpost(ti)
        store_cols(P, E, 192 + 32 * gp, 32, (lambda a, b: Y[0:32, a:b]), 0, TALL, "5_Y")
    P.release(m0)


def phase_C(P, E, l, final=False):
    banks = E["banks"]
    ones_b, cs = E["ones_b"], E["cs"]
    wmod, bmod, norm2 = E["wmodC"][l], E["bmodC"][l], E["norm2"][l]
    snorm, wglu, wout, wup, convw, wdown = E["snorm"][l], E["wglu"][l], E["wout"][l], E["wup"][l], E["convw"][l], E["wdown"][l]
    XS, XSo, GXB, xTo = E["XS"][l % 2], E["XS"][(l + 1) % 2], E["GXB"], E["xTo"]
    itC, itX = E["idxC"], E["idxX"]
    m_top0 = P.mark()
    bmod_sb = P.al([128, 32], F32)
    P.dma(bmod_sb[:], bmod, writes=["bmod"])
    n2 = P.al([128, KC], F32)
    P.dma(n2[:], norm2, writes=["n2"])
    fn = P.al([128, KC], F32)
    P.dma(fn[:], E["fnorm"], writes=["fn"])
    sn = P.al([128, 2], F32)
    P.dma(sn[:], snorm, writes=["sn"])
    hm = P.al([128, 4], F32)
    P.dma(hm[:], E["hmask"], writes=["hm"])
    cwf = P.al([128, NFF, 3], F32)
    P.dma(cwf[:], convw, writes=["cwf"])
    modv = P.al([128, 4, KC, 2], F32)
    A2 = P.al([128, KC, 2], F32)
    nb = P.al([128, KC, 2, 4], F32)
    P.v("pool", "memset", nb[:], 0.0, writes=["nb"])
    for k in range(KC):
        for side in range(2):
            c = k * 2 + side
            P.add("pool", (lambda k, side, c: (lambda e: e.indirect_dma_start(
                out=nb[:, k, side, :], out_offset=None, in_=GXB[:, :],
                in_offset=bass.IndirectOffsetOnAxis(ap=itX[:, c:c + 1], axis=0))))(k, side, c), reads=["GXB", "idxX", "nb"], writes=["nb"], dma=True)
    m_top = P.mark()
    wst = [P.al([128, 1024], F32) for _ in range(2)]
    _mod_vectors(P, wmod, bmod_sb, cs, [0, 1, 2, 3], wst, ["wst0", "wst1"], banks[7], "bank7", modv)
    P.v("dve", "tensor_scalar_add", A2[:], modv[:, 2, :, :], 1.0, reads=["modv"], writes=["A2"])
    for s_ in range(2):
        P.v("dve", "tensor_tensor", A2[:, :, s_], A2[:, :, s_], n2[:], ALU.mult, reads=["A2", "n2"], writes=["A2"])
    P.release(m_top)

    g0 = [(0, 0, 1026, 1, 1025, (0, None))]
    g1 = [(0, 1024, 2050, 1025, 2049, (None, 1))]
    if not final:
        g1.append((1, CX0, CX1, CX0 + 1, CX1 - 1, (2, 3)))
    evi = [0]
    for gi, segs in enumerate((g0, g1)):
        m_g = P.mark()
        W = sum(e1 - e0 for (_, e0, e1, _, _, _) in segs)
        WO = sum(o1 - o0 for (_, _, _, o0, o1, _) in segs)
        xres = P.al([128, KC, W], F32)
        h2 = P.al([128, KC, W], BF16)
        aT = P.al([128, NFF, WO], BF16)
        loc = []
        off = 0
        ooff = 0
        for (kind, e0, e1, o0, o1, hf) in segs:
            loc.append((off, ooff))
            off += e1 - e0
            ooff += o1 - o0
        kx = f"xres{gi}"
        m_s1 = P.mark()
        wout_b = P.al([128, KC, D], BF16)
        wglu_b = P.al([128, 2, 256], BF16)
        wstg = [P.al([128, D], F32) for _ in range(2)]
        for k in range(KC):
            P.dma(wstg[k % 2][:, :], wout[k * 128:(k + 1) * 128, :], writes=[f"wstg{k % 2}"])
            P.cast(wout_b[:, k, :], wstg[k % 2][:, :], reads=[f"wstg{k % 2}"], writes=["woutb"])
        for k in range(2):
            P.dma(wstg[k][:, 0:256], wglu[k * 128:(k + 1) * 128, :], writes=[f"wstg{k}"])
            P.v("pool", "tensor_copy", wglu_b[:, k, :], wstg[k][:, 0:256], reads=[f"wstg{k}"], writes=["wglub"])
        yst = [P.al([128, KC, 512], F32) for _ in range(2)]
        ycb = P.al([128, KC, 512], BF16)
        sqb = P.al([128, KC, 512], BF16)
        gel = P.al([128, 2, 512], F32)
        gelb = P.al([128, 2, 512], BF16)
        sig = P.al([128, 512], F32)
        rstd = P.al([128, 512], F32)
        tmpn = P.al([128, 512], F32)
        tmpx = [P.al([128, 512], F32) for _ in range(2)]
        ti = 0
        for si, (kind, e0, e1, o0, o1, hf) in enumerate(segs):
            lo = loc[si][0]
            halo = {LAT0: (0, 1), LAT1 - 1: (1, 0), CX0: (0, 3), CX1 - 1: (1, 2)}
            oa = e0 + 1 if e0 in halo else e0
            ob = e1 - 1 if (e1 - 1) in halo else e1
            xs0 = (oa - 1) if kind == 0 else (NLAT + oa - CX0 - 1)
            P.dma(xres[:, :, lo + (oa - e0):lo + (ob - e0)], XS[:, xs0:xs0 + (ob - oa)].rearrange("(k p) n -> p k n", p=128),
                  reads=[f"XS{l % 2}"], writes=[kx])
            for ecol in (e0, e1 - 1):
                if ecol in halo:
                    side, bc = halo[ecol]
                    P.v("dve", "tensor_copy", xres[:, :, lo + (ecol - e0)], nb[:, :, side, bc], reads=["nb", kx], writes=[kx])
            for (c0, n) in _col_tiles(e0, e1):
                l0 = lo + (c0 - e0)
                ys, yk = yst[ti % 2], f"yst{ti % 2}"
                ti += 1
                pi = Y_PIECES.index((c0, n))
                srcp = _flat(E["GEXY"]).rearrange("(r w) -> r w", w=n)
                for k in range(KC):
                    cix = pi * KC + k
                    P.add("pool", (lambda k, ys, n, srcp, cix, eoff: (lambda e: e.indirect_dma_start(
                        out=ys[:, k, :n], out_offset=None, in_=srcp,
                        in_offset=bass.IndirectOffsetOnAxis(ap=itC[:, cix:cix + 1], axis=0), element_offset=eoff)))(k, ys, n, srcp, cix, 0), reads=["GEXY", "idxC"], writes=[yk], dma=True)
                P.cast(ycb[:, 0:4, :n], ys[:, 0:4, :n], reads=[yk], writes=["ycb"])
                P.actf(sqb[:, 0:2, :n], ys[:, 4:6, :n], AF.Square, reads=[yk], writes=["sqb"])
                pss = banks[7]
                for j in range(2):
                    P.mm(pss[:, :n], ones_b[:], sqb[:, j, :n], start=(j == 0), stop=(j == 1), reads=["ones", "sqb"], writes=["bank7"])
                _rms_rstd(P, pss, n, 256, rstd, "bank7", "rstd", tmpn, "tmpn")
                for j in range(2):
                    P.v("dve", "tensor_tensor", tmpx[j][:, :n], ys[:, 4 + j, :n], rstd[:, :n], ALU.mult, reads=[yk, "rstd"], writes=[f"tmpx{j}"])
                    P.actf(ycb[:, 4 + j, :n], tmpx[j][:, :n], AF.Identity, scale=sn[:, j:j + 1], reads=[f"tmpx{j}", "sn"], writes=["ycb"])
                P.actf(gel[:, :, :n], ys[:, 6:8, :n], AF.Gelu, reads=[yk], writes=["gel"])
                P.v("dve", "tensor_copy", gelb[:, :, :n], gel[:, :, :n], reads=["gel"], writes=["gelb"])
                for mc in range(2):
                    pg_, pgk = banks[6], "bank6"
                    for j in range(2):
                        P.mm(pg_[:, :n], wglu_b[:, j, mc * 128:(mc + 1) * 128], gelb[:, j, :n], start=(j == 0), stop=(j == 1),
                             reads=["wglub", "gelb"], writes=[pgk])
                    P.actf(sig[:, :n], pg_[:, :n], AF.Sigmoid, reads=[pgk], writes=["sig"])
                    P.v("dve", "tensor_tensor", ycb[:, 6 + mc, :n], gel[:, mc, :n], sig[:, :n], ALU.mult, reads=["gel", "sig"], writes=["ycb"])
                for dc in range(KC):
                    po, pok = banks[dc % 4], f"bank{dc % 4}"
                    for k in range(KC):
                        P.mm(po[:, :n], wout_b[:, k, dc * 128:(dc + 1) * 128], ycb[:, k, :n], start=(k == 0), stop=(k == KC - 1),
                             reads=["woutb", "ycb"], writes=[pok])
                    P.v("dve", "scalar_tensor_tensor", xres[:, dc, l0:l0 + n], po[:, :n], modv[:, 0, dc, kind:kind + 1],
                        xres[:, dc, l0:l0 + n], ALU.mult, ALU.add, reads=[pok, "modv", kx], writes=[kx])
                P.actf(sqb[:, :, :n], xres[:, :, l0:l0 + n], AF.Square, reads=[kx], writes=["sqb"])
                for k in range(KC):
                    P.mm(pss[:, :n], ones_b[:], sqb[:, k, :n], start=(k == 0), stop=(k == KC - 1), reads=["ones", "sqb"], writes=["bank7"])
                _rms_rstd(P, pss, n, D, rstd, "bank7", "rstd", tmpn, "tmpn")
                for k in range(KC):
                    P.v("dve", "tensor_tensor", tmpx[k % 2][:, :n], xres[:, k, l0:l0 + n], rstd[:, :n], ALU.mult,
                        reads=[kx, "rstd"], writes=[f"tmpx{k % 2}"])
                    P.actf(h2[:, k, l0:l0 + n], tmpx[k % 2][:, :n], AF.Identity, bias=modv[:, 1, k, kind:kind + 1],
                           scale=A2[:, k, kind:kind + 1], reads=[f"tmpx{k % 2}", "modv", "A2"], writes=["h2"])
            for hidx, col in ((hf[0], lo), (hf[1], lo + (e1 - e0) - 1)):
                if hidx is not None:
                    P.v("dve", "tensor_scalar", h2[:, :, col], h2[:, :, col], hm[:, hidx:hidx + 1], None, ALU.mult,
                        reads=["h2", "hm"], writes=["h2"])
        P.release(m_s1)
        wus = [P.al([128, KC, 256], F32) for _ in range(2)]
        wub = [P.al([128, KC, 256], BF16) for _ in range(2)]
        wds = [P.al([128, NFF, 128], F32) for _ in range(2)]
        wdb = [P.al([128, NFF, 128], BF16) for _ in range(2)]
        tcv = [P.al([128, 512], F32) for _ in range(2)]
        tsl = [P.al([128, 512], F32) for _ in range(2)]
        ftiles = []
        for si, (kind, e0, e1, o0, o1, hf) in enumerate(segs):
            lo, oo = loc[si]
            for (c0, n) in _col_tiles(o0, o1, 410):
                ftiles.append((kind, lo + (c0 - e0), oo + (c0 - o0), n))
        it = 0
        for f in range(NFF):
            r = f % 2
            P.dma(wus[r][:, :, 0:128], wup[:, f * 128:(f + 1) * 128].rearrange("(k p) c -> p k c", p=128), writes=[f"wus{r}"])
            P.dma(wus[r][:, :, 128:256], wup[:, DFF + f * 128:DFF + (f + 1) * 128].rearrange("(k p) c -> p k c", p=128), writes=[f"wus{r}"])
            P.cast(wub[r][:, :, :], wus[r][:, :, :], reads=[f"wus{r}"], writes=[f"wub{r}"])
            for (kind, lc, oc, n) in ftiles:
                q = it % 2
                it += 1
                pu, puk = banks[q], f"bank{q}"
                pg_, pgk = banks[2 + q], f"bank{2 + q}"
                for k in range(KC):
                    P.mm(pu[:, :n], wub[r][:, k, 0:128], h2[:, k, lc:lc + n], start=(k == 0), stop=(k == KC - 1),
                         reads=[f"wub{r}", "h2"], writes=[puk])
                for k in range(KC):
                    P.mm(pg_[:, :n + 2], wub[r][:, k, 128:256], h2[:, k, lc - 1:lc + n + 1], start=(k == 0), stop=(k == KC - 1),
                         reads=[f"wub{r}", "h2"], writes=[pgk])
                tc_, tck = tcv[q], f"tcv{q}"
                ts_, tsk = tsl[q], f"tsl{q}"
                P.actf(tc_[:, :n], pg_[:, 1:n + 1], AF.Identity, scale=cwf[:, f, 1:2], reads=[pgk, "cwf"], writes=[tck])
                P.v("dve", "scalar_tensor_tensor", tc_[:, :n], pg_[:, 0:n], cwf[:, f, 0:1], tc_[:, :n], ALU.mult, ALU.add,
                    reads=[pgk, "cwf", tck], writes=[tck])
                P.v("dve", "scalar_tensor_tensor", tc_[:, :n], pg_[:, 2:n + 2], cwf[:, f, 2:3], tc_[:, :n], ALU.mult, ALU.add,
                    reads=[pgk, "cwf", tck], writes=[tck])
                P.actf(ts_[:, :n], tc_[:, :n], AF.Silu, reads=[tck], writes=[tsk])
                P.v("dve", "tensor_tensor", aT[:, f, oc:oc + n], ts_[:, :n], pu[:, :n], ALU.mult, reads=[tsk, puk], writes=[f"aT{gi}"])
        for dc in range(KC):
            r = dc % 2
            P.dma(wds[r][:, :, :], wdown[:, dc * 128:(dc + 1) * 128].rearrange("(f p) c -> p f c", p=128), writes=[f"wds{r}"])
            P.cast(wdb[r][:, :, :], wds[r][:, :, :], reads=[f"wds{r}"], writes=[f"wdb{r}"])
            for (kind, lc, oc, n) in ftiles:
                q = it % 2
                it += 1
                pd, pdk = banks[4 + q], f"bank{4 + q}"
                for f in range(NFF):
                    P.mm(pd[:, :n], wdb[r][:, f, :], aT[:, f, oc:oc + n], start=(f == 0), stop=(f == NFF - 1),
                         reads=[f"wdb{r}", f"aT{gi}"], writes=[pdk])
                P.v("dve", "scalar_tensor_tensor", xres[:, dc, lc:lc + n], pd[:, :n], modv[:, 3, dc, kind:kind + 1],
                    xres[:, dc, lc:lc + n], ALU.mult, ALU.add, reads=[pdk, "modv", kx], writes=[kx])
        if final:
            sqf = P.al([128, KC, 512], BF16)
            rstd = P.al([128, 512], F32)
            tmpn = P.al([128, 512], F32)
            tmpx = [P.al([128, 512], F32) for _ in range(2)]
            ost = [P.al([128, 512], F32) for _ in range(2)]
            (kind, e0, e1, o0, o1, hf) = segs[0]
            lo = loc[0][0]
            oi = 0
            for (c0, n) in _col_tiles(o0, o1):
                l0 = lo + (c0 - e0)
                P.actf(sqf[:, :, :n], xres[:, :, l0:l0 + n], AF.Square, reads=[kx], writes=["sqf"])
                for k in range(KC):
                    P.mm(banks[7][:, :n], ones_b[:], sqf[:, k, :n], start=(k == 0), stop=(k == KC - 1), reads=["ones", "sqf"], writes=["bank7"])
                _rms_rstd(P, banks[7], n, D, rstd, "bank7", "rstdf", tmpn, "tmpnf")
                for k in range(KC):
                    P.v("dve", "tensor_tensor", tmpx[k % 2][:, :n], xres[:, k, l0:l0 + n], rstd[:, :n], ALU.mult,
                        reads=[kx, "rstdf"], writes=[f"tmpxf{k % 2}"])
                    o_, ok_ = ost[oi % 2], f"ostf{oi % 2}"
                    oi += 1
                    P.actf(o_[:, :n], tmpx[k % 2][:, :n], AF.Identity, scale=fn[:, k:k + 1], reads=[f"tmpxf{k % 2}", "fn"], writes=[ok_])
                    P.dma(xTo[k * 128:(k + 1) * 128, c0 - 1:c0 - 1 + n], o_[:, :n], reads=[ok_], writes=["xTo"])
        else:
            for si, (kind, e0, e1, o0, o1, hf) in enumerate(segs):
                lo = loc[si][0]
                l0 = lo + (o0 - e0)
                dst0 = (o0 - 1) if kind == 0 else (NLAT + (o0 - CX0 - 1))
                P.dma(XSo[:, dst0:dst0 + (o1 - o0)].rearrange("(k p) n -> p k n", p=128), xres[:, :, l0:l0 + (o1 - o0)],
                      reads=[kx], writes=[f"XS{(l + 1) % 2}"])
                for (ecol, bc) in ((1, 0), (NLAT, 1), (CX0 + 1, 2), (CX1 - 2, 3)):
                    if o0 <= ecol < o1 and ((kind == 0) == (ecol < CX0)):
                        P.dma(E["XBND"][:, bc:bc + 1].rearrange("(k p) o -> p k o", p=128), xres[:, :, lo + (ecol - e0):lo + (ecol - e0) + 1],
                              reads=[kx], writes=["XBND"], allow_slow_non_contiguous=True)
        P.release(m_g)
    P.release(m_top0)


LAYER_W = [
    ("wmodA", [D, 2 * D]), ("bmodA", [128, 16]), ("wmodC", [D, 4 * D]), ("bmodC", [128, 32]),
    ("norm1", [128, KC]), ("norm2", [128, KC]), ("win", [D, P_IN]), ("winsw", [D, 32]), ("qn", [128, 2]), ("kvn", [128, 1]),
    ("wuq", [256, 384]), ("wuqsw", [256, 384]), ("wukv", [128, 512]), ("snorm", [128, 2]), ("wglu", [256, 256]),
    ("wout", [D, D]), ("wup", [D, 2 * DFF]), ("convw", [128, NFF, 3]), ("wdown", [DFF, D]),
    ("l_gbias", [1, 4]), ("l_norm", [1, 64]), ("s_convw", [128, 3, 4]), ("s_dtbias", [1, 2]), ("s_alog", [1, 2]), ("s_dskip", [1, 1]),
    ("s5_are", [128, 4]), ("s5_aim", [128, 4]), ("s5_ldt", [128, 4]), ("s5_bre", [128, 2, 32]), ("s5_bim", [128, 2, 32]),
    ("s5_cre", [128, 2, 32]), ("s5_cim", [128, 2, 32]), ("s5_d", [32, 2]),
]
GLOBAL_IN = [("x0", [D, NT], F32), ("cvec", [128, KC, 2], F32), ("ropeC", [32, NT], F32), ("ropeS", [32, NT], F32),
             ("fnorm", [128, KC], F32), ("hmask", [128, 4], F32), ("idxB", [128, NIB], I32), ("idxC", [128, 7 * KC], I32),
             ("idxX", [128, 2 * KC], I32)]
RG = [[0, 1, 2, 3], [4, 5, 6, 7]]


def _allgather(P, src, dst, ksrc, kdst, rows=None, chunks=None, chunk_keys=False):
    R = src.shape[0]
    rows = rows or R
    assert R % rows == 0
    for c in (range(R // rows) if chunks is None else chunks):
        a, b = src[c * rows:(c + 1) * rows, :], dst[4 * c * rows:4 * (c + 1) * rows, :]
        P.add("pool", (lambda a, b: (lambda e: e.collective_compute("AllGather", ALU.bypass, replica_groups=RG,
                                                                      ins=[a.opt()], outs=[b.opt()])))(a, b),
              reads=[ksrc], writes=[(f"{kdst}{c}" if chunk_keys else kdst)], cc=True)


def build_fused(NL=4):
    nc = bass.Bass("TRN2", target_bir_lowering=False)
    E = {}
    for name, shape, dt_ in GLOBAL_IN:
        E[name + "_d"] = _dram(nc, name, shape, dtype=dt_)
    for name, shape in LAYER_W:
        E[name] = _dram(nc, name, [NL] + shape)
    E["xTo"] = _dram(nc, "xTo", [D, NLAT], kind="ExternalOutput")
    for name, shape in (("EXA", [RA_PAD // 512, 512]), ("GEXA", [4 * RA_PAD // 512, 512]), ("EXY", [RY_PAD // 512, 512]),
                        ("GEXY", [4 * RY_PAD // 512, 512]), ("XS0", [D, NT]), ("XS1", [D, NT]), ("XBND", [D, 4]), ("GXB", [4 * D, 4])):
        E[name] = nc.dram_tensor(name, shape, F32).ap()
    E["ropeC"], E["ropeS"], E["fnorm"], E["hmask"] = E["ropeC_d"], E["ropeS_d"], E["fnorm_d"], E["hmask_d"]
    P = Prog(nc)
    C = b_consts(P)
    E["banks"], E["ident"], E["ones_b"] = C["banks"], C["ident"], C["ones_b"]
    cs = P.sb([128, KC, 2], F32, "cs")
    P.dma(cs[:], E["cvec_d"], writes=["cs"])
    P.actf(cs[:], cs[:], AF.Silu, reads=["cs"], writes=["cs"])
    E["cs"] = cs
    for nm, w in (("idxB", NIB), ("idxC", 7 * KC), ("idxX", 2 * KC)):
        t = P.sb([128, w], I32, nm)
        P.dma(t[:], E[nm + "_d"], writes=[nm])
        E[nm] = t
    P.arena_init(196 * 1024)
    mz = P.mark()
    zt = P.al([128, 8192], F32)
    P.v("pool", "memset", zt[:], 0.0, writes=["zeros"])
    nrow = RY_PAD // 512
    for j0 in range(0, nrow, 128 * 16):
        nr_ = min(128 * 16, nrow - j0)
        if nr_ % 128 == 0:
            P.dma(E["EXY"][j0:j0 + nr_, :].rearrange("(p a) w -> p (a w)", p=128), zt[:, 0:(nr_ // 128) * 512], reads=["zeros"], writes=["EXY"])
        else:
            for j1 in range(j0, j0 + nr_, 128):
                n1 = min(128, j0 + nr_ - j1)
                P.dma(E["EXY"][j1:j1 + n1, :], zt[0:n1, 0:512], reads=["zeros"], writes=["EXY"])
    P.release(mz)
    E["XS"] = [E["XS0"], E["XS1"]]
    P.dma(E["XS"][0], E["x0_d"], writes=["XS0"])
    for bc, col in ((0, 0), (1, NLAT - 1), (2, NLAT), (3, NT - 1)):
        P.dma(E["XBND"][:, bc:bc + 1], E["x0_d"][:, col:col + 1], writes=["XBND"], allow_slow_non_contiguous=True)
    for l in range(NL):
        phase_A(P, E, l)
        first = sorted(set(_op_chunks("m_QT") + _op_chunks("m_KT") + _op_chunks("m_V")))
        rest = [c for c in range(NCHA) if c not in first]
        _allgather(P, E["EXA"], E["GEXA"], "PT", "GEXA", rows=CHA // 512, chunks=first, chunk_keys=True)
        b_mla(P, E, l, C, after_first_gather=lambda: _allgather(P, E["EXA"], E["GEXA"], "PT", "GEXA", rows=CHA // 512,
                                                                chunks=rest, chunk_keys=True))
        b_mlstm(P, E, l, C)
        b_ssd(P, E, l, C)
        b_s5(P, E, l, C)
        _allgather(P, E["EXY"], E["GEXY"], "EXY", "GEXY", rows=CHY // 512)
        _allgather(P, E["XBND"], E["GXB"], "XBND", "GXB")
        phase_C(P, E, l, final=(l == NL - 1))
    P.emit()
    return nc


def _rope_tables():
    rows = SEQ // 64
    row = np.broadcast_to(np.arange(rows)[:, None], (rows, 64)).reshape(-1).astype(np.float32)
    col = np.broadcast_to(np.arange(64)[None, :], (rows, 64)).reshape(-1).astype(np.float32)
    inv = (10000.0 ** (-np.arange(8, dtype=np.float32) / 8)).astype(np.float32)
    ang = np.concatenate([row[:, None] * inv, col[:, None] * inv], axis=-1)
    return np.cos(ang).astype(np.float32), np.sin(ang).astype(np.float32)


def _c32(a):
    return np.ascontiguousarray(a, dtype=np.float32)


def _index_tables(h, q):
    OOB = -1
    g = h // 2
    SS = R_SS
    p = np.arange(128)
    rows = {
        "m_QT": np.where(p < 96, R_Q + 96 * h + p, OOB),
        "m_KT": np.where(p < 64, R_KN + 64 * h + p, np.where(p < 96, R_KR + (p - 64), OOB)),
        "m_V": np.where(p < 64, R_V + 64 * h + p, OOB),
        "l_q": np.where(p < 64, 0 + 64 * h + p, OOB), "l_k": np.where(p < 64, 256 + 64 * h + p, OOB),
        "l_v": np.where(p < 64, 512 + 64 * h + p, OOB), "l_o": np.where(p < 64, 768 + 64 * h + p, OOB),
        "l_g": np.where(p < 4, 1024 + 4 * p + h, OOB),
        "l_grow1": np.full(128, 1024 + 4 * 1 + h), "l_grow3": np.full(128, 1024 + 4 * 3 + h),
        "s_x": np.where(p < 64, SS + 256 + 64 * h + p, OOB), "s_B": SS + 512 + 128 * g + p, "s_C": SS + 768 + 128 * g + p,
        "s_z": np.where(p < 64, SS + 64 * h + p, OOB), "s_dt": np.where(p < 2, SS + 1024 + 4 * p + h, OOB),
        "s_dtrow0": np.full(128, SS + 1024 + h), "s_dtrow1": np.full(128, SS + 1024 + 4 + h),
        "s5_u0": np.where(p < 32, SS + 1032 + 64 * h + p, OOB), "s5_u1": np.where(p < 32, SS + 1032 + 64 * h + 32 + p, OOB),
    }
    idxB = np.zeros((128, NIB), np.int64)
    for name, r in rows.items():
        for qp in range(4):
            for pc, w in enumerate((NLAT, NCTX)):
                off = (A_LAT_OFF, A_CTX_OFF)[pc]
                rr = np.maximum(r, 0)
                idxB[:, (OPIDX[name] * 4 + qp) * 2 + pc] = _gaddr(off + rr * w, qp, CHA) // w
    idxC = np.zeros((128, 7 * KC), np.int64)
    for pi, (pc0, pw) in enumerate(Y_PIECES):
        for k in range(KC):
            m, hp = k // 2, 2 * (k % 2) + p // 64
            idxC[:, pi * KC + k] = _gaddr(Y_OFF[pi] + (q * 256 + m * 64 + (p % 64)) * pw, hp, CHY) // pw
    idxX = np.zeros((128, 2 * KC), np.int64)
    for k in range(KC):
        for side, qn_ in ((0, q - 1), (1, q + 1)):
            idxX[:, 2 * k + side] = ((qn_ if 0 <= qn_ <= 3 else q) * D + k * 128 + p)
    return idxB.astype(np.int32), idxC.astype(np.int32), idxX.astype(np.int32)


def _layer_weights(inp, NL, h):
    g = h // 2
    W = {name: [] for name, _ in LAYER_W}
    for l in range(NL):
        win = inp['w_in'][l]
        W["wmodA"].append(inp['w_mod'][l][:, 0:2 * D]); W["bmodA"].append(inp['b_mod'][l][0:2 * D].reshape(16, 128).T)
        W["wmodC"].append(inp['w_mod'][l][:, 2 * D:6 * D]); W["bmodC"].append(inp['b_mod'][l][2 * D:6 * D].reshape(32, 128).T)
        W["norm1"].append(inp['norm1'][l].reshape(8, 128).T); W["norm2"].append(inp['norm2'][l].reshape(8, 128).T)
        W["win"].append(win); W["winsw"].append(np.concatenate([win[:, 1440:1456], win[:, 1424:1440]], axis=1))
        W["qn"].append(inp['mla_q_norm'][l].reshape(2, 128).T); W["kvn"].append(inp['mla_kv_norm'][l].reshape(1, 128).T)
        wuq = inp['mla_w_uq'][l]; w4 = wuq.reshape(256, 4, 96)
        W["wuq"].append(wuq); W["wuqsw"].append(np.concatenate([w4[:, :, :64], w4[:, :, 80:96], w4[:, :, 64:80]], axis=2).reshape(256, 384))
        W["wukv"].append(inp['mla_w_ukv'][l].reshape(128, 4, 2, 64).transpose(0, 2, 1, 3).reshape(128, 512))
        W["snorm"].append(inp['ssd_norm'][l].reshape(2, 128).T); W["wglu"].append(inp['s5_w_glu'][l]); W["wout"].append(inp['w_out'][l])
        W["wup"].append(inp['ffn_w_up'][l]); W["convw"].append(inp['ffn_conv_w'][l].reshape(3, NFF, 128).transpose(2, 1, 0))
        W["wdown"].append(inp['ffn_w_down'][l])
        W["l_gbias"].append(inp['ml_gate_bias'][l][:, h].reshape(1, 4)); W["l_norm"].append(inp['ml_norm'][l][64 * h:64 * h + 64].reshape(1, 64))
        cw = np.zeros((128, 3, 4), np.float32)
        for blk, (ch0, n) in enumerate(((64 * h, 64), (256 + 128 * g, 128), (512 + 128 * g, 128))):
            cw[:n, blk, 0:3] = inp['ssd_conv_w'][l][:, ch0:ch0 + n].T
            cw[:n, blk, 3] = inp['ssd_conv_b'][l][ch0:ch0 + n]
        W["s_convw"].append(cw)
        W["s_dtbias"].append(inp['ssd_dt_bias'][l][:, h].reshape(1, 2)); W["s_alog"].append(inp['ssd_a_log'][l][:, h].reshape(1, 2))
        W["s_dskip"].append(inp['ssd_d'][l][h].reshape(1, 1))
        are = np.zeros((128, 4), np.float32); aim = np.zeros((128, 4), np.float32); ldt = np.zeros((128, 4), np.float32)
        bre = np.zeros((128, 2, 32), np.float32); bim = np.zeros((128, 2, 32), np.float32)
        cre = np.zeros((128, 2, 32), np.float32); cim = np.zeros((128, 2, 32), np.float32)
        dsk = np.zeros((32, 2), np.float32)
        for gp in range(2):
            for g2 in range(2):
                gg = 4 * h + 2 * gp + g2
                ps = slice(64 * g2, 64 * g2 + 64)
                for d in range(2):
                    are[ps, gp * 2 + d] = inp['s5_a_re'][l][d, gg]; aim[ps, gp * 2 + d] = inp['s5_a_im'][l][d, gg]
                    ldt[ps, gp * 2 + d] = inp['s5_log_dt'][l][d, gg]
                bre[ps, gp, 16 * g2:16 * g2 + 16] = inp['s5_b_re'][l][gg]; bim[ps, gp, 16 * g2:16 * g2 + 16] = inp['s5_b_im'][l][gg]
                cre[ps, gp, 16 * g2:16 * g2 + 16] = inp['s5_c_re'][l][gg].T; cim[ps, gp, 16 * g2:16 * g2 + 16] = inp['s5_c_im'][l][gg].T
                dsk[16 * g2:16 * g2 + 16, gp] = inp['s5_d'][l][16 * gg:16 * gg + 16]
        for nm, v in (("s5_are", are), ("s5_aim", aim), ("s5_ldt", ldt), ("s5_bre", bre), ("s5_bim", bim), ("s5_cre", cre), ("s5_cim", cim), ("s5_d", dsk)):
            W[nm].append(v)
    return {k: _c32(np.stack(v)) for k, v in W.items()}


def _fused_inputs(inp, NL=4):
    cos, sin = _rope_tables()
    maps = []
    shared = {}
    for k in range(8):
        b, q = k // 4, k % 4
        if q not in shared:
            shared[q] = _layer_weights(inp, NL, q)
        m = dict(shared[q])
        m["x0"] = _c32(np.concatenate([inp['x'][b, NLAT * q:NLAT * (q + 1)].T, inp['ctx'][b, NCTX * q:NCTX * (q + 1)].T], axis=1))
        m["cvec"] = _c32(np.stack([inp['c'][b].reshape(8, 128).T, inp['c_ctx'].reshape(8, 128).T], axis=-1))
        c_l = cos[NLAT * q:NLAT * (q + 1)].T
        s_l = sin[NLAT * q:NLAT * (q + 1)].T
        m["ropeC"] = _c32(np.concatenate([np.concatenate([c_l, c_l], 0), np.ones((32, NCTX), np.float32)], axis=1))
        m["ropeS"] = _c32(np.concatenate([np.concatenate([-s_l, s_l], 0), np.zeros((32, NCTX), np.float32)], axis=1))
        m["fnorm"] = _c32(inp['final_norm'].reshape(8, 128).T)
        m["hmask"] = _c32(np.broadcast_to(np.array([q > 0, q < 3, q > 0, q < 3], np.float32), (128, 4)))
        m["idxB"], m["idxC"], m["idxX"] = _index_tables(q, q)
        maps.append(m)
    return maps


_PROG = {}


def kernel(**inputs):
    inp = {k: np.asarray(v) for k, v in inputs.items()}
    if "f" not in _PROG:
        _PROG["f"] = build_fused(4)
    res = run_bass_kernel_spmd(_PROG["f"], _fused_inputs(inp, 4), core_ids=list(range(8)))
    out = np.stack([np.concatenate([res.results[4 * b + q]["xTo"].T for q in range(4)], axis=0) for b in range(2)])
    return np.ascontiguousarray(out, dtype=np.float32)
```

```python
import numpy as np
import concourse.bass as bass
import concourse.mybir as mybir
from concourse.bass_utils import run_bass_kernel_spmd
from contextlib import ExitStack

F32 = mybir.dt.float32
BF16 = mybir.dt.bfloat16
I32 = mybir.dt.int32
AF = mybir.ActivationFunctionType
ALU = mybir.AluOpType
AX = mybir.AxisListType

NDMASEM = 8
NCCSEM = 16
EPS = 1e-6


class _Op:
    __slots__ = ("eng", "fn", "deps", "id", "ticket", "dma", "has_dep", "pre", "sem", "cc", "info")


class Prog:
    ENGS = ("pe", "act", "dve", "pool", "sp")

    def __init__(self, nc):
        self.nc = nc
        self.st = ExitStack()
        self.ops = []
        self.state = {}
        self.ntile = 0
        self.evi = 0
        self.bar = set()
        self.nowaw_ops = set()
        self.arena = None
        self.aoff = 0
        self.last_by_eng = {}
        self.dma_recent = {}

    def sb(self, shape, dtype=F32, name=None):
        self.ntile += 1
        name = name or "t"
        return self.st.enter_context(self.nc.sbuf_tensor(f"{name}_{self.ntile}", list(shape), dtype))

    def ps(self, shape, dtype=F32, name=None):
        self.ntile += 1
        name = name or "p"
        return self.st.enter_context(self.nc.psum_tensor(f"{name}_{self.ntile}", list(shape), dtype))

    def arena_init(self, nbytes):
        self.arena = self.sb([128, nbytes // 4], F32, "arena")
        self.acap = nbytes
        self.aoff = 0

    def al(self, shape, dtype=F32, name=None):
        esz = 4 if dtype in (F32, I32) else 2
        nfree = 1
        for d_ in shape[1:]:
            nfree *= d_
        nb = (nfree * esz + 3) // 4 * 4
        assert self.aoff + nb <= self.acap, f"arena overflow {self.aoff + nb} > {self.acap}"
        v = self.arena[0:shape[0], self.aoff // 4:(self.aoff + nb) // 4]
        self.aoff += nb
        if esz == 2:
            v = v.bitcast(dtype)
        if len(shape) == 3:
            v = v.rearrange("p (a b) -> p a b", b=shape[2])
        elif len(shape) == 4:
            v = v.rearrange("p (a b c) -> p a b c", b=shape[2], c=shape[3])
        return v

    def mark(self):
        return self.aoff

    def release(self, mark):
        self.aoff = mark
        bar = set(self.last_by_eng.values())
        for lst in self.dma_recent.values():
            bar.update(lst)
        self.bar = bar

    def add(self, eng, fn, reads=(), writes=(), dma=False, cc=False, nowaw=False):
        op = _Op()
        op.eng = eng
        op.fn = fn
        op.id = len(self.ops)
        op.dma = dma
        op.cc = cc
        op.info = (tuple(reads), tuple(writes))
        op.has_dep = False
        op.ticket = None
        op.pre = None
        op.sem = None
        deps = set()
        for k in reads:
            s = self.state.setdefault(k, [[], []])
            deps.update(s[0])
        for k in writes:
            s = self.state.setdefault(k, [[], []])
            deps.update(s[1])
            if not nowaw:
                deps.update(s[0])
            else:
                deps.update(w for w in s[0] if w not in self.nowaw_ops)
        for k in reads:
            self.state[k][1].append(op.id)
        for k in writes:
            s = self.state[k]
            if nowaw and not s[1]:
                s[0] = s[0] + [op.id]
            else:
                s[0] = [op.id]
            s[1] = []
        if nowaw:
            self.nowaw_ops.add(op.id)
        deps.update(self.bar)
        deps.discard(op.id)
        op.deps = deps
        self.ops.append(op)
        if cc:
            self.dma_recent.setdefault("cc", []).append(op.id)
        elif dma:
            lst = self.dma_recent.setdefault(eng, [])
            lst.append(op.id)
            if len(lst) > NDMASEM:
                lst.pop(0)
        else:
            self.last_by_eng[eng] = op.id
        return op

    def mm(self, out, lhsT, rhs, start=True, stop=True, reads=(), writes=(), **kw):
        return self.add("pe", lambda e: e.matmul(out, lhsT, rhs, start=start, stop=stop, **kw), reads, writes)

    def tr(self, out, in_, ident, reads=(), writes=()):
        return self.add("pe", lambda e: e.transpose(out, in_, ident), reads, writes)

    def actf(self, out, in_, func, bias=None, scale=1.0, accum_out=None, reads=(), writes=()):
        kw = {}
        if bias is not None:
            kw["bias"] = bias
        if accum_out is not None:
            kw["accum_out"] = accum_out
        return self.add("act", lambda e: e.activation(out, in_, func, scale=scale, **kw), reads, writes)

    def dma(self, out, in_, reads=(), writes=(), eng="sp", nowaw=False, **kw):
        return self.add(eng, lambda e: e.dma_start(out=out, in_=in_, **kw), reads, writes, dma=True, nowaw=nowaw)

    def v(self, eng, name, *args, reads=(), writes=(), **kw):
        return self.add(eng, lambda e: getattr(e, name)(*args, **kw), reads, writes)

    def cast(self, out, in_, reads=(), writes=()):
        self.cvi = getattr(self, "cvi", 0) + 1
        m = self.cvi % 6
        if m in (0, 2, 4):
            return self.actf(out, in_, AF.Identity, reads=reads, writes=writes)
        if m in (1, 3):
            return self.v("dve", "tensor_copy", out, in_, reads=reads, writes=writes)
        return self.v("pool", "tensor_copy", out, in_, reads=reads, writes=writes)

    def evac(self, out, in_, scale=None, reads=(), writes=()):
        self.evi += 1
        if self.evi % 2 == 0:
            return self.actf(out, in_, AF.Identity, scale=(1.0 if scale is None else scale), reads=reads, writes=writes)
        if scale is None:
            return self.v("dve", "tensor_copy", out, in_, reads=reads, writes=writes)
        return self.v("dve", "tensor_scalar_mul", out, in_, scale, reads=reads, writes=writes)

    def emit(self):
        nc = self.nc
        ops = self.ops
        for op in ops:
            for d in op.deps:
                p = ops[d]
                if p.eng == "pe" and op.eng == "pe" and not p.dma and not op.dma:
                    continue
                p.has_dep = True
        cnt = {e: 0 for e in self.ENGS}
        dcnt = {e: 0 for e in self.ENGS}
        ncc = 0
        for op in ops:
            if op.cc:
                op.sem = ("cc", ncc % NCCSEM)
                op.ticket = ncc // NCCSEM + 1
                op.pre = (op.sem, op.ticket - 1) if op.ticket > 1 else None
                ncc += 1
            elif op.dma:
                i = dcnt[op.eng]
                dcnt[op.eng] += 1
                slot = i % NDMASEM
                val = 16 * (i // NDMASEM + 1)
                op.sem = ("d", op.eng, slot)
                op.ticket = val
                op.pre = (op.sem, val - 16) if val > 16 else None
            elif op.has_dep:
                cnt[op.eng] += 1
                op.sem = ("c", op.eng)
                op.ticket = cnt[op.eng]
        sems = {}
        for e in self.ENGS:
            sems[("c", e)] = self.st.enter_context(nc.semaphore(f"c_{e}"))
            if dcnt[e]:
                for s in range(NDMASEM):
                    sems[("d", e, s)] = self.st.enter_context(nc.semaphore(f"d_{e}_{s}"))
        for i in range(min(ncc, NCCSEM)):
            sems[("cc", i)] = self.st.enter_context(nc.semaphore(f"cc_{i}"))
        by_eng = {e: [op for op in ops if op.eng == e] for e in self.ENGS}
        self.stats = {e: len(by_eng[e]) for e in self.ENGS}

        def run(E, e):
            known = {}

            def wait(sk, val):
                if known.get(sk, 0) >= val:
                    return
                e.wait_ge(sems[sk], val)
                known[sk] = val

            last_dma = {}
            for op in by_eng[E]:
                if op.pre is not None:
                    wait(*op.pre)
                for d in sorted(op.deps):
                    p = ops[d]
                    if p.ticket is None:
                        continue
                    if (not p.dma) and (not op.dma) and p.eng == "pe" and E == "pe":
                        continue
                    wait(p.sem, p.ticket)
                try:
                    ins = op.fn(e)
                except Exception:
                    print("EMIT FAILED at op", op.id, op.eng, op.info, flush=True)
                    raise
                if op.cc:
                    ins.then_inc(sems[op.sem], 1)
                    last_dma[op.sem] = op.ticket
                elif op.dma:
                    ins.then_inc(sems[op.sem], 16)
                    last_dma[op.sem] = op.ticket
                elif op.ticket is not None:
                    ins.then_inc(sems[op.sem], 1)
            for sk, val in last_dma.items():
                wait(sk, val)

        with nc.Block() as block:
            @block.tensor
            def _(e):
                run("pe", e)

            @block.scalar
            def _(e):
                run("act", e)

            @block.vector
            def _(e):
                run("dve", e)

            @block.gpsimd
            def _(e):
                run("pool", e)

            @block.sync
            def _(e):
                run("sp", e)
        self.st.close()


D = 1024
KC = 8
B_ = 2
SEQ = 8192
CTX = 256
NLAT = 2048
NCTX = 64
NT = NLAT + NCTX
P_IN = 2744
DFF = 2816
NFF = 22
R_ML = 0
R_SS = 1040
R_Q = 2328
R_KN = 2712
R_V = 2968
R_KR = 3224
NPT = 3256


def _dram(nc, name, shape, kind="ExternalInput", dtype=F32):
    return nc.dram_tensor(name, list(shape), dtype, kind=kind).ap()


def _rms_rstd(P, ps_ss, n, nfeat, rstd, key_ps, key_out, tmp, key_tmp):
    P.v("dve", "tensor_scalar", tmp[:, :n], ps_ss[:, :n], 1.0 / nfeat, EPS, ALU.mult, ALU.add,
        reads=[key_ps], writes=[key_tmp])
    P.actf(tmp[:, :n], tmp[:, :n], AF.Sqrt, reads=[key_tmp], writes=[key_tmp])
    P.v("dve", "reciprocal", rstd[:, :n], tmp[:, :n], reads=[key_tmp], writes=[key_out])


def _mod_vectors(P, wmod, bmod_sb, cs, parts, wst, wst_keys, psb, psb_key, modv):
    for j, part in enumerate(parts):
        for k in range(KC):
            buf = k % 2
            P.dma(wst[buf][:, 0:1024], wmod[k * 128:(k + 1) * 128, part * 1024:(part + 1) * 1024],
                  writes=[wst_keys[buf]])
            for dc in range(KC):
                P.mm(psb[:, 2 * dc:2 * dc + 2], wst[buf][:, dc * 128:(dc + 1) * 128], cs[:, k, :],
                     start=(k == 0 and dc == 0), stop=(k == KC - 1), skip_group_check=True,
                     reads=[wst_keys[buf], "cs"], writes=[psb_key])
        for dc in range(KC):
            P.v("dve", "tensor_scalar", modv[:, j, dc, :], psb[:, 2 * dc:2 * dc + 2],
                bmod_sb[:, part * 8 + dc:part * 8 + dc + 1], None, ALU.add,
                reads=[psb_key, "bmod"], writes=["modv"])


def phase_A(P, E, l):
    m0 = P.mark()
    banks = E["banks"]
    xT = E["XS"][l % 2]
    wmod, bmod, norm1 = E["wmodA"][l], E["bmodA"][l], E["norm1"][l]
    win, winsw, qn, kvn = E["win"][l], E["winsw"][l], E["qn"][l], E["kvn"][l]
    wuq, wuqsw, wukv = E["wuq"][l], E["wuqsw"][l], E["wukv"][l]
    ropeC, ropeS = E["ropeC"], E["ropeS"]
    EXf = _flat(E["EXA"])
    PTl = EXf[A_LAT_OFF:A_LAT_OFF + NPT * NLAT].rearrange("(r w) -> r w", w=NLAT)
    PTc = EXf[A_CTX_OFF:A_CTX_OFF + NPT * NCTX].rearrange("(r w) -> r w", w=NCTX)

    def PTdst(r0, r1, c0, n, seg):
        return PTl[r0:r1, c0:c0 + n] if seg == 0 else PTc[r0:r1, 0:n]
    ones_b, cs = E["ones_b"], E["cs"]
    bmod_sb = P.al([128, 16], F32)
    P.dma(bmod_sb[:], bmod, writes=["bmod"])
    n1 = P.al([128, KC], F32)
    P.dma(n1[:], norm1, writes=["n1"])
    qn_sb = P.al([128, 2], F32)
    P.dma(qn_sb[:], qn, writes=["qn"])
    kvn_sb = P.al([128, 1], F32)
    P.dma(kvn_sb[:], kvn, writes=["kvn"])
    tabC = P.al([96, NT], F32)
    tabS = P.al([96, NT], F32)
    P.dma(tabC[64:96, :], ropeC, writes=["tab"])
    P.dma(tabS[64:96, :], ropeS, writes=["tab"])
    P.dma(tabC[0:32, :], ropeC, writes=["tab"])
    P.dma(tabS[0:32, :], ropeS, writes=["tab"])

    wst = [P.al([128, P_IN], F32, f"wst{i}") for i in range(2)]
    wst_keys = ["wst0", "wst1"]
    psb = banks[6]
    modv = P.al([128, 2, KC, 2], F32, "modv")
    _mod_vectors(P, wmod, bmod_sb, cs, [0, 1], wst, wst_keys, psb, "bank6", modv)
    Amod = P.al([128, KC, 2], F32, "Amod")
    P.v("dve", "tensor_scalar_add", Amod[:], modv[:, 1, :, :], 1.0, reads=["modv"], writes=["Amod"])
    for s_ in range(2):
        P.v("dve", "tensor_tensor", Amod[:, :, s_], Amod[:, :, s_], n1[:], ALU.mult,
            reads=["Amod", "n1"], writes=["Amod"])

    win_b = P.al([128, KC, P_IN], BF16, "winb")
    winsw_b = P.al([128, KC, 32], BF16, "winswb")
    for k in range(KC):
        buf = k % 2
        P.dma(wst[buf][:, :], win[k * 128:(k + 1) * 128, :], writes=[wst_keys[buf]])
        P.cast(win_b[:, k, :], wst[buf][:, :], reads=[wst_keys[buf]], writes=["winb"])
    sm = P.al([128, KC, 32], F32, "smallst")
    P.dma(sm[:], winsw.rearrange("(k p) c -> p k c", p=128), writes=["smallst"])
    P.v("pool", "tensor_copy", winsw_b[:], sm[:], reads=["smallst"], writes=["winb"])
    wuq_b = P.al([128, 2, 384], BF16, "wuqb")
    wuqsw_b = P.al([128, 2, 384], BF16, "wuqswb")
    wukv_b = P.al([128, 512], BF16, "wukvb")
    st2 = P.al([128, 2, 384], F32, "st2")
    P.dma(st2[:], wuq.rearrange("(k p) c -> p k c", p=128), writes=["st2"])
    P.v("pool", "tensor_copy", wuq_b[:], st2[:], reads=["st2"], writes=["wmla"])
    P.dma(st2[:], wuqsw.rearrange("(k p) c -> p k c", p=128), writes=["st2"])
    P.v("pool", "tensor_copy", wuqsw_b[:], st2[:], reads=["st2"], writes=["wmla"])
    P.dma(st2[:, 0, :], wukv[:, 0:384], writes=["st2"])
    P.dma(st2[:, 1, 0:128], wukv[:, 384:512], writes=["st2"])
    P.v("pool", "tensor_copy", wukv_b[:, 0:384], st2[:, 0, :], reads=["st2"], writes=["wmla"])
    P.v("pool", "tensor_copy", wukv_b[:, 384:512], st2[:, 1, 0:128], reads=["st2"], writes=["wmla"])

    xt = [P.al([128, KC, 512], F32, f"xt{i}") for i in range(2)]
    xsq = P.al([128, KC, 512], BF16, "xsq")
    hT = P.al([128, KC, 512], BF16, "hT")
    tmpn = P.al([128, 512], F32, "tmpn")
    rstd = P.al([128, 512], F32, "rstd")
    tmpx = [P.al([128, 512], F32, f"tmpx{i}") for i in range(2)]
    stage = [P.al([128, 512], F32, f"stg{i}") for i in range(4)]
    cq = P.al([128, 2, 512], F32, "cq")
    ckv = P.al([128, 512], F32, "ckv")
    krr = P.al([32, 2, 512], F32, "krr")
    cqn = P.al([128, 2, 512], BF16, "cqn")
    ckvn = P.al([128, 512], BF16, "ckvn")
    sq2 = P.al([128, 2, 512], BF16, "sq2")
    rt1 = P.al([96, 512], F32, "rt1")
    rt2 = P.al([96, 512], F32, "rt2")
    pss = banks[7]
    psm = banks[0:4]
    psq = banks[4:6]

    tiles = [(i * 512, 512, 0) for i in range(4)] + [(NLAT, NCTX, 1)]
    nstage = 0
    nps = 0
    for ti, (c0, n, seg) in enumerate(tiles):
        xb = xt[ti % 2]
        xk = f"xt{ti % 2}"
        P.dma(xb[:, :, :n], xT[:, c0:c0 + n].rearrange("(k p) n -> p k n", p=128), reads=[f"XS{l % 2}"], writes=[xk])
        P.actf(xsq[:, :, :n], xb[:, :, :n], AF.Square, reads=[xk], writes=["xsq"])
        for k in range(KC):
            P.mm(pss[:, :n], ones_b[:], xsq[:, k, :n], start=(k == 0), stop=(k == KC - 1),
                 reads=["ones", "xsq"], writes=["bank7"])
        _rms_rstd(P, pss, n, D, rstd, "bank7", "rstd", tmpn, "tmpn")
        for k in range(KC):
            tb = tmpx[k % 2]
            tk = f"tmpx{k % 2}"
            P.v("dve", "tensor_tensor", tb[:, :n], xb[:, k, :n], rstd[:, :n], ALU.mult,
                reads=[xk, "rstd"], writes=[tk])
            P.actf(hT[:, k, :n], tb[:, :n], AF.Identity, bias=modv[:, 0, k, seg:seg + 1],
                   scale=Amod[:, k, seg:seg + 1], reads=[tk, "Amod", "modv"], writes=["hT"])

        def inproj(col0, m, wsrc=None):
            nonlocal nps
            ps = psm[nps % 4]
            pk = f"bank{nps % 4}"
            nps += 1
            for k in range(KC):
                lhs = win_b[:, k, col0:col0 + m] if wsrc is None else wsrc[:, k, 0:m]
                P.mm(ps[:m, :n], lhs, hT[:, k, :n], start=(k == 0), stop=(k == KC - 1),
                     reads=["winb", "hT"], writes=[pk])
            return ps, pk

        def out_chunk(ps, pk, m, row0, scale=None):
            nonlocal nstage
            sg = stage[nstage % 4]
            sk = f"stg{nstage % 4}"
            nstage += 1
            P.evac(sg[:m, :n], ps[:m, :n], scale=scale, reads=[pk], writes=[sk])
            P.dma(PTdst(row0, row0 + m, c0, n, seg), sg[:m, :n], reads=[sk], writes=["PT"], nowaw=True)

        for i in range(9):
            col0 = i * 128
            m = 128 if i < 8 else 16
            ps, pk = inproj(col0, m)
            out_chunk(ps, pk, m, R_ML + col0, scale=(0.125 if i in (2, 3) else None))
        for i in range(11):
            col0 = 1456 + i * 128
            m = 128 if i < 10 else 8
            ps, pk = inproj(col0, m)
            out_chunk(ps, pk, m, R_SS + i * 128)
        for j in range(2):
            ps, pk = inproj(1040 + j * 128, 128)
            P.evac(cq[:, j, :n], ps[:, :n], reads=[pk], writes=["cq"])
        ps, pk = inproj(1296, 128)
        P.evac(ckv[:, :n], ps[:, :n], reads=[pk], writes=["ckv"])
        ps, pk = inproj(1424, 32)
        P.evac(krr[:, 0, :n], ps[:32, :n], reads=[pk], writes=["krr"])
        ps, pk = inproj(0, 32, wsrc=winsw_b)
        P.evac(krr[:, 1, :n], ps[:32, :n], reads=[pk], writes=["krr"])
        P.actf(sq2[:, :, :n], cq[:, :, :n], AF.Square, reads=["cq"], writes=["sq2"])
        for j in range(2):
            P.mm(pss[:, :n], ones_b[:], sq2[:, j, :n], start=(j == 0), stop=(j == 1),
                 reads=["ones", "sq2"], writes=["bank7"])
        _rms_rstd(P, pss, n, 256, rstd, "bank7", "rstd", tmpn, "tmpn")
        for j in range(2):
            tb = tmpx[j % 2]
            tk = f"tmpx{j % 2}"
            P.v("dve", "tensor_tensor", tb[:, :n], cq[:, j, :n], rstd[:, :n], ALU.mult,
                reads=["cq", "rstd"], writes=[tk])
            P.actf(cqn[:, j, :n], tb[:, :n], AF.Identity, scale=qn_sb[:, j:j + 1], reads=[tk, "qn"], writes=["cqn"])
        P.actf(sq2[:, 0, :n], ckv[:, :n], AF.Square, reads=["ckv"], writes=["sq2"])
        P.mm(pss[:, :n], ones_b[:], sq2[:, 0, :n], reads=["ones", "sq2"], writes=["bank7"])
        _rms_rstd(P, pss, n, 128, rstd, "bank7", "rstd", tmpn, "tmpn")
        P.v("dve", "tensor_tensor", tmpx[0][:, :n], ckv[:, :n], rstd[:, :n], ALU.mult,
            reads=["ckv", "rstd"], writes=["tmpx0"])
        P.actf(ckvn[:, :n], tmpx[0][:, :n], AF.Identity, scale=kvn_sb[:, 0:1], reads=["tmpx0", "kvn"], writes=["ckvn"])
        for h in range(4):
            pa, pb = psq[0], psq[1]
            for j in range(2):
                P.mm(pa[:96, :n], wuq_b[:, j, h * 96:(h + 1) * 96], cqn[:, j, :n], start=(j == 0), stop=(j == 1),
                     reads=["wmla", "cqn"], writes=["bank4"])
            for j in range(2):
                P.mm(pb[:96, :n], wuqsw_b[:, j, h * 96:(h + 1) * 96], cqn[:, j, :n], start=(j == 0), stop=(j == 1),
                     reads=["wmla", "cqn"], writes=["bank5"])
            sg = stage[nstage % 4]
            sk = f"stg{nstage % 4}"
            nstage += 1
            P.actf(sg[0:64, :n], pa[0:64, :n], AF.Identity, reads=["bank4"], writes=[sk])
            P.v("dve", "tensor_tensor", rt1[64:96, :n], pa[64:96, :n], tabC[64:96, c0:c0 + n], ALU.mult,
                reads=["bank4", "tab"], writes=["rt1"])
            P.v("dve", "tensor_tensor", rt2[64:96, :n], pb[64:96, :n], tabS[64:96, c0:c0 + n], ALU.mult,
                reads=["bank5", "tab"], writes=["rt2"])
            P.v("pool", "tensor_tensor", sg[64:96, :n], rt1[64:96, :n], rt2[64:96, :n], ALU.add,
                reads=["rt1", "rt2"], writes=[sk])
            P.dma(PTdst(R_Q + h * 96, R_Q + (h + 1) * 96, c0, n, seg), sg[:96, :n], reads=[sk], writes=["PT"], nowaw=True)
        for c in range(2):
            for which, row0 in ((0, R_KN), (1, R_V)):
                ps = psm[nps % 4]
                pk = f"bank{nps % 4}"
                nps += 1
                P.mm(ps[:, :n], wukv_b[:, which * 256 + c * 128:which * 256 + (c + 1) * 128], ckvn[:, :n],
                     reads=["wmla", "ckvn"], writes=[pk])
                out_chunk(ps, pk, 128, row0 + c * 128)
        sg = stage[nstage % 4]
        sk = f"stg{nstage % 4}"
        nstage += 1
        P.v("dve", "tensor_tensor", rt1[0:32, :n], krr[:, 0, :n], tabC[0:32, c0:c0 + n], ALU.mult,
            reads=["krr", "tab"], writes=["rt1"])
        P.v("dve", "tensor_tensor", rt2[0:32, :n], krr[:, 1, :n], tabS[0:32, c0:c0 + n], ALU.mult,
            reads=["krr", "tab"], writes=["rt2"])
        P.v("pool", "tensor_tensor", sg[0:32, :n], rt1[0:32, :n], rt2[0:32, :n], ALU.add,
            reads=["rt1", "rt2"], writes=[sk])
        P.dma(PTdst(R_KR, R_KR + 32, c0, n, seg), sg[:32, :n], reads=[sk], writes=["PT"], nowaw=True)
    P.release(m0)


TALL = CTX + SEQ
NCH = TALL // 64
NJ = TALL // 128
CT = 512
EXT = NLAT + 2 + NCTX + 2
LAT0, LAT1 = 0, NLAT + 2
CX0, CX1 = NLAT + 2, EXT


def _col_tiles(a, b, w=CT):
    out = []
    c = a
    while c < b:
        out.append((c, min(w, b - c)))
        c += w
    return out


CHA = 262144
NCHA = 27
RA_PAD = NCHA * CHA
A_LAT_OFF, A_CTX_OFF = 0, NPT * NLAT
Y_PIECES = [(0, 512), (512, 512), (1024, 2), (1024, 512), (1536, 512), (2048, 2), (2050, 66)]
CHY = 15 * 16896
Y_OFF = []
_o = 0
for (_c, _w) in Y_PIECES:
    _o = -(-_o // 16896) * 16896
    Y_OFF.append(_o)
    _o += D * _w
NCHY = -(-_o // CHY)
RY_PAD = NCHY * CHY
assert NPT * NT <= RA_PAD and A_CTX_OFF % NCTX == 0 and CHA % NLAT == 0


def _gaddr(f, rank, ch):
    return (f // ch) * 4 * ch + rank * ch + (f % ch)


def _flat(ap):
    return ap.rearrange("a b -> (a b)")


SSR = R_SS
OP_ROWS = {
    "m_QT": [(R_Q, R_Q + 384)], "m_KT": [(R_KN, R_KN + 256), (R_KR, R_KR + 32)], "m_V": [(R_V, R_V + 256)],
    "l_q": [(0, 256)], "l_k": [(256, 512)], "l_v": [(512, 768)], "l_o": [(768, 1024)],
    "l_g": [(1024, 1040)], "l_grow1": [(1024, 1040)], "l_grow3": [(1024, 1040)],
    "s_x": [(SSR + 256, SSR + 512)], "s_B": [(SSR + 512, SSR + 768)], "s_C": [(SSR + 768, SSR + 1024)], "s_z": [(SSR, SSR + 256)],
    "s_dt": [(SSR + 1024, SSR + 1032)], "s_dtrow0": [(SSR + 1024, SSR + 1032)], "s_dtrow1": [(SSR + 1024, SSR + 1032)],
    "s5_u0": [(SSR + 1032, SSR + 1288)], "s5_u1": [(SSR + 1032, SSR + 1288)],
}


def _op_chunks(op):
    cs = set()
    for (a, b) in OP_ROWS[op]:
        for r in (a, b - 1):
            pass
        cs.update(range((a * NLAT) // CHA, ((b - 1) * NLAT) // CHA + 1))
        cs.update(range((A_CTX_OFF + a * NCTX) // CHA, (A_CTX_OFF + (b - 1) * NCTX) // CHA + 1))
    return sorted(cs)


OPS_B = ["m_QT", "m_KT", "m_V", "l_q", "l_k", "l_v", "l_o", "l_g", "l_grow1", "l_grow3",
         "s_x", "s_B", "s_C", "s_z", "s_dt", "s_dtrow0", "s_dtrow1", "s5_u0", "s5_u1"]
OPIDX = {n: i for i, n in enumerate(OPS_B)}
NIB = 8 * len(OPS_B)


def gather_fm(P, E, dst, op, nr, key):
    it = E["idxB"]
    Gf = _flat(E["GEXA"])
    for qp in range(4):
        for pc, (o0, o1, w, eoff) in enumerate(((CTX + NLAT * qp, CTX + NLAT * (qp + 1), NLAT, 0),
                                                (NCTX * qp, NCTX * (qp + 1), NCTX, 0))):
            c = (OPIDX[op] * 4 + qp) * 2 + pc
            src = Gf.rearrange("(r w) -> r w", w=w)
            P.add("pool", (lambda c, o0, o1, src, eoff: (lambda e: e.indirect_dma_start(
                out=dst[0:nr, o0:o1], out_offset=None, in_=src,
                in_offset=bass.IndirectOffsetOnAxis(ap=it[0:nr, c:c + 1], axis=0), element_offset=eoff)))(c, o0, o1, src, eoff),
                reads=[f"GEXA{c_}" for c_ in _op_chunks(op)] + ["idxB"], writes=[key], dma=True, nowaw=True)


def fm_to_tok(P, E, src, r, dst, key_src, key_dst):
    B, ident = E["banks"], E["ident"]
    for g in range(0, NJ, 4):
        ps, pk = B[6 + (g // 4) % 2], f"bank{6 + (g // 4) % 2}"
        nj = min(4, NJ - g)
        for jj in range(nj):
            j = g + jj
            P.tr(ps[:, 128 * jj:128 * jj + r], src[0:r, 128 * j:128 * j + 128], ident[0:r, 0:r], reads=[key_src, "ident"], writes=[pk])
        view = ps[:, 0:128 * nj].rearrange("p (j e) -> p j e", e=128)[:, :, 0:r]
        P.evac(dst[:, g:g + nj, 0:r], view, reads=[pk], writes=[key_dst])


def tok_to_fm(P, E, src, dst, key_src, key_dst):
    B, ident = E["banks"], E["ident"]
    for g in range(0, NJ, 4):
        ps, pk = B[6 + (g // 4) % 2], f"bank{6 + (g // 4) % 2}"
        nj = min(4, NJ - g)
        for jj in range(nj):
            P.tr(ps[0:64, 128 * jj:128 * jj + 128], src[:, g + jj, :], ident[:, :], reads=[key_src, "ident"], writes=[pk])
        P.evac(dst[0:64, 128 * g:128 * (g + nj)], ps[0:64, 0:128 * nj], reads=[pk], writes=[key_dst])


def store_cols(P, E, row0, nr, src_fn, col0, n, key):
    Yf = _flat(E["EXY"])
    a, b = col0, col0 + n
    for q in range(4):
        segs = []
        lo, hi = max(a, CTX + NLAT * q - 1, CTX), min(b, CTX + NLAT * (q + 1) + 1, TALL)
        if lo < hi:
            segs.append((lo, hi, lo - (CTX + NLAT * q - 1)))
        lo, hi = max(a, NCTX * q - 1, 0), min(b, NCTX * (q + 1) + 1, CTX)
        if lo < hi:
            segs.append((lo, hi, CX0 + lo - (NCTX * q - 1)))
        for (lo, hi, e0) in segs:
            for pi, (pc0, pw) in enumerate(Y_PIECES):
                x0, x1 = max(e0, pc0), min(e0 + (hi - lo), pc0 + pw)
                if x0 < x1:
                    piece = Yf[Y_OFF[pi]:Y_OFF[pi] + D * pw].rearrange("(r w) -> r w", w=pw)
                    P.dma(piece[q * 256 + row0:q * 256 + row0 + nr, x0 - pc0:x1 - pc0], src_fn(lo + (x0 - e0), lo + (x1 - e0)),
                          reads=[key], writes=["EXY"], nowaw=True, allow_slow_non_contiguous=True)


def b_mla(P, E, l, C, after_first_gather=None):
    m0 = P.mark()
    QT = P.al([97, TALL], BF16)
    KT = P.al([97, TALL], BF16)
    Vt = P.al([128, NJ, 65], BF16)
    kst = P.al([96, TALL], F32)
    sq = P.al([96, 2112], BF16)
    kmx = P.al([128, 20], F32)
    nkm = P.al([128, 1], F32)
    tmpq = P.al([128, 512], F32)
    pT = [P.al([128, 512], BF16) for _ in range(4)]
    drow = P.al([65, 512], F32)
    rden = P.al([64, 512], F32)
    ost = [P.al([64, 512], F32) for _ in range(2)]
    ones_b, ones_f = C["ones_b"], C["ones_f"]
    B = C["banks"]
    scale = 96.0 ** -0.5
    P.v("pool", "memset", KT[96:97, :], 1.0, writes=["KT"])
    P.v("pool", "memset", kmx[:], 0.0, writes=["kmx"])
    ti = 0
    gather_fm(P, E, kst, "m_KT", 96, "mkst")
    if after_first_gather is not None:
        after_first_gather()
    for ci in range(4):
        c0 = ci * 2112
        sb_, sk = kst[:, c0:c0 + 2112], "mkst"
        P.v("dve", "tensor_copy", KT[0:96, c0:c0 + 2112], sb_[0:96, :], reads=[sk], writes=["KT"])
        P.actf(sq[:, :], sb_[0:96, :], AF.Square, reads=[sk], writes=["msq"])
        for (t0, n) in _col_tiles(0, 2112):
            ps, pk = B[6 + ti % 2], f"bank{6 + ti % 2}"
            P.mm(ps[:, :n], ones_b[0:96, :], sq[:, t0:t0 + n], reads=["ones", "msq"], writes=[pk])
            P.v("dve", "reduce_max", kmx[:, ti:ti + 1], ps[:, :n], AX.X, reads=[pk], writes=["kmx"])
            ti += 1
    P.v("dve", "reduce_max", nkm[:], kmx[:], AX.X, reads=["kmx"], writes=["nkm"])
    P.actf(nkm[:], nkm[:], AF.Sqrt, reads=["nkm"], writes=["nkm"])
    P.v("dve", "tensor_scalar_mul", nkm[:], nkm[:], -1.0, reads=["nkm"], writes=["nkm"])
    ti = 0
    gather_fm(P, E, kst, "m_QT", 96, "mkst")
    for ci in range(4):
        c0 = ci * 2112
        sb_, sk = kst[:, c0:c0 + 2112], "mkst"
        P.v("dve", "tensor_copy", QT[0:96, c0:c0 + 2112], sb_[0:96, :], reads=[sk], writes=["QT"])
        P.actf(sq[:, :], sb_[0:96, :], AF.Square, reads=[sk], writes=["msq"])
        for (t0, n) in _col_tiles(0, 2112):
            ps, pk = B[6 + ti % 2], f"bank{6 + ti % 2}"
            P.mm(ps[:, :n], ones_b[0:96, :], sq[:, t0:t0 + n], reads=["ones", "msq"], writes=[pk])
            P.actf(tmpq[96:97, :n], ps[96:97, :n], AF.Sqrt, reads=[pk], writes=["tmpq"])
            P.v("dve", "tensor_scalar_mul", QT[96:97, c0 + t0:c0 + t0 + n], tmpq[96:97, :n], nkm[96:97, 0:1],
                reads=["tmpq", "nkm"], writes=["QT"])
            ti += 1
    gather_fm(P, E, kst, "m_V", 64, "mkst")
    fm_to_tok(P, E, kst, 64, Vt, "mkst", "Vt")
    P.v("pool", "memset", Vt[:, :, 64:65], 1.0, writes=["Vt"])
    qtiles = [(0, 256, [0, 1])] + [(c0, n, list(range(NJ))) for (c0, n) in _col_tiles(256, TALL)]
    it = 0
    LOOK = 2
    for qi, (q0, n, kbs) in enumerate(qtiles):
        po, pok = B[4 + qi % 2], f"bank{4 + qi % 2}"
        slots = []

        def score(kb):
            nonlocal it
            ps, pk = B[it % 4], f"bank{it % 4}"
            pt, ptk = pT[it % 4], f"pT{it % 4}"
            it += 1
            P.mm(ps[:, :n], KT[:, kb * 128:(kb + 1) * 128], QT[:, q0:q0 + n], reads=["KT", "QT"], writes=[pk])
            P.actf(pt[:, :n], ps[:, :n], AF.Exp, scale=scale, reads=[pk], writes=[ptk])
            slots.append((pt, ptk))

        for ki in range(min(LOOK, len(kbs))):
            score(kbs[ki])
        for ki, kb in enumerate(kbs):
            if ki + LOOK < len(kbs):
                score(kbs[ki + LOOK])
            pt, ptk = slots[ki]
            P.mm(po[0:65, :n], Vt[:, kb, :], pt[:, :n], start=(ki == 0), stop=(ki == len(kbs) - 1),
                 reads=["Vt", ptk], writes=[pok])
        P.v("dve", "tensor_copy", drow[64:65, :n], po[64:65, :n], reads=[pok], writes=["drow"])
        pb, pbk = B[6 + qi % 2], f"bank{6 + qi % 2}"
        P.mm(pb[0:64, :n], ones_f[64:65, 0:64], drow[64:65, :n], reads=["onesf", "drow"], writes=[pbk])
        P.v("dve", "reciprocal", rden[:, :n], pb[0:64, :n], reads=[pbk], writes=["rden"])
        o, ok_ = ost[qi % 2], f"most{qi % 2}"
        P.v("dve", "tensor_tensor", o[:, :n], po[0:64, :n], rden[:, :n], ALU.mult, reads=[pok, "rden"], writes=[ok_])
        store_cols(P, E, 64, 64, (lambda a, b, o=o, q0=q0: o[:, a - q0:b - q0]), q0, n, ok_)
    P.release(m0)


def b_consts(P):
    C = {}
    C["banks"] = [P.ps([128, 512], F32, f"bank{i}") for i in range(8)]
    ones_b = P.sb([128, 128], BF16, "ones_b")
    ones_f = P.sb([128, 128], F32, "ones_f")
    ident = P.sb([128, 128], F32, "ident")
    triF = P.sb([128, 128], F32, "triF")
    triB = P.sb([128, 128], F32, "triB")
    mTF = P.sb([128, 64], F32, "mTF")
    mTB = P.sb([128, 64], F32, "mTB")
    mRF = P.sb([128, 512], F32, "mRF")
    mRB = P.sb([128, 512], F32, "mRB")
    P.v("pool", "memset", ones_b[:], 1.0, writes=["ones"])
    P.v("pool", "memset", ones_f[:], 1.0, writes=["onesf"])
    P.v("pool", "memset", ident[:], 0.0, writes=["ident"])
    P.add("pool", lambda e: e.affine_select(out=ident[:], in_=ident[:], pattern=[[-1, 128]], compare_op=ALU.not_equal,
                                            fill=1.0, base=0, channel_multiplier=1), reads=["ident"], writes=["ident"])
    for t, key, cm, st in ((triF, "triF", -1, 1), (triB, "triB", 1, -1)):
        P.v("pool", "memset", t[:], 0.0, writes=[key])
        for h in range(2):
            blk = t[64 * h:64 * h + 64, 64 * h:64 * h + 64]
            P.v("pool", "memset", blk, 1.0, reads=[key], writes=[key])
            P.add("pool", (lambda blk, cm, st: (lambda e: e.affine_select(out=blk, in_=blk, pattern=[[st, 64]],
                  compare_op=ALU.is_ge, fill=0.0, base=0, channel_multiplier=cm)))(blk, cm, st), reads=[key], writes=[key])
    for t, key, cm, st in ((mTF, "mTF", -1, 1), (mTB, "mTB", 1, -1)):
        P.v("pool", "memset", t[:], 0.0, writes=[key])
        for h in range(2):
            blk = t[64 * h:64 * h + 64, :]
            P.add("pool", (lambda blk, cm, st: (lambda e: e.affine_select(out=blk, in_=blk, pattern=[[st, 64]],
                  compare_op=ALU.is_ge, fill=-30000.0, base=0, channel_multiplier=cm)))(blk, cm, st), reads=[key], writes=[key])
    P.v("pool", "memset", mRF[:], 1.0, writes=["mRF"])
    P.v("pool", "memset", mRF[:, 0::64], 0.0, reads=["mRF"], writes=["mRF"])
    P.v("pool", "memset", mRB[:], 1.0, writes=["mRB"])
    P.v("pool", "memset", mRB[:, 63::64], 0.0, reads=["mRB"], writes=["mRB"])
    C.update(ones_b=ones_b, ones_f=ones_f, ident=ident, triF=triF, triB=triB, mTF=mTF, mTB=mTB, mRF=mRF, mRB=mRB)
    return C


def dla(P, C, dv1, qT, kT, ktok, get_vb, make_gates, finish_dir, tag, NUM, kNUM, accum=False):
    B = C["banks"]
    dk = 128
    brow = P.al([128, TALL], F32)
    lftok = P.al([128, NJ], F32)
    igtok = P.al([128, NJ], F32)
    negb = P.al([128, NJ], F32)
    wtok = P.al([128, NJ], F32)
    dec = P.al([128, NCH], F32)
    qTb = P.al([dk, TALL], BF16)
    Crun = [P.al([dk, dv1], F32) for _ in range(4)]
    Dt = [P.al([128, 64], F32) for _ in range(4)]
    Dm = [P.al([128, 64], F32) for _ in range(4)]
    pTt = [P.al([128, 64], BF16) for _ in range(4)]
    etmp = [P.al([128, 512], F32) for _ in range(2)]
    K = lambda s: f"{tag}_{s}"
    mdir = P.mark()
    for d in range(2):
        rev = d == 1
        tri = C["triB"] if rev else C["triF"]
        mT = C["mTB"] if rev else C["mTF"]
        mR = C["mRB"] if rev else C["mRF"]
        lfrow = P.al([128, TALL], F32)
        has_ig = make_gates(d, lfrow, lftok, igtok, K("lfrow"), K("lftok"), K("igtok"))
        vb, vbk = get_vb(d)
        for ti, (c0, n) in enumerate(_col_tiles(0, TALL)):
            o_, a_, b_ = brow[:, c0:c0 + n], mR[:, :n], lfrow[:, c0:c0 + n]
            if rev:
                o_, a_, b_ = o_[:, ::-1], a_[:, ::-1], b_[:, ::-1]
            P.v("dve", "tensor_tensor_scan", o_, a_, b_, 0.0, ALU.mult, ALU.add,
                reads=[K("lfrow"), "mR"], writes=[K("brow")])
            et, ek = etmp[ti % 2], K(f"etmp{ti % 2}")
            P.actf(et[:, :n], brow[:, c0:c0 + n], AF.Exp, reads=[K("brow")], writes=[ek])
            P.v("dve", "tensor_tensor", qTb[:, c0:c0 + n], qT[:, c0:c0 + n], et[:, :n], ALU.mult,
                reads=[K("qT"), ek], writes=[K("qTb")])
        P.release(mdir)
        kw = P.al([128, NJ, dk], BF16)
        Call = P.al([dk, NCH, dv1], BF16)
        P.actf(dec[:, :], brow[:, (0 if rev else 63)::64], AF.Exp, reads=[K("brow")], writes=[K("dec")])
        pm, pmk = B[6], "bank6"
        P.mm(pm[:, 0:NJ], tri[:, :], lftok[:, :], reads=["tri", K("lftok")], writes=[pmk])
        if has_ig:
            P.v("dve", "tensor_tensor", negb[:, :], igtok[:, :], pm[:, 0:NJ], ALU.subtract, reads=[pmk, K("igtok")], writes=[K("negb")])
        else:
            P.v("dve", "tensor_scalar_mul", negb[:, :], pm[:, 0:NJ], -1.0, reads=[pmk], writes=[K("negb")])
        for h in range(2):
            off = 64 * h + (0 if rev else 63)
            P.v("dve", "tensor_tensor", wtok[64 * h:64 * h + 64, :], negb[64 * h:64 * h + 64, :],
                brow[64 * h:64 * h + 64, off::128], ALU.add, reads=[K("negb"), K("brow")], writes=[K("wtok")])
        P.actf(wtok[:, :], wtok[:, :], AF.Exp, reads=[K("wtok")], writes=[K("wtok")])
        P.v("dve", "tensor_tensor", kw[:, :, :], ktok[:, :, :], wtok[:, :].unsqueeze(2).to_broadcast([128, NJ, dk]), ALU.mult,
            reads=[K("ktok"), K("wtok")], writes=[K("kw")])
        P.v("pool", "memset", Crun[0][:, :], 0.0, writes=[K("Crun0")])
        order = list(range(NCH)) if not rev else [3, 2, 1, 0] + list(range(NCH - 1, 3, -1))

        def slots(it, c):
            pb = 64 * (c % 2)
            sl = (it // 2) % 7
            s8 = (it // 2) % 8
            par = 3 * (c % 2)
            psU = B[par + 2][0:dk, 65 * sl:65 * sl + dv1]
            psN = B[par + 1][pb:pb + 64, 65 * sl:65 * sl + dv1]
            psS = B[par + 0][pb:pb + 64, 64 * s8:64 * s8 + 64]
            return pb, c // 2, slice(64 * c, 64 * c + 64), it % 4, psU, psN, psS, f"psU{par}_{sl}", f"psN{par}_{sl}", f"psS{par}_{s8}"

        def stage1(it, c):
            pb, j, cols, r4, psU, psN, psS, kU, kN, kS = slots(it, c)
            P.mm(psU, kw[pb:pb + 64, j, :], vb[pb:pb + 64, j, :], reads=[K("kw"), vbk], writes=[kU])
            P.mm(psS, kT[:, cols], qT[:, cols], reads=[K("kT"), K("qT")], writes=[kS])
            P.v("pool", "tensor_tensor", Dm[r4][pb:pb + 64, :], brow[pb:pb + 64, cols], mT[pb:pb + 64, :], ALU.add,
                reads=[K("brow"), "mT"], writes=[K(f"Dm{r4}")])
            P.actf(Dt[r4][pb:pb + 64, :], Dm[r4][pb:pb + 64, :], AF.Exp, bias=negb[pb:pb + 64, j:j + 1],
                   reads=[K(f"Dm{r4}"), K("negb")], writes=[K(f"Dt{r4}")])
            P.v("dve", "tensor_tensor", pTt[r4][pb:pb + 64, :], psS, Dt[r4][pb:pb + 64, :], ALU.mult,
                reads=[kS, K(f"Dt{r4}")], writes=[K(f"pTt{r4}")])

        def stage2(it, c):
            pb, j, cols, r4, psU, psN, psS, kU, kN, kS = slots(it, c)
            a, b = it % 4, (it + 1) % 4
            P.actf(Call[:, c, :], Crun[a][:, :], AF.Identity, reads=[K(f"Crun{a}")], writes=[K(f"Call{c % 16}")])
            P.v("dve", "scalar_tensor_tensor", Crun[b][:, :], Crun[a][:, :], dec[:, c:c + 1], psU, ALU.mult, ALU.add,
                reads=[K(f"Crun{a}"), K("dec"), kU], writes=[K(f"Crun{b}")])

        def stage3(it, c):
            pb, j, cols, r4, psU, psN, psS, kU, kN, kS = slots(it, c)
            P.mm(psN, pTt[r4][pb:pb + 64, :], vb[pb:pb + 64, j, :], start=True, stop=False,
                 reads=[K(f"pTt{r4}"), vbk], writes=[kN])
            P.mm(psN, qTb[:, cols], Call[:, c, :], start=False, stop=True,
                 reads=[K("qTb"), K(f"Call{c % 16}")], writes=[kN])
            if accum and d == 1:
                P.v("dve", "tensor_tensor", NUM[pb:pb + 64, j, :], NUM[pb:pb + 64, j, :], psN, ALU.add, reads=[kN, kNUM], writes=[kNUM])
            else:
                P.evac(NUM[pb:pb + 64, j, :], psN, reads=[kN], writes=[kNUM])

        LOOK = 2
        for it in range(min(LOOK, len(order))):
            stage1(it, order[it])
        for it, c in enumerate(order):
            if it + LOOK < len(order):
                stage1(it + LOOK, order[it + LOOK])
            stage2(it, c)
            stage3(it, c)
        finish_dir(d, NUM, kNUM)
        P.release(mdir)


def _softplus(P, ap, bias_ap, key, deps=()):
    P.actf(ap, ap, AF.Exp, bias=bias_ap, reads=[key] + list(deps), writes=[key])
    P.actf(ap, ap, AF.Ln, bias=1.0, reads=[key], writes=[key])


def b_mlstm(P, E, l, C):
    m0 = P.mark()
    qT = P.al([128, TALL], BF16)
    kT = P.al([128, TALL], BF16)
    ktok = P.al([128, NJ, 128], BF16)
    vb = P.al([128, NJ, 65], BF16)
    gtok = P.al([128, NJ, 4], F32)
    gb = P.al([128, 4], F32)
    ngb = P.al([128, 4], F32)
    H = P.al([128, NJ, 64], F32)
    rd = P.al([128, NJ], F32)
    m1 = P.mark()
    stg = P.al([128, TALL], F32)
    P.dma(gb[:, :], E["l_gbias"][l].partition_broadcast(128), writes=["l_gb"])
    P.v("dve", "tensor_scalar_mul", ngb[:, :], gb[:, :], -1.0, reads=["l_gb"], writes=["l_ngb"])
    for op, dst, key in (("l_q", qT, "l_qT"), ("l_k", kT, "l_kT")):
        P.v("pool", "memset", dst[64:128, :], 0.0, writes=[key])
        gather_fm(P, E, stg, op, 64, "l_stg")
        P.v("dve", "tensor_copy", dst[0:64, :], stg[0:64, :], reads=["l_stg"], writes=[key])
        if op == "l_k":
            P.v("pool", "memset", ktok[:, :, 64:128], 0.0, writes=["l_ktok"])
            fm_to_tok(P, E, stg, 64, ktok, "l_stg", "l_ktok")
    gather_fm(P, E, stg, "l_v", 64, "l_stg")
    fm_to_tok(P, E, stg, 64, vb, "l_stg", "l_vb")
    P.v("pool", "memset", vb[:, :, 64:65], 1.0, writes=["l_vb"])
    gather_fm(P, E, stg, "l_g", 4, "l_stg")
    fm_to_tok(P, E, stg, 4, gtok, "l_stg", "l_gtok")

    def make_gates(d, lfrow, lftok, igtok, klr, klt, kit):
        gather_fm(P, E, lfrow, f"l_grow{2 * d + 1}", 128, klr)
        P.actf(lfrow[:, :], lfrow[:, :], AF.Exp, bias=ngb[:, 2 * d + 1:2 * d + 2], scale=-1.0, reads=[klr, "l_ngb"], writes=[klr])
        P.actf(lfrow[:, :], lfrow[:, :], AF.Ln, bias=1.0, reads=[klr], writes=[klr])
        P.v("dve", "tensor_scalar_mul", lfrow[:, :], lfrow[:, :], -1.0, reads=[klr], writes=[klr])
        P.actf(lftok[:, :], gtok[:, :, 2 * d + 1], AF.Exp, bias=ngb[:, 2 * d + 1:2 * d + 2], scale=-1.0, reads=["l_gtok", "l_ngb"], writes=[klt])
        P.actf(lftok[:, :], lftok[:, :], AF.Ln, bias=1.0, reads=[klt], writes=[klt])
        P.v("dve", "tensor_scalar_mul", lftok[:, :], lftok[:, :], -1.0, reads=[klt], writes=[klt])
        P.v("dve", "tensor_scalar", igtok[:, :], gtok[:, :, 2 * d], gb[:, 2 * d:2 * d + 1], None, ALU.add, reads=["l_gtok", "l_gb"], writes=[kit])
        return True

    def finish_dir(d, NUM, kn):
        P.actf(rd[:, :], NUM[:, :, 64], AF.Abs, reads=[kn], writes=["l_rd"])
        P.v("dve", "tensor_scalar_max", rd[:, :], rd[:, :], 1.0, reads=["l_rd"], writes=["l_rd"])
        P.v("dve", "reciprocal", rd[:, :], rd[:, :], reads=["l_rd"], writes=["l_rd"])
        rb = rd[:, :].unsqueeze(2).to_broadcast([128, NJ, 64])
        if d == 0:
            P.v("dve", "tensor_tensor", H[:, :, :], NUM[:, :, 0:64], rb, ALU.mult, reads=[kn, "l_rd"], writes=["l_H"])
        else:
            P.v("dve", "tensor_tensor", NUM[:, :, 0:64], NUM[:, :, 0:64], rb, ALU.mult, reads=[kn, "l_rd"], writes=[kn])
            P.v("dve", "tensor_tensor", H[:, :, :], H[:, :, :], NUM[:, :, 0:64], ALU.add, reads=[kn, "l_H"], writes=["l_H"])

    P.release(m1)
    NUM = P.al([128, NJ, 65], F32)
    dla(P, C, 65, qT, kT, ktok, lambda d: (vb, "l_vb"), make_gates, finish_dir, "l", NUM, "l_NUM")
    P.release(m1)
    sqh = P.al([128, NJ, 64], F32)
    ss = P.al([128, NJ], F32)
    otok = P.al([128, NJ, 64], F32)
    gn = P.al([128, 64], F32)
    ostg = P.al([64, TALL], F32)
    P.dma(gn[:, :], E["l_norm"][l].partition_broadcast(128), writes=["l_gn"])
    gather_fm(P, E, ostg, "l_o", 64, "l_ostg")
    fm_to_tok(P, E, ostg, 64, otok, "l_ostg", "l_otok")
    P.v("dve", "tensor_tensor", sqh[:, :, :], H[:, :, :], H[:, :, :], ALU.mult, reads=["l_H"], writes=["l_sqh"])
    P.v("dve", "reduce_sum", ss[:, :], sqh[:, :, :], AX.X, reads=["l_sqh"], writes=["l_ss"])
    P.v("dve", "tensor_scalar", ss[:, :], ss[:, :], 1.0 / 64, EPS, ALU.mult, ALU.add, reads=["l_ss"], writes=["l_ss"])
    P.actf(ss[:, :], ss[:, :], AF.Sqrt, reads=["l_ss"], writes=["l_ss"])
    P.v("dve", "reciprocal", ss[:, :], ss[:, :], reads=["l_ss"], writes=["l_ss"])
    P.v("dve", "tensor_tensor", H[:, :, :], H[:, :, :], ss[:, :].unsqueeze(2).to_broadcast([128, NJ, 64]), ALU.mult,
        reads=["l_H", "l_ss"], writes=["l_H"])
    P.v("dve", "tensor_tensor", H[:, :, :], H[:, :, :], gn[:, :].unsqueeze(1).to_broadcast([128, NJ, 64]), ALU.mult,
        reads=["l_H", "l_gn"], writes=["l_H"])
    P.actf(otok[:, :, :], otok[:, :, :], AF.Sigmoid, reads=["l_otok"], writes=["l_otok"])
    P.v("dve", "tensor_tensor", H[:, :, :], H[:, :, :], otok[:, :, :], ALU.mult, reads=["l_H", "l_otok"], writes=["l_H"])
    tok_to_fm(P, E, H, ostg, "l_H", "l_ostg")
    store_cols(P, E, 0, 64, (lambda a, b: ostg[0:64, a:b]), 0, TALL, "l_ostg")
    P.release(m0)


def b_ssd(P, E, l, C):
    B = C["banks"]
    ident = C["ident"]
    m0 = P.mark()
    qT = P.al([128, TALL], BF16)
    kT = P.al([128, TALL], BF16)
    ktok = P.al([128, NJ, 128], BF16)
    vtok = P.al([128, NJ, 64], F32)
    vbd = P.al([128, NJ, 64], BF16)
    dttok = P.al([128, NJ, 2], F32)
    Y = P.al([128, NJ, 64], F32)
    cw = P.al([128, 3, 4], F32)
    dtb = P.al([128, 2], F32)
    Ad = P.al([128, 2], F32)
    dsk = P.al([128, 1], F32)
    P.dma(cw[:, :, :], E["s_convw"][l], writes=["s_cw"])
    P.dma(dtb[:, :], E["s_dtbias"][l].partition_broadcast(128), writes=["s_dtb"])
    P.dma(Ad[:, :], E["s_alog"][l].partition_broadcast(128), writes=["s_Ad"])
    P.actf(Ad[:, :], Ad[:, :], AF.Exp, reads=["s_Ad"], writes=["s_Ad"])
    P.v("dve", "tensor_scalar_mul", Ad[:, :], Ad[:, :], -1.0, reads=["s_Ad"], writes=["s_Ad"])
    P.dma(dsk[:, :], E["s_dskip"][l].partition_broadcast(128), writes=["s_dsk"])
    m1 = P.mark()
    raw = P.al([128, TALL], F32)
    acc = P.al([128, TALL], F32)
    gather_fm(P, E, raw, "s_dt", 2, "s_raw")
    fm_to_tok(P, E, raw, 2, dttok, "s_raw", "s_dttok")
    for d in range(2):
        _softplus(P, dttok[:, :, d], dtb[:, d:d + 1], "s_dttok", deps=["s_dtb"])
    for blk, (r0, nr) in enumerate(((0, 64), (64, 128), (192, 128))):
        gather_fm(P, E, raw, ("s_x", "s_B", "s_C")[blk], nr, "s_raw")
        P.actf(acc[0:nr, :], raw[0:nr, :], AF.Identity, bias=cw[0:nr, blk, 3:4], scale=cw[0:nr, blk, 1:2],
               reads=["s_raw", "s_cw"], writes=["s_acc"])
        for (a, b) in ((0, CTX), (CTX, TALL)):
            P.v("dve", "scalar_tensor_tensor", acc[0:nr, a + 1:b], raw[0:nr, a:b - 1], cw[0:nr, blk, 0:1], acc[0:nr, a + 1:b],
                ALU.mult, ALU.add, reads=["s_raw", "s_cw", "s_acc"], writes=["s_acc"])
            P.v("dve", "scalar_tensor_tensor", acc[0:nr, a:b - 1], raw[0:nr, a + 1:b], cw[0:nr, blk, 2:3], acc[0:nr, a:b - 1],
                ALU.mult, ALU.add, reads=["s_raw", "s_cw", "s_acc"], writes=["s_acc"])
        P.actf(acc[0:nr, :], acc[0:nr, :], AF.Silu, reads=["s_acc"], writes=["s_acc"])
        if blk == 2:
            P.v("dve", "tensor_copy", qT[:, :], acc[:, :], reads=["s_acc"], writes=["s_qT"])
            continue
        if blk == 1:
            P.v("dve", "tensor_copy", kT[:, :], acc[:, :], reads=["s_acc"], writes=["s_kT"])
        for g in range(0, NJ, 4):
            ps, pk = B[6 + (g // 4) % 2], f"bank{6 + (g // 4) % 2}"
            nj = min(4, NJ - g)
            for jj in range(nj):
                j = g + jj
                P.tr(ps[:, 128 * jj:128 * jj + nr], acc[0:nr, 128 * j:128 * j + 128], ident[0:nr, 0:nr],
                     reads=["s_acc", "ident"], writes=[pk])
            src = ps[:, 0:128 * nj].rearrange("p (j e) -> p j e", e=128)[:, :, 0:nr]
            if blk == 0:
                P.evac(vtok[:, g:g + nj, :], src, reads=[pk], writes=["s_vtok"])
            else:
                P.evac(ktok[:, g:g + nj, :], src, reads=[pk], writes=["s_ktok"])
    P.release(m1)

    def make_gates(d, lfrow, lftok, igtok, klr, klt, kit):
        gather_fm(P, E, lfrow, f"s_dtrow{d}", 128, klr)
        _softplus(P, lfrow[:, :], dtb[:, d:d + 1], klr, deps=["s_dtb"])
        P.actf(lfrow[:, :], lfrow[:, :], AF.Identity, scale=Ad[:, d:d + 1], reads=[klr, "s_Ad"], writes=[klr])
        P.v("dve", "tensor_scalar", lftok[:, :], dttok[:, :, d], Ad[:, d:d + 1], None, ALU.mult, reads=["s_dttok", "s_Ad"], writes=[klt])
        return False

    def get_vb(d):
        P.v("dve", "tensor_tensor", vbd[:, :, :], vtok[:, :, :], dttok[:, :, d:d + 1].to_broadcast([128, NJ, 64]), ALU.mult,
            reads=["s_vtok", "s_dttok"], writes=["s_vbd"])
        return vbd, "s_vbd"

    def finish_dir(d, NUM, kn):
        pass

    dla(P, C, 64, qT, kT, ktok, get_vb, make_gates, finish_dir, "s", Y, "s_Y", accum=True)
    P.release(m1)
    ztok = P.al([128, NJ, 64], F32)
    zst = P.al([64, TALL], F32)
    gather_fm(P, E, zst, "s_z", 64, "s_zst")
    fm_to_tok(P, E, zst, 64, ztok, "s_zst", "s_ztok")
    P.v("dve", "scalar_tensor_tensor", Y[:, :, :], vtok[:, :, :], dsk[:, 0:1], Y[:, :, :], ALU.mult, ALU.add,
        reads=["s_vtok", "s_dsk", "s_Y"], writes=["s_Y"])
    P.actf(ztok[:, :, :], ztok[:, :, :], AF.Silu, reads=["s_ztok"], writes=["s_ztok"])
    P.v("dve", "tensor_tensor", Y[:, :, :], Y[:, :, :], ztok[:, :, :], ALU.mult, reads=["s_Y", "s_ztok"], writes=["s_Y"])
    tok_to_fm(P, E, Y, zst, "s_Y", "s_zst")
    store_cols(P, E, 128, 64, (lambda a, b: zst[0:64, a:b]), 0, TALL, "s_zst")
    P.release(m0)


MAGIC = 12582912.0
TWO_PI = 6.283185307179586


def _sin_rr(P, out, in_, tmp, key_in, key_out, key_tmp, shift=0.0):
    P.v("dve", "tensor_scalar", tmp, in_, 1.0 / TWO_PI, shift / TWO_PI + MAGIC, ALU.mult, ALU.add, reads=[key_in], writes=[key_tmp])
    P.v("dve", "tensor_scalar", tmp, tmp, MAGIC, -TWO_PI, ALU.subtract, ALU.mult, reads=[key_tmp], writes=[key_tmp])
    P.v("dve", "scalar_tensor_tensor", tmp, in_, 1.0, tmp, ALU.mult, ALU.add, reads=[key_in, key_tmp], writes=[key_tmp])
    P.actf(out, tmp, AF.Sin, bias=shift, reads=[key_tmp], writes=[key_out])


def b_s5(P, E, l, C):
    B = C["banks"]
    ident = C["ident"]
    m0 = P.mark()
    pr = {k: P.al([128, 4], F32) for k in ("are", "aim", "ldt", "dt", "lr", "mag", "th", "sn", "cs", "abr", "abi",
                                            "den", "fre", "fim", "t1", "t2", "t3")}
    bre = P.al([128, 2, 32], F32)
    bim = P.al([128, 2, 32], F32)
    cre = P.al([128, 2, 32], F32)
    cim = P.al([128, 2, 32], F32)
    cTr = P.al([128, 2, 32], BF16)
    cTi = P.al([128, 2, 32], BF16)
    bbT = P.al([32, 8, 128], BF16)
    bbtmp = P.al([128, 2, 32], F32)
    dsk = P.al([32, 2], F32)
    pw = P.al([128, 15, 2], F32)
    npw = P.al([128, 15], F32)
    for k, nm in (("are", "s5_are"), ("aim", "s5_aim"), ("ldt", "s5_ldt")):
        P.dma(pr[k][:, :], E[nm][l], writes=["5p_" + k])
    for t, nm in ((bre, "s5_bre"), (bim, "s5_bim"), (cre, "s5_cre"), (cim, "s5_cim")):
        P.dma(t[:, :, :], E[nm][l], writes=["5p_" + nm])
    P.dma(dsk[:, :], E["s5_d"][l], writes=["5p_dsk"])
    P.v("dve", "tensor_copy", cTr[:, :, :], cre[:, :, :], reads=["5p_s5_cre"], writes=["5p_cT"])
    P.v("dve", "tensor_scalar_mul", cTi[:, :, :], cim[:, :, :], -1.0, reads=["5p_s5_cim"], writes=["5p_cT"])
    kk = "5p_small"
    a = lambda k: pr[k][:, :]
    P.actf(a("dt"), a("ldt"), AF.Exp, reads=["5p_ldt"], writes=[kk])
    P.v("dve", "tensor_scalar_min", a("lr"), a("are"), -1e-4, reads=["5p_are"], writes=[kk])
    P.v("dve", "tensor_tensor", a("t1"), a("lr"), a("dt"), ALU.mult, reads=[kk], writes=[kk])
    P.actf(a("mag"), a("t1"), AF.Exp, reads=[kk], writes=[kk])
    P.v("dve", "tensor_tensor", a("th"), a("aim"), a("dt"), ALU.mult, reads=[kk, "5p_aim"], writes=[kk])
    _sin_rr(P, a("sn"), a("th"), a("t2"), kk, kk, kk)
    P.v("dve", "tensor_scalar_add", a("t3"), a("th"), TWO_PI / 4, reads=[kk], writes=[kk])
    _sin_rr(P, a("cs"), a("t3"), a("t2"), kk, kk, kk)
    P.v("dve", "tensor_tensor", a("abr"), a("mag"), a("cs"), ALU.mult, reads=[kk], writes=[kk])
    P.v("dve", "tensor_tensor", a("abi"), a("mag"), a("sn"), ALU.mult, reads=[kk], writes=[kk])
    P.v("dve", "tensor_tensor", a("den"), a("lr"), a("lr"), ALU.mult, reads=[kk], writes=[kk])
    P.v("dve", "tensor_tensor", a("t1"), a("aim"), a("aim"), ALU.mult, reads=[kk, "5p_aim"], writes=[kk])
    P.v("dve", "tensor_tensor", a("den"), a("den"), a("t1"), ALU.add, reads=[kk], writes=[kk])
    P.v("dve", "reciprocal", a("den"), a("den"), reads=[kk], writes=[kk])
    P.v("dve", "tensor_scalar_add", a("t1"), a("abr"), -1.0, reads=[kk], writes=[kk])
    P.v("dve", "tensor_tensor", a("t2"), a("t1"), a("lr"), ALU.mult, reads=[kk], writes=[kk])
    P.v("dve", "tensor_tensor", a("t3"), a("abi"), a("aim"), ALU.mult, reads=[kk, "5p_aim"], writes=[kk])
    P.v("dve", "tensor_tensor", a("t2"), a("t2"), a("t3"), ALU.add, reads=[kk], writes=[kk])
    P.v("dve", "tensor_tensor", a("fre"), a("t2"), a("den"), ALU.mult, reads=[kk], writes=[kk])
    P.v("dve", "tensor_tensor", a("t2"), a("abi"), a("lr"), ALU.mult, reads=[kk], writes=[kk])
    P.v("dve", "tensor_tensor", a("t3"), a("t1"), a("aim"), ALU.mult, reads=[kk, "5p_aim"], writes=[kk])
    P.v("dve", "tensor_tensor", a("t2"), a("t2"), a("t3"), ALU.subtract, reads=[kk], writes=[kk])
    P.v("dve", "tensor_tensor", a("fim"), a("t2"), a("den"), ALU.mult, reads=[kk], writes=[kk])
    P.v("dve", "tensor_scalar_mul", a("t1"), a("fim"), -1.0, reads=[kk], writes=[kk])
    for gp in range(2):
        for d in range(2):
            ix = gp * 2 + d
            for comp in range(2):
                if comp == 0:
                    P.v("dve", "tensor_scalar", bbtmp[:, 0, :], bre[:, gp, :], pr["fre"][:, ix:ix + 1], None, ALU.mult,
                        reads=[kk, "5p_s5_bre"], writes=["5p_bbtmp"])
                    P.v("dve", "scalar_tensor_tensor", bbtmp[:, 0, :], bim[:, gp, :], pr["t1"][:, ix:ix + 1], bbtmp[:, 0, :],
                        ALU.mult, ALU.add, reads=[kk, "5p_s5_bim", "5p_bbtmp"], writes=["5p_bbtmp"])
                else:
                    P.v("dve", "tensor_scalar", bbtmp[:, 0, :], bim[:, gp, :], pr["fre"][:, ix:ix + 1], None, ALU.mult,
                        reads=[kk, "5p_s5_bim"], writes=["5p_bbtmp"])
                    P.v("dve", "scalar_tensor_tensor", bbtmp[:, 0, :], bre[:, gp, :], pr["fim"][:, ix:ix + 1], bbtmp[:, 0, :],
                        ALU.mult, ALU.add, reads=[kk, "5p_s5_bre", "5p_bbtmp"], writes=["5p_bbtmp"])
                P.tr(B[6][0:32, 0:128], bbtmp[:, 0, :], ident[:, :], reads=["5p_bbtmp", "ident"], writes=["bank6"])
                P.v("dve", "tensor_copy", bbT[:, gp * 4 + d * 2 + comp, :], B[6][0:32, 0:128], reads=["bank6"], writes=["5p_bbT"])
    Er = P.al([128, TALL], F32)
    Ei = P.al([128, TALL], F32)
    DEC = P.al([128, 512], F32)
    ust = P.al([32, TALL], F32)
    ub = P.al([32, TALL], BF16)
    Y = P.al([32, TALL], F32)
    tt = {k: [P.al([128, 512], F32) for _ in range(2)] for k in ("m1", "m2", "m3", "m4", "vr", "vi", "zr", "zi")}
    for k in ("n1", "n2"):
        nbuf = P.al([128, 512], F32)
        tt[k] = [nbuf, nbuf]
    xr = [P.al([128, 512], BF16) for _ in range(2)]
    xi = [P.al([128, 512], BF16) for _ in range(2)]
    for gp in range(2):
        gather_fm(P, E, ust, f"s5_u{gp}", 32, "5_ust")
        P.cast(ub[:, :], ust[:, :], reads=["5_ust"], writes=["5_ub"])
        for d in range(2):
            ix = gp * 2 + d
            rev = d == 1
            P.v("dve", "tensor_copy", pw[:, 0, 0:1], pr["cs"][:, ix:ix + 1], reads=[kk], writes=["5_pw"])
            P.v("dve", "tensor_copy", pw[:, 0, 1:2], pr["sn"][:, ix:ix + 1], reads=[kk], writes=["5_pw"])
            for k in range(14):
                c_, s_ = pw[:, k, 0:1], pw[:, k, 1:2]
                P.v("dve", "tensor_tensor", pr["t2"][:, 0:1], c_, c_, ALU.mult, reads=["5_pw"], writes=["5_t"])
                P.v("dve", "tensor_tensor", pr["t2"][:, 1:2], s_, s_, ALU.mult, reads=["5_pw"], writes=["5_t"])
                P.v("dve", "tensor_tensor", pw[:, k + 1, 0:1], pr["t2"][:, 0:1], pr["t2"][:, 1:2], ALU.subtract, reads=["5_t"], writes=["5_pw"])
                P.v("dve", "scalar_tensor_tensor", pw[:, k + 1, 1:2], c_, 2.0, s_, ALU.mult, ALU.mult, reads=["5_pw"], writes=["5_pw"])
            P.v("dve", "tensor_scalar_mul", npw[:, :], pw[:, :, 1], -1.0, reads=["5_pw"], writes=["5_npw"])
            P.v("pool", "memset", Er[:, 0:1], 1.0, writes=["5_E"])
            P.v("pool", "memset", Ei[:, 0:1], 0.0, writes=["5_E"])
            for k in range(14):
                L = 1 << k
                n = min(L, TALL - L)
                c_, s_, ns_ = pw[:, k, 0:1], pw[:, k, 1:2], npw[:, k:k + 1]
                P.v("dve", "tensor_scalar", Er[:, L:L + n], Er[:, 0:n], c_, None, ALU.mult, reads=["5_E", "5_pw"], writes=["5_E"])
                P.v("dve", "scalar_tensor_tensor", Er[:, L:L + n], Ei[:, 0:n], ns_, Er[:, L:L + n], ALU.mult, ALU.add,
                    reads=["5_E", "5_npw"], writes=["5_E"])
                P.v("dve", "tensor_scalar", Ei[:, L:L + n], Ei[:, 0:n], c_, None, ALU.mult, reads=["5_E", "5_pw"], writes=["5_E"])
                P.v("dve", "scalar_tensor_tensor", Ei[:, L:L + n], Er[:, 0:n], s_, Ei[:, L:L + n], ALU.mult, ALU.add,
                    reads=["5_E", "5_pw"], writes=["5_E"])
            P.v("pool", "memset", DEC[:, :], 1.0, writes=["5_DEC"])
            P.v("dve", "tensor_scalar", DEC[:, :], DEC[:, :], pr["mag"][:, ix:ix + 1], None, ALU.mult, reads=["5_DEC", kk], writes=["5_DEC"])
            lat = _col_tiles(CTX, TALL)
            tiles = [(0, CTX)] + (lat if not rev else lat[::-1])
            prev = [None]

            def views(ti):
                c0, n = tiles[ti]
                r = ti % 2
                if not rev:
                    EV = lambda E_, c0=c0, n=n: E_[:, c0:c0 + n]
                else:
                    seg_hi = (CTX - 1) if c0 < CTX else (TALL - 1 + CTX)
                    lo, hi = seg_hi - (c0 + n - 1), seg_hi - c0 + 1
                    EV = lambda E_, lo=lo, hi=hi: E_[:, lo:hi][:, ::-1]
                T_ = lambda k, r=r, n=n: tt[k][r][:, :n]
                Kk = lambda k, r=r: (f"5_{k}" if k in ("n1", "n2") else f"5_{k}{r}")
                return c0, n, r, slice(c0, c0 + n), EV, T_, Kk, B[r], B[2 + r], f"bank{r}", f"bank{2 + r}"

            def pre(ti):
                c0, n, r, cols, EV, T_, Kk, pR, pI, kR, kI = views(ti)
                P.mm(pR[:, :n], bbT[:, gp * 4 + d * 2 + 0, :], ub[:, cols], reads=["5p_bbT", "5_ub"], writes=[kR])
                P.mm(pI[:, :n], bbT[:, gp * 4 + d * 2 + 1, :], ub[:, cols], reads=["5p_bbT", "5_ub"], writes=[kI])
                P.v("dve", "tensor_tensor", T_("m1"), pR[:, :n], EV(Er), ALU.mult, reads=[kR, "5_E"], writes=[Kk("m1")])
                P.v("dve", "tensor_tensor", T_("m2"), pI[:, :n], EV(Ei), ALU.mult, reads=[kI, "5_E"], writes=[Kk("m2")])
                P.v("pool", "tensor_tensor", T_("vr"), T_("m1"), T_("m2"), ALU.add, reads=[Kk("m1"), Kk("m2")], writes=[Kk("vr")])
                P.v("dve", "tensor_tensor", T_("m3"), pI[:, :n], EV(Er), ALU.mult, reads=[kI, "5_E"], writes=[Kk("m3")])
                P.v("dve", "tensor_tensor", T_("m4"), pR[:, :n], EV(Ei), ALU.mult, reads=[kR, "5_E"], writes=[Kk("m4")])
                P.v("pool", "tensor_tensor", T_("vi"), T_("m3"), T_("m4"), ALU.subtract, reads=[Kk("m3"), Kk("m4")], writes=[Kk("vi")])

            def scanpost(ti):
                c0, n, r, cols, EV, T_, Kk, pR, pI, kR, kI = views(ti)
                for comp, vk in (("zr", "vr"), ("zi", "vi")):
                    o_, d1 = T_(comp), T_(vk)
                    if rev:
                        o_, d1 = o_[:, ::-1], d1[:, ::-1]
                    if prev[0] is None:
                        init, ik = 0.0, []
                    else:
                        pr_r, pn = prev[0]
                        init = tt[comp][pr_r][:, 0:1] if rev else tt[comp][pr_r][:, pn - 1:pn]
                        ik = [f"5_{comp}{pr_r}"]
                    P.v("dve", "tensor_tensor_scan", o_, DEC[:, :n], d1, init, ALU.mult, ALU.add,
                        reads=["5_DEC", Kk(vk)] + ik, writes=[Kk(comp)])
                prev[0] = (r, n)
                P.v("dve", "tensor_tensor", T_("n1"), T_("zr"), EV(Er), ALU.mult, reads=[Kk("zr"), "5_E"], writes=[Kk("n1")])
                P.v("pool", "tensor_tensor", T_("n2"), T_("zi"), EV(Ei), ALU.mult, reads=[Kk("zi"), "5_E"], writes=[Kk("n2")])
                P.v("dve", "tensor_tensor", xr[r][:, :n], T_("n1"), T_("n2"), ALU.subtract, reads=[Kk("n1"), Kk("n2")], writes=[f"5_xr{r}"])
                P.v("dve", "tensor_tensor", T_("n1"), T_("zr"), EV(Ei), ALU.mult, reads=[Kk("zr"), "5_E", f"5_xr{r}"], writes=[Kk("n1")])
                P.v("dve", "tensor_tensor", T_("n2"), T_("zi"), EV(Er), ALU.mult, reads=[Kk("zi"), "5_E", f"5_xr{r}"], writes=[Kk("n2")])
                P.v("dve", "tensor_tensor", xi[r][:, :n], T_("n1"), T_("n2"), ALU.add, reads=[Kk("n1"), Kk("n2")], writes=[f"5_xi{r}"])
                pY, kY = B[4 + r], f"bank{4 + r}"
                P.mm(pY[0:32, :n], cTr[:, gp, :], xr[r][:, :n], start=True, stop=False, reads=["5p_cT", f"5_xr{r}"], writes=[kY])
                P.mm(pY[0:32, :n], cTi[:, gp, :], xi[r][:, :n], start=False, stop=True, reads=["5p_cT", f"5_xi{r}"], writes=[kY])
                if d == 0:
                    P.v("dve", "scalar_tensor_tensor", Y[:, cols], ust[:, cols], dsk[:, gp:gp + 1], pY[0:32, :n], ALU.mult, ALU.add,
                        reads=["5_ust", "5p_dsk", kY], writes=["5_Y"])
                else:
                    P.v("dve", "tensor_tensor", Y[:, cols], Y[:, cols], pY[0:32, :n], ALU.add, reads=["5_Y", kY], writes=["5_Y"])

            pre(0)
            for ti in range(len(tiles)):
                if ti + 1 < len(tiles):
                    pre(ti + 1)
                scanpost(ti)
        store_cols(P, E, 192 + 32 * gp, 32, (lambda a, b: Y[0:32, a:b]), 0, TALL, "5_Y")
    P.release(m0)


def phase_C(P, E, l, final=False):
    banks = E["banks"]
    ones_b, cs = E["ones_b"], E["cs"]
    wmod, bmod, norm2 = E["wmodC"][l], E["bmodC"][l], E["norm2"][l]
    snorm, wglu, wout, wup, convw, wdown = E["snorm"][l], E["wglu"][l], E["wout"][l], E["wup"][l], E["convw"][l], E["wdown"][l]
    XS, XSo, GXB, xTo = E["XS"][l % 2], E["XS"][(l + 1) % 2], E["GXB"], E["xTo"]
    itC, itX = E["idxC"], E["idxX"]
    m_top0 = P.mark()
    bmod_sb = P.al([128, 32], F32)
    P.dma(bmod_sb[:], bmod, writes=["bmod"])
    n2 = P.al([128, KC], F32)
    P.dma(n2[:], norm2, writes=["n2"])
    fn = P.al([128, KC], F32)
    P.dma(fn[:], E["fnorm"], writes=["fn"])
    sn = P.al([128, 2], F32)
    P.dma(sn[:], snorm, writes=["sn"])
    hm = P.al([128, 4], F32)
    P.dma(hm[:], E["hmask"], writes=["hm"])
    cwf = P.al([128, NFF, 3], F32)
    P.dma(cwf[:], convw, writes=["cwf"])
    modv = P.al([128, 4, KC, 2], F32)
    A2 = P.al([128, KC, 2], F32)
    nb = P.al([128, KC, 2, 4], F32)
    P.v("pool", "memset", nb[:], 0.0, writes=["nb"])
    for k in range(KC):
        for side in range(2):
            c = k * 2 + side
            P.add("pool", (lambda k, side, c: (lambda e: e.indirect_dma_start(
                out=nb[:, k, side, :], out_offset=None, in_=GXB[:, :],
                in_offset=bass.IndirectOffsetOnAxis(ap=itX[:, c:c + 1], axis=0))))(k, side, c), reads=["GXB", "idxX", "nb"], writes=["nb"], dma=True)
    m_top = P.mark()
    wst = [P.al([128, 1024], F32) for _ in range(2)]
    _mod_vectors(P, wmod, bmod_sb, cs, [0, 1, 2, 3], wst, ["wst0", "wst1"], banks[7], "bank7", modv)
    P.v("dve", "tensor_scalar_add", A2[:], modv[:, 2, :, :], 1.0, reads=["modv"], writes=["A2"])
    for s_ in range(2):
        P.v("dve", "tensor_tensor", A2[:, :, s_], A2[:, :, s_], n2[:], ALU.mult, reads=["A2", "n2"], writes=["A2"])
    P.release(m_top)

    g0 = [(0, 0, 1026, 1, 1025, (0, None))]
    g1 = [(0, 1024, 2050, 1025, 2049, (None, 1))]
    if not final:
        g1.append((1, CX0, CX1, CX0 + 1, CX1 - 1, (2, 3)))
    evi = [0]
    for gi, segs in enumerate((g0, g1)):
        m_g = P.mark()
        W = sum(e1 - e0 for (_, e0, e1, _, _, _) in segs)
        WO = sum(o1 - o0 for (_, _, _, o0, o1, _) in segs)
        xres = P.al([128, KC, W], F32)
        h2 = P.al([128, KC, W], BF16)
        aT = P.al([128, NFF, WO], BF16)
        loc = []
        off = 0
        ooff = 0
        for (kind, e0, e1, o0, o1, hf) in segs:
            loc.append((off, ooff))
            off += e1 - e0
            ooff += o1 - o0
        kx = f"xres{gi}"
        m_s1 = P.mark()
        wout_b = P.al([128, KC, D], BF16)
        wglu_b = P.al([128, 2, 256], BF16)
        wstg = [P.al([128, D], F32) for _ in range(2)]
        for k in range(KC):
            P.dma(wstg[k % 2][:, :], wout[k * 128:(k + 1) * 128, :], writes=[f"wstg{k % 2}"])
            P.cast(wout_b[:, k, :], wstg[k % 2][:, :], reads=[f"wstg{k % 2}"], writes=["woutb"])
        for k in range(2):
            P.dma(wstg[k][:, 0:256], wglu[k * 128:(k + 1) * 128, :], writes=[f"wstg{k}"])
            P.v("pool", "tensor_copy", wglu_b[:, k, :], wstg[k][:, 0:256], reads=[f"wstg{k}"], writes=["wglub"])
        yst = [P.al([128, KC, 512], F32) for _ in range(2)]
        ycb = P.al([128, KC, 512], BF16)
        sqb = P.al([128, KC, 512], BF16)
        gel = P.al([128, 2, 512], F32)
        gelb = P.al([128, 2, 512], BF16)
        sig = P.al([128, 512], F32)
        rstd = P.al([128, 512], F32)
        tmpn = P.al([128, 512], F32)
        tmpx = [P.al([128, 512], F32) for _ in range(2)]
        ti = 0
        for si, (kind, e0, e1, o0, o1, hf) in enumerate(segs):
            lo = loc[si][0]
            halo = {LAT0: (0, 1), LAT1 - 1: (1, 0), CX0: (0, 3), CX1 - 1: (1, 2)}
            oa = e0 + 1 if e0 in halo else e0
            ob = e1 - 1 if (e1 - 1) in halo else e1
            xs0 = (oa - 1) if kind == 0 else (NLAT + oa - CX0 - 1)
            P.dma(xres[:, :, lo + (oa - e0):lo + (ob - e0)], XS[:, xs0:xs0 + (ob - oa)].rearrange("(k p) n -> p k n", p=128),
                  reads=[f"XS{l % 2}"], writes=[kx])
            for ecol in (e0, e1 - 1):
                if ecol in halo:
                    side, bc = halo[ecol]
                    P.v("dve", "tensor_copy", xres[:, :, lo + (ecol - e0)], nb[:, :, side, bc], reads=["nb", kx], writes=[kx])
            for (c0, n) in _col_tiles(e0, e1):
                l0 = lo + (c0 - e0)
                ys, yk = yst[ti % 2], f"yst{ti % 2}"
                ti += 1
                pi = Y_PIECES.index((c0, n))
                srcp = _flat(E["GEXY"]).rearrange("(r w) -> r w", w=n)
                for k in range(KC):
                    cix = pi * KC + k
                    P.add("pool", (lambda k, ys, n, srcp, cix, eoff: (lambda e: e.indirect_dma_start(
                        out=ys[:, k, :n], out_offset=None, in_=srcp,
                        in_offset=bass.IndirectOffsetOnAxis(ap=itC[:, cix:cix + 1], axis=0), element_offset=eoff)))(k, ys, n, srcp, cix, 0), reads=["GEXY", "idxC"], writes=[yk], dma=True, nowaw=True)
                P.cast(ycb[:, 0:4, :n], ys[:, 0:4, :n], reads=[yk], writes=["ycb"])
                P.actf(sqb[:, 0:2, :n], ys[:, 4:6, :n], AF.Square, reads=[yk], writes=["sqb"])
                pss = banks[7]
                for j in range(2):
                    P.mm(pss[:, :n], ones_b[:], sqb[:, j, :n], start=(j == 0), stop=(j == 1), reads=["ones", "sqb"], writes=["bank7"])
                _rms_rstd(P, pss, n, 256, rstd, "bank7", "rstd", tmpn, "tmpn")
                for j in range(2):
                    P.v("dve", "tensor_tensor", tmpx[j][:, :n], ys[:, 4 + j, :n], rstd[:, :n], ALU.mult, reads=[yk, "rstd"], writes=[f"tmpx{j}"])
                    P.actf(ycb[:, 4 + j, :n], tmpx[j][:, :n], AF.Identity, scale=sn[:, j:j + 1], reads=[f"tmpx{j}", "sn"], writes=["ycb"])
                P.actf(gel[:, :, :n], ys[:, 6:8, :n], AF.Gelu, reads=[yk], writes=["gel"])
                P.v("dve", "tensor_copy", gelb[:, :, :n], gel[:, :, :n], reads=["gel"], writes=["gelb"])
                for mc in range(2):
                    pg_, pgk = banks[6], "bank6"
                    for j in range(2):
                        P.mm(pg_[:, :n], wglu_b[:, j, mc * 128:(mc + 1) * 128], gelb[:, j, :n], start=(j == 0), stop=(j == 1),
                             reads=["wglub", "gelb"], writes=[pgk])
                    P.actf(sig[:, :n], pg_[:, :n], AF.Sigmoid, reads=[pgk], writes=["sig"])
                    P.v("dve", "tensor_tensor", ycb[:, 6 + mc, :n], gel[:, mc, :n], sig[:, :n], ALU.mult, reads=["gel", "sig"], writes=["ycb"])
                for dc in range(KC):
                    po, pok = banks[dc % 4], f"bank{dc % 4}"
                    for k in range(KC):
                        P.mm(po[:, :n], wout_b[:, k, dc * 128:(dc + 1) * 128], ycb[:, k, :n], start=(k == 0), stop=(k == KC - 1),
                             reads=["woutb", "ycb"], writes=[pok])
                    P.v("dve", "scalar_tensor_tensor", xres[:, dc, l0:l0 + n], po[:, :n], modv[:, 0, dc, kind:kind + 1],
                        xres[:, dc, l0:l0 + n], ALU.mult, ALU.add, reads=[pok, "modv", kx], writes=[kx])
                P.actf(sqb[:, :, :n], xres[:, :, l0:l0 + n], AF.Square, reads=[kx], writes=["sqb"])
                for k in range(KC):
                    P.mm(pss[:, :n], ones_b[:], sqb[:, k, :n], start=(k == 0), stop=(k == KC - 1), reads=["ones", "sqb"], writes=["bank7"])
                _rms_rstd(P, pss, n, D, rstd, "bank7", "rstd", tmpn, "tmpn")
                for k in range(KC):
                    P.v("dve", "tensor_tensor", tmpx[k % 2][:, :n], xres[:, k, l0:l0 + n], rstd[:, :n], ALU.mult,
                        reads=[kx, "rstd"], writes=[f"tmpx{k % 2}"])
                    P.actf(h2[:, k, l0:l0 + n], tmpx[k % 2][:, :n], AF.Identity, bias=modv[:, 1, k, kind:kind + 1],
                           scale=A2[:, k, kind:kind + 1], reads=[f"tmpx{k % 2}", "modv", "A2"], writes=["h2"])
            for hidx, col in ((hf[0], lo), (hf[1], lo + (e1 - e0) - 1)):
                if hidx is not None:
                    P.v("dve", "tensor_scalar", h2[:, :, col], h2[:, :, col], hm[:, hidx:hidx + 1], None, ALU.mult,
                        reads=["h2", "hm"], writes=["h2"])
        P.release(m_s1)
        wus = [P.al([128, KC, 256], F32) for _ in range(2)]
        wub = [P.al([128, KC, 256], BF16) for _ in range(2)]
        wds = [P.al([128, NFF, 128], F32) for _ in range(2)]
        wdb = [P.al([128, NFF, 128], BF16) for _ in range(2)]
        tcv = [P.al([128, 512], F32) for _ in range(2)]
        tsl = [P.al([128, 512], F32) for _ in range(2)]
        ftiles = []
        for si, (kind, e0, e1, o0, o1, hf) in enumerate(segs):
            lo, oo = loc[si]
            for (c0, n) in _col_tiles(o0, o1, 410):
                ftiles.append((kind, lo + (c0 - e0), oo + (c0 - o0), n))
        it = 0
        for f in range(NFF):
            r = f % 2
            P.dma(wus[r][:, :, 0:128], wup[:, f * 128:(f + 1) * 128].rearrange("(k p) c -> p k c", p=128), writes=[f"wus{r}"])
            P.dma(wus[r][:, :, 128:256], wup[:, DFF + f * 128:DFF + (f + 1) * 128].rearrange("(k p) c -> p k c", p=128), writes=[f"wus{r}"])
            P.cast(wub[r][:, :, :], wus[r][:, :, :], reads=[f"wus{r}"], writes=[f"wub{r}"])
            for (kind, lc, oc, n) in ftiles:
                q = it % 2
                it += 1
                pu, puk = banks[q], f"bank{q}"
                pg_, pgk = banks[2 + q], f"bank{2 + q}"
                for k in range(KC):
                    P.mm(pu[:, :n], wub[r][:, k, 0:128], h2[:, k, lc:lc + n], start=(k == 0), stop=(k == KC - 1),
                         reads=[f"wub{r}", "h2"], writes=[puk])
                for k in range(KC):
                    P.mm(pg_[:, :n + 2], wub[r][:, k, 128:256], h2[:, k, lc - 1:lc + n + 1], start=(k == 0), stop=(k == KC - 1),
                         reads=[f"wub{r}", "h2"], writes=[pgk])
                tc_, tck = tcv[q], f"tcv{q}"
                ts_, tsk = tsl[q], f"tsl{q}"
                P.actf(tc_[:, :n], pg_[:, 1:n + 1], AF.Identity, scale=cwf[:, f, 1:2], reads=[pgk, "cwf"], writes=[tck])
                P.v("dve", "scalar_tensor_tensor", tc_[:, :n], pg_[:, 0:n], cwf[:, f, 0:1], tc_[:, :n], ALU.mult, ALU.add,
                    reads=[pgk, "cwf", tck], writes=[tck])
                P.v("dve", "scalar_tensor_tensor", tc_[:, :n], pg_[:, 2:n + 2], cwf[:, f, 2:3], tc_[:, :n], ALU.mult, ALU.add,
                    reads=[pgk, "cwf", tck], writes=[tck])
                P.actf(ts_[:, :n], tc_[:, :n], AF.Silu, reads=[tck], writes=[tsk])
                P.v("dve", "tensor_tensor", aT[:, f, oc:oc + n], ts_[:, :n], pu[:, :n], ALU.mult, reads=[tsk, puk], writes=[f"aT{gi}"])
        for dc in range(KC):
            r = dc % 2
            P.dma(wds[r][:, :, :], wdown[:, dc * 128:(dc + 1) * 128].rearrange("(f p) c -> p f c", p=128), writes=[f"wds{r}"])
            P.cast(wdb[r][:, :, :], wds[r][:, :, :], reads=[f"wds{r}"], writes=[f"wdb{r}"])
            for (kind, lc, oc, n) in ftiles:
                q = it % 2
                it += 1
                pd, pdk = banks[4 + q], f"bank{4 + q}"
                for f in range(NFF):
                    P.mm(pd[:, :n], wdb[r][:, f, :], aT[:, f, oc:oc + n], start=(f == 0), stop=(f == NFF - 1),
                         reads=[f"wdb{r}", f"aT{gi}"], writes=[pdk])
                P.v("dve", "scalar_tensor_tensor", xres[:, dc, lc:lc + n], pd[:, :n], modv[:, 3, dc, kind:kind + 1],
                    xres[:, dc, lc:lc + n], ALU.mult, ALU.add, reads=[pdk, "modv", kx], writes=[kx])
        if final:
            sqf = P.al([128, KC, 512], BF16)
            rstd = P.al([128, 512], F32)
            tmpn = P.al([128, 512], F32)
            tmpx = [P.al([128, 512], F32) for _ in range(2)]
            ost = [P.al([128, 512], F32) for _ in range(2)]
            (kind, e0, e1, o0, o1, hf) = segs[0]
            lo = loc[0][0]
            oi = 0
            for (c0, n) in _col_tiles(o0, o1):
                l0 = lo + (c0 - e0)
                P.actf(sqf[:, :, :n], xres[:, :, l0:l0 + n], AF.Square, reads=[kx], writes=["sqf"])
                for k in range(KC):
                    P.mm(banks[7][:, :n], ones_b[:], sqf[:, k, :n], start=(k == 0), stop=(k == KC - 1), reads=["ones", "sqf"], writes=["bank7"])
                _rms_rstd(P, banks[7], n, D, rstd, "bank7", "rstdf", tmpn, "tmpnf")
                for k in range(KC):
                    P.v("dve", "tensor_tensor", tmpx[k % 2][:, :n], xres[:, k, l0:l0 + n], rstd[:, :n], ALU.mult,
                        reads=[kx, "rstdf"], writes=[f"tmpxf{k % 2}"])
                    o_, ok_ = ost[oi % 2], f"ostf{oi % 2}"
                    oi += 1
                    P.actf(o_[:, :n], tmpx[k % 2][:, :n], AF.Identity, scale=fn[:, k:k + 1], reads=[f"tmpxf{k % 2}", "fn"], writes=[ok_])
                    P.dma(xTo[k * 128:(k + 1) * 128, c0 - 1:c0 - 1 + n], o_[:, :n], reads=[ok_], writes=["xTo"])
        else:
            for si, (kind, e0, e1, o0, o1, hf) in enumerate(segs):
                lo = loc[si][0]
                l0 = lo + (o0 - e0)
                dst0 = (o0 - 1) if kind == 0 else (NLAT + (o0 - CX0 - 1))
                P.dma(XSo[:, dst0:dst0 + (o1 - o0)].rearrange("(k p) n -> p k n", p=128), xres[:, :, l0:l0 + (o1 - o0)],
                      reads=[kx], writes=[f"XS{(l + 1) % 2}"])
                for (ecol, bc) in ((1, 0), (NLAT, 1), (CX0 + 1, 2), (CX1 - 2, 3)):
                    if o0 <= ecol < o1 and ((kind == 0) == (ecol < CX0)):
                        P.dma(E["XBND"][:, bc:bc + 1].rearrange("(k p) o -> p k o", p=128), xres[:, :, lo + (ecol - e0):lo + (ecol - e0) + 1],
                              reads=[kx], writes=["XBND"], allow_slow_non_contiguous=True)
        P.release(m_g)
    P.release(m_top0)


LAYER_W = [
    ("wmodA", [D, 2 * D]), ("bmodA", [128, 16]), ("wmodC", [D, 4 * D]), ("bmodC", [128, 32]),
    ("norm1", [128, KC]), ("norm2", [128, KC]), ("win", [D, P_IN]), ("winsw", [D, 32]), ("qn", [128, 2]), ("kvn", [128, 1]),
    ("wuq", [256, 384]), ("wuqsw", [256, 384]), ("wukv", [128, 512]), ("snorm", [128, 2]), ("wglu", [256, 256]),
    ("wout", [D, D]), ("wup", [D, 2 * DFF]), ("convw", [128, NFF, 3]), ("wdown", [DFF, D]),
    ("l_gbias", [1, 4]), ("l_norm", [1, 64]), ("s_convw", [128, 3, 4]), ("s_dtbias", [1, 2]), ("s_alog", [1, 2]), ("s_dskip", [1, 1]),
    ("s5_are", [128, 4]), ("s5_aim", [128, 4]), ("s5_ldt", [128, 4]), ("s5_bre", [128, 2, 32]), ("s5_bim", [128, 2, 32]),
    ("s5_cre", [128, 2, 32]), ("s5_cim", [128, 2, 32]), ("s5_d", [32, 2]),
]
GLOBAL_IN = [("x0", [D, NT], F32), ("cvec", [128, KC, 2], F32), ("ropeC", [32, NT], F32), ("ropeS", [32, NT], F32),
             ("fnorm", [128, KC], F32), ("hmask", [128, 4], F32), ("idxB", [128, NIB], I32), ("idxC", [128, 7 * KC], I32),
             ("idxX", [128, 2 * KC], I32)]
RG = [[0, 1, 2, 3], [4, 5, 6, 7]]


def _allgather(P, src, dst, ksrc, kdst, rows=None, chunks=None, chunk_keys=False):
    R = src.shape[0]
    rows = rows or R
    assert R % rows == 0
    for c in (range(R // rows) if chunks is None else chunks):
        a, b = src[c * rows:(c + 1) * rows, :], dst[4 * c * rows:4 * (c + 1) * rows, :]
        P.add("pool", (lambda a, b: (lambda e: e.collective_compute("AllGather", ALU.bypass, replica_groups=RG,
                                                                      ins=[a.opt()], outs=[b.opt()])))(a, b),
              reads=[ksrc], writes=[(f"{kdst}{c}" if chunk_keys else kdst)], cc=True)


def build_fused(NL=4):
    nc = bass.Bass("TRN2", target_bir_lowering=False)
    E = {}
    for name, shape, dt_ in GLOBAL_IN:
        E[name + "_d"] = _dram(nc, name, shape, dtype=dt_)
    for name, shape in LAYER_W:
        E[name] = _dram(nc, name, [NL] + shape)
    E["xTo"] = _dram(nc, "xTo", [D, NLAT], kind="ExternalOutput")
    for name, shape in (("EXA", [RA_PAD // 512, 512]), ("GEXA", [4 * RA_PAD // 512, 512]), ("EXY", [RY_PAD // 512, 512]),
                        ("GEXY", [4 * RY_PAD // 512, 512]), ("XS0", [D, NT]), ("XS1", [D, NT]), ("XBND", [D, 4]), ("GXB", [4 * D, 4])):
        E[name] = nc.dram_tensor(name, shape, F32).ap()
    E["ropeC"], E["ropeS"], E["fnorm"], E["hmask"] = E["ropeC_d"], E["ropeS_d"], E["fnorm_d"], E["hmask_d"]
    P = Prog(nc)
    C = b_consts(P)
    E["banks"], E["ident"], E["ones_b"] = C["banks"], C["ident"], C["ones_b"]
    cs = P.sb([128, KC, 2], F32, "cs")
    P.dma(cs[:], E["cvec_d"], writes=["cs"])
    P.actf(cs[:], cs[:], AF.Silu, reads=["cs"], writes=["cs"])
    E["cs"] = cs
    for nm, w in (("idxB", NIB), ("idxC", 7 * KC), ("idxX", 2 * KC)):
        t = P.sb([128, w], I32, nm)
        P.dma(t[:], E[nm + "_d"], writes=[nm])
        E[nm] = t
    P.arena_init(196 * 1024)
    mz = P.mark()
    zt = P.al([128, 8192], F32)
    P.v("pool", "memset", zt[:], 0.0, writes=["zeros"])
    nrow = RY_PAD // 512
    for j0 in range(0, nrow, 128 * 16):
        nr_ = min(128 * 16, nrow - j0)
        if nr_ % 128 == 0:
            P.dma(E["EXY"][j0:j0 + nr_, :].rearrange("(p a) w -> p (a w)", p=128), zt[:, 0:(nr_ // 128) * 512], reads=["zeros"], writes=["EXY"])
        else:
            for j1 in range(j0, j0 + nr_, 128):
                n1 = min(128, j0 + nr_ - j1)
                P.dma(E["EXY"][j1:j1 + n1, :], zt[0:n1, 0:512], reads=["zeros"], writes=["EXY"])
    P.release(mz)
    E["XS"] = [E["XS0"], E["XS1"]]
    P.dma(E["XS"][0], E["x0_d"], writes=["XS0"])
    for bc, col in ((0, 0), (1, NLAT - 1), (2, NLAT), (3, NT - 1)):
        P.dma(E["XBND"][:, bc:bc + 1], E["x0_d"][:, col:col + 1], writes=["XBND"], allow_slow_non_contiguous=True)
    for l in range(NL):
        phase_A(P, E, l)
        first = sorted(set(_op_chunks("m_QT") + _op_chunks("m_KT") + _op_chunks("m_V")))
        rest = [c for c in range(NCHA) if c not in first]
        _allgather(P, E["EXA"], E["GEXA"], "PT", "GEXA", rows=CHA // 512, chunks=first, chunk_keys=True)
        b_mla(P, E, l, C, after_first_gather=lambda: _allgather(P, E["EXA"], E["GEXA"], "PT", "GEXA", rows=CHA // 512,
                                                                chunks=rest, chunk_keys=True))
        b_mlstm(P, E, l, C)
        b_ssd(P, E, l, C)
        b_s5(P, E, l, C)
        _allgather(P, E["EXY"], E["GEXY"], "EXY", "GEXY", rows=CHY // 512)
        _allgather(P, E["XBND"], E["GXB"], "XBND", "GXB")
        phase_C(P, E, l, final=(l == NL - 1))
    P.emit()
    return nc


def _rope_tables():
    rows = SEQ // 64
    row = np.broadcast_to(np.arange(rows)[:, None], (rows, 64)).reshape(-1).astype(np.float32)
    col = np.broadcast_to(np.arange(64)[None, :], (rows, 64)).reshape(-1).astype(np.float32)
    inv = (10000.0 ** (-np.arange(8, dtype=np.float32) / 8)).astype(np.float32)
    ang = np.concatenate([row[:, None] * inv, col[:, None] * inv], axis=-1)
    return np.cos(ang).astype(np.float32), np.sin(ang).astype(np.float32)


def _c32(a):
    return np.ascontiguousarray(a, dtype=np.float32)


def _index_tables(h, q):
    OOB = -1
    g = h // 2
    SS = R_SS
    p = np.arange(128)
    rows = {
        "m_QT": np.where(p < 96, R_Q + 96 * h + p, OOB),
        "m_KT": np.where(p < 64, R_KN + 64 * h + p, np.where(p < 96, R_KR + (p - 64), OOB)),
        "m_V": np.where(p < 64, R_V + 64 * h + p, OOB),
        "l_q": np.where(p < 64, 0 + 64 * h + p, OOB), "l_k": np.where(p < 64, 256 + 64 * h + p, OOB),
        "l_v": np.where(p < 64, 512 + 64 * h + p, OOB), "l_o": np.where(p < 64, 768 + 64 * h + p, OOB),
        "l_g": np.where(p < 4, 1024 + 4 * p + h, OOB),
        "l_grow1": np.full(128, 1024 + 4 * 1 + h), "l_grow3": np.full(128, 1024 + 4 * 3 + h),
        "s_x": np.where(p < 64, SS + 256 + 64 * h + p, OOB), "s_B": SS + 512 + 128 * g + p, "s_C": SS + 768 + 128 * g + p,
        "s_z": np.where(p < 64, SS + 64 * h + p, OOB), "s_dt": np.where(p < 2, SS + 1024 + 4 * p + h, OOB),
        "s_dtrow0": np.full(128, SS + 1024 + h), "s_dtrow1": np.full(128, SS + 1024 + 4 + h),
        "s5_u0": np.where(p < 32, SS + 1032 + 64 * h + p, OOB), "s5_u1": np.where(p < 32, SS + 1032 + 64 * h + 32 + p, OOB),
    }
    idxB = np.zeros((128, NIB), np.int64)
    for name, r in rows.items():
        for qp in range(4):
            for pc, w in enumerate((NLAT, NCTX)):
                off = (A_LAT_OFF, A_CTX_OFF)[pc]
                rr = np.maximum(r, 0)
                idxB[:, (OPIDX[name] * 4 + qp) * 2 + pc] = _gaddr(off + rr * w, qp, CHA) // w
    idxC = np.zeros((128, 7 * KC), np.int64)
    for pi, (pc0, pw) in enumerate(Y_PIECES):
        for k in range(KC):
            m, hp = k // 2, 2 * (k % 2) + p // 64
            idxC[:, pi * KC + k] = _gaddr(Y_OFF[pi] + (q * 256 + m * 64 + (p % 64)) * pw, hp, CHY) // pw
    idxX = np.zeros((128, 2 * KC), np.int64)
    for k in range(KC):
        for side, qn_ in ((0, q - 1), (1, q + 1)):
            idxX[:, 2 * k + side] = ((qn_ if 0 <= qn_ <= 3 else q) * D + k * 128 + p)
    return idxB.astype(np.int32), idxC.astype(np.int32), idxX.astype(np.int32)


def _layer_weights(inp, NL, h):
    g = h // 2
    W = {name: [] for name, _ in LAYER_W}
    for l in range(NL):
        win = inp['w_in'][l]
        W["wmodA"].append(inp['w_mod'][l][:, 0:2 * D]); W["bmodA"].append(inp['b_mod'][l][0:2 * D].reshape(16, 128).T)
        W["wmodC"].append(inp['w_mod'][l][:, 2 * D:6 * D]); W["bmodC"].append(inp['b_mod'][l][2 * D:6 * D].reshape(32, 128).T)
        W["norm1"].append(inp['norm1'][l].reshape(8, 128).T); W["norm2"].append(inp['norm2'][l].reshape(8, 128).T)
        W["win"].append(win); W["winsw"].append(np.concatenate([win[:, 1440:1456], win[:, 1424:1440]], axis=1))
        W["qn"].append(inp['mla_q_norm'][l].reshape(2, 128).T); W["kvn"].append(inp['mla_kv_norm'][l].reshape(1, 128).T)
        wuq = inp['mla_w_uq'][l]; w4 = wuq.reshape(256, 4, 96)
        W["wuq"].append(wuq); W["wuqsw"].append(np.concatenate([w4[:, :, :64], w4[:, :, 80:96], w4[:, :, 64:80]], axis=2).reshape(256, 384))
        W["wukv"].append(inp['mla_w_ukv'][l].reshape(128, 4, 2, 64).transpose(0, 2, 1, 3).reshape(128, 512))
        W["snorm"].append(inp['ssd_norm'][l].reshape(2, 128).T); W["wglu"].append(inp['s5_w_glu'][l]); W["wout"].append(inp['w_out'][l])
        W["wup"].append(inp['ffn_w_up'][l]); W["convw"].append(inp['ffn_conv_w'][l].reshape(3, NFF, 128).transpose(2, 1, 0))
        W["wdown"].append(inp['ffn_w_down'][l])
        W["l_gbias"].append(inp['ml_gate_bias'][l][:, h].reshape(1, 4)); W["l_norm"].append(inp['ml_norm'][l][64 * h:64 * h + 64].reshape(1, 64))
        cw = np.zeros((128, 3, 4), np.float32)
        for blk, (ch0, n) in enumerate(((64 * h, 64), (256 + 128 * g, 128), (512 + 128 * g, 128))):
            cw[:n, blk, 0:3] = inp['ssd_conv_w'][l][:, ch0:ch0 + n].T
            cw[:n, blk, 3] = inp['ssd_conv_b'][l][ch0:ch0 + n]
        W["s_convw"].append(cw)
        W["s_dtbias"].append(inp['ssd_dt_bias'][l][:, h].reshape(1, 2)); W["s_alog"].append(inp['ssd_a_log'][l][:, h].reshape(1, 2))
        W["s_dskip"].append(inp['ssd_d'][l][h].reshape(1, 1))
        are = np.zeros((128, 4), np.float32); aim = np.zeros((128, 4), np.float32); ldt = np.zeros((128, 4), np.float32)
        bre = np.zeros((128, 2, 32), np.float32); bim = np.zeros((128, 2, 32), np.float32)
        cre = np.zeros((128, 2, 32), np.float32); cim = np.zeros((128, 2, 32), np.float32)
        dsk = np.zeros((32, 2), np.float32)
        for gp in range(2):
            for g2 in range(2):
                gg = 4 * h + 2 * gp + g2
                ps = slice(64 * g2, 64 * g2 + 64)
                for d in range(2):
                    are[ps, gp * 2 + d] = inp['s5_a_re'][l][d, gg]; aim[ps, gp * 2 + d] = inp['s5_a_im'][l][d, gg]
                    ldt[ps, gp * 2 + d] = inp['s5_log_dt'][l][d, gg]
                bre[ps, gp, 16 * g2:16 * g2 + 16] = inp['s5_b_re'][l][gg]; bim[ps, gp, 16 * g2:16 * g2 + 16] = inp['s5_b_im'][l][gg]
                cre[ps, gp, 16 * g2:16 * g2 + 16] = inp['s5_c_re'][l][gg].T; cim[ps, gp, 16 * g2:16 * g2 + 16] = inp['s5_c_im'][l][gg].T
                dsk[16 * g2:16 * g2 + 16, gp] = inp['s5_d'][l][16 * gg:16 * gg + 16]
        for nm, v in (("s5_are", are), ("s5_aim", aim), ("s5_ldt", ldt), ("s5_bre", bre), ("s5_bim", bim), ("s5_cre", cre), ("s5_cim", cim), ("s5_d", dsk)):
            W[nm].append(v)
    return {k: _c32(np.stack(v)) for k, v in W.items()}


def _fused_inputs(inp, NL=4):
    cos, sin = _rope_tables()
    maps = []
    shared = {}
    for k in range(8):
        b, q = k // 4, k % 4
        if q not in shared:
            shared[q] = _layer_weights(inp, NL, q)
        m = dict(shared[q])
        m["x0"] = _c32(np.concatenate([inp['x'][b, NLAT * q:NLAT * (q + 1)].T, inp['ctx'][b, NCTX * q:NCTX * (q + 1)].T], axis=1))
        m["cvec"] = _c32(np.stack([inp['c'][b].reshape(8, 128).T, inp['c_ctx'].reshape(8, 128).T], axis=-1))
        c_l = cos[NLAT * q:NLAT * (q + 1)].T
        s_l = sin[NLAT * q:NLAT * (q + 1)].T
        m["ropeC"] = _c32(np.concatenate([np.concatenate([c_l, c_l], 0), np.ones((32, NCTX), np.float32)], axis=1))
        m["ropeS"] = _c32(np.concatenate([np.concatenate([-s_l, s_l], 0), np.zeros((32, NCTX), np.float32)], axis=1))
        m["fnorm"] = _c32(inp['final_norm'].reshape(8, 128).T)
        m["hmask"] = _c32(np.broadcast_to(np.array([q > 0, q < 3, q > 0, q < 3], np.float32), (128, 4)))
        m["idxB"], m["idxC"], m["idxX"] = _index_tables(q, q)
        maps.append(m)
    return maps


_PROG = {}


def kernel(**inputs):
    inp = {k: np.asarray(v) for k, v in inputs.items()}
    if "f" not in _PROG:
        _PROG["f"] = build_fused(4)
    res = run_bass_kernel_spmd(_PROG["f"], _fused_inputs(inp, 4), core_ids=list(range(8)))
    out = np.stack([np.concatenate([res.results[4 * b + q]["xTo"].T for q in range(4)], axis=0) for b in range(2)])
    return np.ascontiguousarray(out, dtype=np.float32)
```

```python
import numpy as np
import concourse.bass as bass
import concourse.mybir as mybir
from concourse.bass_utils import run_bass_kernel_spmd
from contextlib import ExitStack

F32 = mybir.dt.float32
BF16 = mybir.dt.bfloat16
I32 = mybir.dt.int32
AF = mybir.ActivationFunctionType
ALU = mybir.AluOpType
AX = mybir.AxisListType

NDMASEM = 8
NCCSEM = 16
EPS = 1e-6


class _Op:
    __slots__ = ("eng", "fn", "deps", "id", "ticket", "dma", "has_dep", "pre", "sem", "cc", "info")


class Prog:
    ENGS = ("pe", "act", "dve", "pool", "sp")

    def __init__(self, nc):
        self.nc = nc
        self.st = ExitStack()
        self.ops = []
        self.state = {}
        self.ntile = 0
        self.evi = 0
        self.bar = set()
        self.nowaw_ops = set()
        self.arena = None
        self.aoff = 0
        self.last_by_eng = {}
        self.dma_recent = {}

    def sb(self, shape, dtype=F32, name=None):
        self.ntile += 1
        name = name or "t"
        return self.st.enter_context(self.nc.sbuf_tensor(f"{name}_{self.ntile}", list(shape), dtype))

    def ps(self, shape, dtype=F32, name=None):
        self.ntile += 1
        name = name or "p"
        return self.st.enter_context(self.nc.psum_tensor(f"{name}_{self.ntile}", list(shape), dtype))

    def arena_init(self, nbytes):
        self.arena = self.sb([128, nbytes // 4], F32, "arena")
        self.acap = nbytes
        self.aoff = 0

    def al(self, shape, dtype=F32, name=None):
        esz = 4 if dtype in (F32, I32) else 2
        nfree = 1
        for d_ in shape[1:]:
            nfree *= d_
        nb = (nfree * esz + 3) // 4 * 4
        assert self.aoff + nb <= self.acap, f"arena overflow {self.aoff + nb} > {self.acap}"
        v = self.arena[0:shape[0], self.aoff // 4:(self.aoff + nb) // 4]
        self.aoff += nb
        if esz == 2:
            v = v.bitcast(dtype)
        if len(shape) == 3:
            v = v.rearrange("p (a b) -> p a b", b=shape[2])
        elif len(shape) == 4:
            v = v.rearrange("p (a b c) -> p a b c", b=shape[2], c=shape[3])
        return v

    def mark(self):
        return self.aoff

    def release(self, mark):
        self.aoff = mark
        bar = set(self.last_by_eng.values())
        for lst in self.dma_recent.values():
            bar.update(lst)
        self.bar = bar

    def add(self, eng, fn, reads=(), writes=(), dma=False, cc=False, nowaw=False):
        op = _Op()
        op.eng = eng
        op.fn = fn
        op.id = len(self.ops)
        op.dma = dma
        op.cc = cc
        op.info = (tuple(reads), tuple(writes))
        op.has_dep = False
        op.ticket = None
        op.pre = None
        op.sem = None
        deps = set()
        for k in reads:
            s = self.state.setdefault(k, [[], []])
            deps.update(s[0])
        for k in writes:
            s = self.state.setdefault(k, [[], []])
            deps.update(s[1])
            if not nowaw:
                deps.update(s[0])
            else:
                deps.update(w for w in s[0] if w not in self.nowaw_ops)
        for k in reads:
            self.state[k][1].append(op.id)
        for k in writes:
            s = self.state[k]
            if nowaw and not s[1]:
                s[0] = s[0] + [op.id]
            else:
                s[0] = [op.id]
            s[1] = []
        if nowaw:
            self.nowaw_ops.add(op.id)
        deps.update(self.bar)
        deps.discard(op.id)
        op.deps = deps
        self.ops.append(op)
        if cc:
            self.dma_recent.setdefault("cc", []).append(op.id)
        elif dma:
            lst = self.dma_recent.setdefault(eng, [])
            lst.append(op.id)
            if len(lst) > NDMASEM:
                lst.pop(0)
        else:
            self.last_by_eng[eng] = op.id
        return op

    def mm(self, out, lhsT, rhs, start=True, stop=True, reads=(), writes=(), **kw):
        return self.add("pe", lambda e: e.matmul(out, lhsT, rhs, start=start, stop=stop, **kw), reads, writes)

    def tr(self, out, in_, ident, reads=(), writes=()):
        return self.add("pe", lambda e: e.transpose(out, in_, ident), reads, writes)

    def actf(self, out, in_, func, bias=None, scale=1.0, accum_out=None, reads=(), writes=()):
        kw = {}
        if bias is not None:
            kw["bias"] = bias
        if accum_out is not None:
            kw["accum_out"] = accum_out
        return self.add("act", lambda e: e.activation(out, in_, func, scale=scale, **kw), reads, writes)

    def dma(self, out, in_, reads=(), writes=(), eng="sp", nowaw=False, **kw):
        return self.add(eng, lambda e: e.dma_start(out=out, in_=in_, **kw), reads, writes, dma=True, nowaw=nowaw)

    def v(self, eng, name, *args, reads=(), writes=(), **kw):
        return self.add(eng, lambda e: getattr(e, name)(*args, **kw), reads, writes)

    def cast(self, out, in_, reads=(), writes=()):
        self.cvi = getattr(self, "cvi", 0) + 1
        m = self.cvi % 6
        if m in (0, 2, 4):
            return self.actf(out, in_, AF.Identity, reads=reads, writes=writes)
        if m in (1, 3):
            return self.v("dve", "tensor_copy", out, in_, reads=reads, writes=writes)
        return self.v("pool", "tensor_copy", out, in_, reads=reads, writes=writes)

    def evac(self, out, in_, scale=None, reads=(), writes=()):
        self.evi += 1
        if self.evi % 2 == 0:
            return self.actf(out, in_, AF.Identity, scale=(1.0 if scale is None else scale), reads=reads, writes=writes)
        if scale is None:
            return self.v("dve", "tensor_copy", out, in_, reads=reads, writes=writes)
        return self.v("dve", "tensor_scalar_mul", out, in_, scale, reads=reads, writes=writes)

    def emit(self):
        nc = self.nc
        ops = self.ops
        for op in ops:
            for d in op.deps:
                p = ops[d]
                if p.eng == "pe" and op.eng == "pe" and not p.dma and not op.dma:
                    continue
                p.has_dep = True
        cnt = {e: 0 for e in self.ENGS}
        dcnt = {e: 0 for e in self.ENGS}
        ncc = 0
        for op in ops:
            if op.cc:
                op.sem = ("cc", ncc % NCCSEM)
                op.ticket = ncc // NCCSEM + 1
                op.pre = (op.sem, op.ticket - 1) if op.ticket > 1 else None
                ncc += 1
            elif op.dma:
                i = dcnt[op.eng]
                dcnt[op.eng] += 1
                slot = i % NDMASEM
                val = 16 * (i // NDMASEM + 1)
                op.sem = ("d", op.eng, slot)
                op.ticket = val
                op.pre = (op.sem, val - 16) if val > 16 else None
            elif op.has_dep:
                cnt[op.eng] += 1
                op.sem = ("c", op.eng)
                op.ticket = cnt[op.eng]
        sems = {}
        for e in self.ENGS:
            sems[("c", e)] = self.st.enter_context(nc.semaphore(f"c_{e}"))
            if dcnt[e]:
                for s in range(NDMASEM):
                    sems[("d", e, s)] = self.st.enter_context(nc.semaphore(f"d_{e}_{s}"))
        for i in range(min(ncc, NCCSEM)):
            sems[("cc", i)] = self.st.enter_context(nc.semaphore(f"cc_{i}"))
        by_eng = {e: [op for op in ops if op.eng == e] for e in self.ENGS}
        self.stats = {e: len(by_eng[e]) for e in self.ENGS}

        def run(E, e):
            known = {}

            def wait(sk, val):
                if known.get(sk, 0) >= val:
                    return
                e.wait_ge(sems[sk], val)
                known[sk] = val

            last_dma = {}
            for op in by_eng[E]:
                if op.pre is not None:
                    wait(*op.pre)
                for d in sorted(op.deps):
                    p = ops[d]
                    if p.ticket is None:
                        continue
                    if (not p.dma) and (not op.dma) and p.eng == "pe" and E == "pe":
                        continue
                    wait(p.sem, p.ticket)
                try:
                    ins = op.fn(e)
                except Exception:
                    print("EMIT FAILED at op", op.id, op.eng, op.info, flush=True)
                    raise
                if op.cc:
                    ins.then_inc(sems[op.sem], 1)
                    last_dma[op.sem] = op.ticket
                elif op.dma:
                    ins.then_inc(sems[op.sem], 16)
                    last_dma[op.sem] = op.ticket
                elif op.ticket is not None:
                    ins.then_inc(sems[op.sem], 1)
            for sk, val in last_dma.items():
                wait(sk, val)

        with nc.Block() as block:
            @block.tensor
            def _(e):
                run("pe", e)

            @block.scalar
            def _(e):
                run("act", e)

            @block.vector
            def _(e):
                run("dve", e)

            @block.gpsimd
            def _(e):
                run("pool", e)

            @block.sync
            def _(e):
                run("sp", e)
        self.st.close()


D = 1024
KC = 8
B_ = 2
SEQ = 8192
CTX = 256
NLAT = 2048
NCTX = 64
NT = NLAT + NCTX
P_IN = 2744
DFF = 2816
NFF = 22
R_ML = 0
R_SS = 1040
R_Q = 2328
R_KN = 2712
R_V = 2968
R_KR = 3224
NPT = 3256


def _dram(nc, name, shape, kind="ExternalInput", dtype=F32):
    return nc.dram_tensor(name, list(shape), dtype, kind=kind).ap()


def _rms_rstd(P, ps_ss, n, nfeat, rstd, key_ps, key_out, tmp, key_tmp):
    P.v("dve", "tensor_scalar", tmp[:, :n], ps_ss[:, :n], 1.0 / nfeat, EPS, ALU.mult, ALU.add,
        reads=[key_ps], writes=[key_tmp])
    P.actf(tmp[:, :n], tmp[:, :n], AF.Sqrt, reads=[key_tmp], writes=[key_tmp])
    P.v("dve", "reciprocal", rstd[:, :n], tmp[:, :n], reads=[key_tmp], writes=[key_out])


def _mod_vectors(P, wmod, bmod_sb, cs, parts, wst, wst_keys, psb, psb_key, modv):
    for j, part in enumerate(parts):
        for k in range(KC):
            buf = k % 2
            P.dma(wst[buf][:, 0:1024], wmod[k * 128:(k + 1) * 128, part * 1024:(part + 1) * 1024],
                  writes=[wst_keys[buf]])
            for dc in range(KC):
                P.mm(psb[:, 2 * dc:2 * dc + 2], wst[buf][:, dc * 128:(dc + 1) * 128], cs[:, k, :],
                     start=(k == 0 and dc == 0), stop=(k == KC - 1), skip_group_check=True,
                     reads=[wst_keys[buf], "cs"], writes=[psb_key])
        for dc in range(KC):
            P.v("dve", "tensor_scalar", modv[:, j, dc, :], psb[:, 2 * dc:2 * dc + 2],
                bmod_sb[:, part * 8 + dc:part * 8 + dc + 1], None, ALU.add,
                reads=[psb_key, "bmod"], writes=["modv"])


def phase_A(P, E, l):
    m0 = P.mark()
    banks = E["banks"]
    xT = E["XS"][l % 2]
    wmod, bmod, norm1 = E["wmodA"][l], E["bmodA"][l], E["norm1"][l]
    win, winsw, qn, kvn = E["win"][l], E["winsw"][l], E["qn"][l], E["kvn"][l]
    wuq, wuqsw, wukv = E["wuq"][l], E["wuqsw"][l], E["wukv"][l]
    ropeC, ropeS = E["ropeC"], E["ropeS"]
    EXf = _flat(E["EXA"])
    PTl = EXf[A_LAT_OFF:A_LAT_OFF + NPT * NLAT].rearrange("(r w) -> r w", w=NLAT)
    PTc = EXf[A_CTX_OFF:A_CTX_OFF + NPT * NCTX].rearrange("(r w) -> r w", w=NCTX)

    def PTdst(r0, r1, c0, n, seg):
        return PTl[r0:r1, c0:c0 + n] if seg == 0 else PTc[r0:r1, 0:n]
    ones_b, cs = E["ones_b"], E["cs"]
    bmod_sb = P.al([128, 16], F32)
    P.dma(bmod_sb[:], bmod, writes=["bmod"])
    n1 = P.al([128, KC], F32)
    P.dma(n1[:], norm1, writes=["n1"])
    qn_sb = P.al([128, 2], F32)
    P.dma(qn_sb[:], qn, writes=["qn"])
    kvn_sb = P.al([128, 1], F32)
    P.dma(kvn_sb[:], kvn, writes=["kvn"])
    tabC = P.al([96, NT], F32)
    tabS = P.al([96, NT], F32)
    P.dma(tabC[64:96, :], ropeC, writes=["tab"])
    P.dma(tabS[64:96, :], ropeS, writes=["tab"])
    P.dma(tabC[0:32, :], ropeC, writes=["tab"])
    P.dma(tabS[0:32, :], ropeS, writes=["tab"])

    wst = [P.al([128, P_IN], F32, f"wst{i}") for i in range(2)]
    wst_keys = ["wst0", "wst1"]
    psb = banks[6]
    modv = P.al([128, 2, KC, 2], F32, "modv")
    _mod_vectors(P, wmod, bmod_sb, cs, [0, 1], wst, wst_keys, psb, "bank6", modv)
    Amod = P.al([128, KC, 2], F32, "Amod")
    P.v("dve", "tensor_scalar_add", Amod[:], modv[:, 1, :, :], 1.0, reads=["modv"], writes=["Amod"])
    for s_ in range(2):
        P.v("dve", "tensor_tensor", Amod[:, :, s_], Amod[:, :, s_], n1[:], ALU.mult,
            reads=["Amod", "n1"], writes=["Amod"])

    win_b = P.al([128, KC, P_IN], BF16, "winb")
    winsw_b = P.al([128, KC, 32], BF16, "winswb")
    for k in range(KC):
        buf = k % 2
        P.dma(wst[buf][:, :], win[k * 128:(k + 1) * 128, :], writes=[wst_keys[buf]])
        P.cast(win_b[:, k, :], wst[buf][:, :], reads=[wst_keys[buf]], writes=["winb"])
    sm = P.al([128, KC, 32], F32, "smallst")
    P.dma(sm[:], winsw.rearrange("(k p) c -> p k c", p=128), writes=["smallst"])
    P.v("pool", "tensor_copy", winsw_b[:], sm[:], reads=["smallst"], writes=["winb"])
    wuq_b = P.al([128, 2, 384], BF16, "wuqb")
    wuqsw_b = P.al([128, 2, 384], BF16, "wuqswb")
    wukv_b = P.al([128, 512], BF16, "wukvb")
    st2 = P.al([128, 2, 384], F32, "st2")
    P.dma(st2[:], wuq.rearrange("(k p) c -> p k c", p=128), writes=["st2"])
    P.v("pool", "tensor_copy", wuq_b[:], st2[:], reads=["st2"], writes=["wmla"])
    P.dma(st2[:], wuqsw.rearrange("(k p) c -> p k c", p=128), writes=["st2"])
    P.v("pool", "tensor_copy", wuqsw_b[:], st2[:], reads=["st2"], writes=["wmla"])
    P.dma(st2[:, 0, :], wukv[:, 0:384], writes=["st2"])
    P.dma(st2[:, 1, 0:128], wukv[:, 384:512], writes=["st2"])
    P.v("pool", "tensor_copy", wukv_b[:, 0:384], st2[:, 0, :], reads=["st2"], writes=["wmla"])
    P.v("pool", "tensor_copy", wukv_b[:, 384:512], st2[:, 1, 0:128], reads=["st2"], writes=["wmla"])

    xt = [P.al([128, KC, 512], F32, f"xt{i}") for i in range(2)]
    xsq = P.al([128, KC, 512], BF16, "xsq")
    hT = P.al([128, KC, 512], BF16, "hT")
    tmpn = P.al([128, 512], F32, "tmpn")
    rstd = P.al([128, 512], F32, "rstd")
    tmpx = [P.al([128, 512], F32, f"tmpx{i}") for i in range(2)]
    stage = [P.al([128, 512], F32, f"stg{i}") for i in range(4)]
    cq = P.al([128, 2, 512], F32, "cq")
    ckv = P.al([128, 512], F32, "ckv")
    krr = P.al([32, 2, 512], F32, "krr")
    cqn = P.al([128, 2, 512], BF16, "cqn")
    ckvn = P.al([128, 512], BF16, "ckvn")
    sq2 = P.al([128, 2, 512], BF16, "sq2")
    rt1 = P.al([96, 512], F32, "rt1")
    rt2 = P.al([96, 512], F32, "rt2")
    pss = banks[7]
    psm = banks[0:4]
    psq = banks[4:6]

    tiles = [(i * 512, 512, 0) for i in range(4)] + [(NLAT, NCTX, 1)]
    nstage = 0
    nps = 0
    for ti, (c0, n, seg) in enumerate(tiles):
        xb = xt[ti % 2]
        xk = f"xt{ti % 2}"
        P.dma(xb[:, :, :n], xT[:, c0:c0 + n].rearrange("(k p) n -> p k n", p=128), reads=[f"XS{l % 2}"], writes=[xk])
        P.actf(xsq[:, :, :n], xb[:, :, :n], AF.Square, reads=[xk], writes=["xsq"])
        for k in range(KC):
            P.mm(pss[:, :n], ones_b[:], xsq[:, k, :n], start=(k == 0), stop=(k == KC - 1),
                 reads=["ones", "xsq"], writes=["bank7"])
        _rms_rstd(P, pss, n, D, rstd, "bank7", "rstd", tmpn, "tmpn")
        for k in range(KC):
            tb = tmpx[k % 2]
            tk = f"tmpx{k % 2}"
            P.v("dve", "tensor_tensor", tb[:, :n], xb[:, k, :n], rstd[:, :n], ALU.mult,
                reads=[xk, "rstd"], writes=[tk])
            P.actf(hT[:, k, :n], tb[:, :n], AF.Identity, bias=modv[:, 0, k, seg:seg + 1],
                   scale=Amod[:, k, seg:seg + 1], reads=[tk, "Amod", "modv"], writes=["hT"])

        def inproj(col0, m, wsrc=None):
            nonlocal nps
            ps = psm[nps % 4]
            pk = f"bank{nps % 4}"
            nps += 1
            for k in range(KC):
                lhs = win_b[:, k, col0:col0 + m] if wsrc is None else wsrc[:, k, 0:m]
                P.mm(ps[:m, :n], lhs, hT[:, k, :n], start=(k == 0), stop=(k == KC - 1),
                     reads=["winb", "hT"], writes=[pk])
            return ps, pk

        def out_chunk(ps, pk, m, row0, scale=None):
            nonlocal nstage
            sg = stage[nstage % 4]
            sk = f"stg{nstage % 4}"
            nstage += 1
            P.evac(sg[:m, :n], ps[:m, :n], scale=scale, reads=[pk], writes=[sk])
            P.dma(PTdst(row0, row0 + m, c0, n, seg), sg[:m, :n], reads=[sk], writes=["PT"], nowaw=True)

        for i in range(9):
            col0 = i * 128
            m = 128 if i < 8 else 16
            ps, pk = inproj(col0, m)
            out_chunk(ps, pk, m, R_ML + col0, scale=(0.125 if i in (2, 3) else None))
        for i in range(11):
            col0 = 1456 + i * 128
            m = 128 if i < 10 else 8
            ps, pk = inproj(col0, m)
            out_chunk(ps, pk, m, R_SS + i * 128)
        for j in range(2):
            ps, pk = inproj(1040 + j * 128, 128)
            P.evac(cq[:, j, :n], ps[:, :n], reads=[pk], writes=["cq"])
        ps, pk = inproj(1296, 128)
        P.evac(ckv[:, :n], ps[:, :n], reads=[pk], writes=["ckv"])
        ps, pk = inproj(1424, 32)
        P.evac(krr[:, 0, :n], ps[:32, :n], reads=[pk], writes=["krr"])
        ps, pk = inproj(0, 32, wsrc=winsw_b)
        P.evac(krr[:, 1, :n], ps[:32, :n], reads=[pk], writes=["krr"])
        P.actf(sq2[:, :, :n], cq[:, :, :n], AF.Square, reads=["cq"], writes=["sq2"])
        for j in range(2):
            P.mm(pss[:, :n], ones_b[:], sq2[:, j, :n], start=(j == 0), stop=(j == 1),
                 reads=["ones", "sq2"], writes=["bank7"])
        _rms_rstd(P, pss, n, 256, rstd, "bank7", "rstd", tmpn, "tmpn")
        for j in range(2):
            tb = tmpx[j % 2]
            tk = f"tmpx{j % 2}"
            P.v("dve", "tensor_tensor", tb[:, :n], cq[:, j, :n], rstd[:, :n], ALU.mult,
                reads=["cq", "rstd"], writes=[tk])
            P.actf(cqn[:, j, :n], tb[:, :n], AF.Identity, scale=qn_sb[:, j:j + 1], reads=[tk, "qn"], writes=["cqn"])
        P.actf(sq2[:, 0, :n], ckv[:, :n], AF.Square, reads=["ckv"], writes=["sq2"])
        P.mm(pss[:, :n], ones_b[:], sq2[:, 0, :n], reads=["ones", "sq2"], writes=["bank7"])
        _rms_rstd(P, pss, n, 128, rstd, "bank7", "rstd", tmpn, "tmpn")
        P.v("dve", "tensor_tensor", tmpx[0][:, :n], ckv[:, :n], rstd[:, :n], ALU.mult,
            reads=["ckv", "rstd"], writes=["tmpx0"])
        P.actf(ckvn[:, :n], tmpx[0][:, :n], AF.Identity, scale=kvn_sb[:, 0:1], reads=["tmpx0", "kvn"], writes=["ckvn"])
        for h in range(4):
            pa, pb = psq[0], psq[1]
            for j in range(2):
                P.mm(pa[:96, :n], wuq_b[:, j, h * 96:(h + 1) * 96], cqn[:, j, :n], start=(j == 0), stop=(j == 1),
                     reads=["wmla", "cqn"], writes=["bank4"])
            for j in range(2):
                P.mm(pb[:96, :n], wuqsw_b[:, j, h * 96:(h + 1) * 96], cqn[:, j, :n], start=(j == 0), stop=(j == 1),
                     reads=["wmla", "cqn"], writes=["bank5"])
            sg = stage[nstage % 4]
            sk = f"stg{nstage % 4}"
            nstage += 1
            P.actf(sg[0:64, :n], pa[0:64, :n], AF.Identity, reads=["bank4"], writes=[sk])
            P.v("dve", "tensor_tensor", rt1[64:96, :n], pa[64:96, :n], tabC[64:96, c0:c0 + n], ALU.mult,
                reads=["bank4", "tab"], writes=["rt1"])
            P.v("dve", "tensor_tensor", rt2[64:96, :n], pb[64:96, :n], tabS[64:96, c0:c0 + n], ALU.mult,
                reads=["bank5", "tab"], writes=["rt2"])
            P.v("pool", "tensor_tensor", sg[64:96, :n], rt1[64:96, :n], rt2[64:96, :n], ALU.add,
                reads=["rt1", "rt2"], writes=[sk])
            P.dma(PTdst(R_Q + h * 96, R_Q + (h + 1) * 96, c0, n, seg), sg[:96, :n], reads=[sk], writes=["PT"], nowaw=True)
        for c in range(2):
            for which, row0 in ((0, R_KN), (1, R_V)):
                ps = psm[nps % 4]
                pk = f"bank{nps % 4}"
                nps += 1
                P.mm(ps[:, :n], wukv_b[:, which * 256 + c * 128:which * 256 + (c + 1) * 128], ckvn[:, :n],
                     reads=["wmla", "ckvn"], writes=[pk])
                out_chunk(ps, pk, 128, row0 + c * 128)
        sg = stage[nstage % 4]
        sk = f"stg{nstage % 4}"
        nstage += 1
        P.v("dve", "tensor_tensor", rt1[0:32, :n], krr[:, 0, :n], tabC[0:32, c0:c0 + n], ALU.mult,
            reads=["krr", "tab"], writes=["rt1"])
        P.v("dve", "tensor_tensor", rt2[0:32, :n], krr[:, 1, :n], tabS[0:32, c0:c0 + n], ALU.mult,
            reads=["krr", "tab"], writes=["rt2"])
        P.v("pool", "tensor_tensor", sg[0:32, :n], rt1[0:32, :n], rt2[0:32, :n], ALU.add,
            reads=["rt1", "rt2"], writes=[sk])
        P.dma(PTdst(R_KR, R_KR + 32, c0, n, seg), sg[:32, :n], reads=[sk], writes=["PT"], nowaw=True)
    P.release(m0)


TALL = CTX + SEQ
NCH = TALL // 64
NJ = TALL // 128
CT = 512
EXT = NLAT + 2 + NCTX + 2
LAT0, LAT1 = 0, NLAT + 2
CX0, CX1 = NLAT + 2, EXT


def _col_tiles(a, b, w=CT):
    out = []
    c = a
    while c < b:
        out.append((c, min(w, b - c)))
        c += w
    return out


CHA = 262144
NCHA = 27
RA_PAD = NCHA * CHA
A_LAT_OFF, A_CTX_OFF = 0, NPT * NLAT
Y_PIECES = [(0, 512), (512, 512), (1024, 2), (1024, 512), (1536, 512), (2048, 2), (2050, 66)]
CHY = 15 * 16896
Y_OFF = []
_o = 0
for (_c, _w) in Y_PIECES:
    _o = -(-_o // 16896) * 16896
    Y_OFF.append(_o)
    _o += D * _w
NCHY = -(-_o // CHY)
RY_PAD = NCHY * CHY
assert NPT * NT <= RA_PAD and A_CTX_OFF % NCTX == 0 and CHA % NLAT == 0


def _gaddr(f, rank, ch):
    return (f // ch) * 4 * ch + rank * ch + (f % ch)


def _flat(ap):
    return ap.rearrange("a b -> (a b)")


SSR = R_SS
OP_ROWS = {
    "m_QT": [(R_Q, R_Q + 384)], "m_KT": [(R_KN, R_KN + 256), (R_KR, R_KR + 32)], "m_V": [(R_V, R_V + 256)],
    "l_q": [(0, 256)], "l_k": [(256, 512)], "l_v": [(512, 768)], "l_o": [(768, 1024)],
    "l_g": [(1024, 1040)], "l_grow1": [(1024, 1040)], "l_grow3": [(1024, 1040)],
    "s_x": [(SSR + 256, SSR + 512)], "s_B": [(SSR + 512, SSR + 768)], "s_C": [(SSR + 768, SSR + 1024)], "s_z": [(SSR, SSR + 256)],
    "s_dt": [(SSR + 1024, SSR + 1032)], "s_dtrow0": [(SSR + 1024, SSR + 1032)], "s_dtrow1": [(SSR + 1024, SSR + 1032)],
    "s5_u0": [(SSR + 1032, SSR + 1288)], "s5_u1": [(SSR + 1032, SSR + 1288)],
}


def _op_chunks(op):
    cs = set()
    for (a, b) in OP_ROWS[op]:
        for r in (a, b - 1):
            pass
        cs.update(range((a * NLAT) // CHA, ((b - 1) * NLAT) // CHA + 1))
        cs.update(range((A_CTX_OFF + a * NCTX) // CHA, (A_CTX_OFF + (b - 1) * NCTX) // CHA + 1))
    return sorted(cs)


OPS_B = ["m_QT", "m_KT", "m_V", "l_q", "l_k", "l_v", "l_o", "l_g", "l_grow1", "l_grow3",
         "s_x", "s_B", "s_C", "s_z", "s_dt", "s_dtrow0", "s_dtrow1", "s5_u0", "s5_u1"]
OPIDX = {n: i for i, n in enumerate(OPS_B)}
NIB = 8 * len(OPS_B)


def gather_fm(P, E, dst, op, nr, key):
    it = E["idxB"]
    Gf = _flat(E["GEXA"])
    for qp in range(4):
        for pc, (o0, o1, w, eoff) in enumerate(((CTX + NLAT * qp, CTX + NLAT * (qp + 1), NLAT, 0),
                                                (NCTX * qp, NCTX * (qp + 1), NCTX, 0))):
            c = (OPIDX[op] * 4 + qp) * 2 + pc
            src = Gf.rearrange("(r w) -> r w", w=w)
            P.add("pool", (lambda c, o0, o1, src, eoff: (lambda e: e.indirect_dma_start(
                out=dst[0:nr, o0:o1], out_offset=None, in_=src,
                in_offset=bass.IndirectOffsetOnAxis(ap=it[0:nr, c:c + 1], axis=0), element_offset=eoff)))(c, o0, o1, src, eoff),
                reads=[f"GEXA{c_}" for c_ in _op_chunks(op)] + ["idxB"], writes=[key], dma=True, nowaw=True)


def fm_to_tok(P, E, src, r, dst, key_src, key_dst):
    B, ident = E["banks"], E["ident"]
    for g in range(0, NJ, 4):
        ps, pk = B[6 + (g // 4) % 2], f"bank{6 + (g // 4) % 2}"
        nj = min(4, NJ - g)
        for jj in range(nj):
            j = g + jj
            P.tr(ps[:, 128 * jj:128 * jj + r], src[0:r, 128 * j:128 * j + 128], ident[0:r, 0:r], reads=[key_src, "ident"], writes=[pk])
        view = ps[:, 0:128 * nj].rearrange("p (j e) -> p j e", e=128)[:, :, 0:r]
        P.evac(dst[:, g:g + nj, 0:r], view, reads=[pk], writes=[key_dst])


def tok_to_fm(P, E, src, dst, key_src, key_dst):
    B, ident = E["banks"], E["ident"]
    for g in range(0, NJ, 4):
        ps, pk = B[6 + (g // 4) % 2], f"bank{6 + (g // 4) % 2}"
        nj = min(4, NJ - g)
        for jj in range(nj):
            P.tr(ps[0:64, 128 * jj:128 * jj + 128], src[:, g + jj, :], ident[:, :], reads=[key_src, "ident"], writes=[pk])
        P.evac(dst[0:64, 128 * g:128 * (g + nj)], ps[0:64, 0:128 * nj], reads=[pk], writes=[key_dst])


def store_cols(P, E, row0, nr, src_fn, col0, n, key):
    Yf = _flat(E["EXY"])
    a, b = col0, col0 + n
    for q in range(4):
        segs = []
        lo, hi = max(a, CTX + NLAT * q - 1, CTX), min(b, CTX + NLAT * (q + 1) + 1, TALL)
        if lo < hi:
            segs.append((lo, hi, lo - (CTX + NLAT * q - 1)))
        lo, hi = max(a, NCTX * q - 1, 0), min(b, NCTX * (q + 1) + 1, CTX)
        if lo < hi:
            segs.append((lo, hi, CX0 + lo - (NCTX * q - 1)))
        for (lo, hi, e0) in segs:
            for pi, (pc0, pw) in enumerate(Y_PIECES):
                x0, x1 = max(e0, pc0), min(e0 + (hi - lo), pc0 + pw)
                if x0 < x1:
                    piece = Yf[Y_OFF[pi]:Y_OFF[pi] + D * pw].rearrange("(r w) -> r w", w=pw)
                    P.dma(piece[q * 256 + row0:q * 256 + row0 + nr, x0 - pc0:x1 - pc0], src_fn(lo + (x0 - e0), lo + (x1 - e0)),
                          reads=[key], writes=["EXY"], nowaw=True, allow_slow_non_contiguous=True)


def b_mla(P, E, l, C, after_first_gather=None):
    m0 = P.mark()
    QT = P.al([97, TALL], BF16)
    KT = P.al([97, TALL], BF16)
    Vt = P.al([128, NJ, 65], BF16)
    kst = P.al([96, TALL], F32)
    sq = P.al([96, 2112], BF16)
    kmx = P.al([128, 20], F32)
    nkm = P.al([128, 1], F32)
    tmpq = P.al([128, 512], F32)
    pT = [P.al([128, 512], BF16) for _ in range(4)]
    drow = P.al([65, 512], F32)
    rden = P.al([64, 512], F32)
    ost = [P.al([64, 512], F32) for _ in range(2)]
    ones_b, ones_f = C["ones_b"], C["ones_f"]
    B = C["banks"]
    scale = 96.0 ** -0.5
    P.v("pool", "memset", KT[96:97, :], 1.0, writes=["KT"])
    P.v("pool", "memset", kmx[:], 0.0, writes=["kmx"])
    ti = 0
    gather_fm(P, E, kst, "m_KT", 96, "mkst")
    if after_first_gather is not None:
        after_first_gather()
    for ci in range(4):
        c0 = ci * 2112
        sb_, sk = kst[:, c0:c0 + 2112], "mkst"
        P.v("dve", "tensor_copy", KT[0:96, c0:c0 + 2112], sb_[0:96, :], reads=[sk], writes=["KT"])
        P.actf(sq[:, :], sb_[0:96, :], AF.Square, reads=[sk], writes=["msq"])
        for (t0, n) in _col_tiles(0, 2112):
            ps, pk = B[6 + ti % 2], f"bank{6 + ti % 2}"
            P.mm(ps[:, :n], ones_b[0:96, :], sq[:, t0:t0 + n], reads=["ones", "msq"], writes=[pk])
            P.v("dve", "reduce_max", kmx[:, ti:ti + 1], ps[:, :n], AX.X, reads=[pk], writes=["kmx"])
            ti += 1
    P.v("dve", "reduce_max", nkm[:], kmx[:], AX.X, reads=["kmx"], writes=["nkm"])
    P.actf(nkm[:], nkm[:], AF.Sqrt, reads=["nkm"], writes=["nkm"])
    P.v("dve", "tensor_scalar_mul", nkm[:], nkm[:], -1.0, reads=["nkm"], writes=["nkm"])
    ti = 0
    gather_fm(P, E, kst, "m_QT", 96, "mkst")
    for ci in range(4):
        c0 = ci * 2112
        sb_, sk = kst[:, c0:c0 + 2112], "mkst"
        P.v("dve", "tensor_copy", QT[0:96, c0:c0 + 2112], sb_[0:96, :], reads=[sk], writes=["QT"])
        P.actf(sq[:, :], sb_[0:96, :], AF.Square, reads=[sk], writes=["msq"])
        for (t0, n) in _col_tiles(0, 2112):
            ps, pk = B[6 + ti % 2], f"bank{6 + ti % 2}"
            P.mm(ps[:, :n], ones_b[0:96, :], sq[:, t0:t0 + n], reads=["ones", "msq"], writes=[pk])
            P.actf(tmpq[96:97, :n], ps[96:97, :n], AF.Sqrt, reads=[pk], writes=["tmpq"])
            P.v("dve", "tensor_scalar_mul", QT[96:97, c0 + t0:c0 + t0 + n], tmpq[96:97, :n], nkm[96:97, 0:1],
                reads=["tmpq", "nkm"], writes=["QT"])
            ti += 1
    gather_fm(P, E, kst, "m_V", 64, "mkst")
    fm_to_tok(P, E, kst, 64, Vt, "mkst", "Vt")
    P.v("pool", "memset", Vt[:, :, 64:65], 1.0, writes=["Vt"])
    qtiles = [(0, 256, [0, 1])] + [(c0, n, list(range(NJ))) for (c0, n) in _col_tiles(256, TALL)]
    it = 0
    LOOK = 2
    for qi, (q0, n, kbs) in enumerate(qtiles):
        po, pok = B[4 + qi % 2], f"bank{4 + qi % 2}"
        slots = []

        def score(kb):
            nonlocal it
            ps, pk = B[it % 4], f"bank{it % 4}"
            pt, ptk = pT[it % 4], f"pT{it % 4}"
            it += 1
            P.mm(ps[:, :n], KT[:, kb * 128:(kb + 1) * 128], QT[:, q0:q0 + n], reads=["KT", "QT"], writes=[pk])
            P.actf(pt[:, :n], ps[:, :n], AF.Exp, scale=scale, reads=[pk], writes=[ptk])
            slots.append((pt, ptk))

        for ki in range(min(LOOK, len(kbs))):
            score(kbs[ki])
        for ki, kb in enumerate(kbs):
            if ki + LOOK < len(kbs):
                score(kbs[ki + LOOK])
            pt, ptk = slots[ki]
            P.mm(po[0:65, :n], Vt[:, kb, :], pt[:, :n], start=(ki == 0), stop=(ki == len(kbs) - 1),
                 reads=["Vt", ptk], writes=[pok])
        P.v("dve", "tensor_copy", drow[64:65, :n], po[64:65, :n], reads=[pok], writes=["drow"])
        pb, pbk = B[6 + qi % 2], f"bank{6 + qi % 2}"
        P.mm(pb[0:64, :n], ones_f[64:65, 0:64], drow[64:65, :n], reads=["onesf", "drow"], writes=[pbk])
        P.v("dve", "reciprocal", rden[:, :n], pb[0:64, :n], reads=[pbk], writes=["rden"])
        o, ok_ = ost[qi % 2], f"most{qi % 2}"
        P.v("dve", "tensor_tensor", o[:, :n], po[0:64, :n], rden[:, :n], ALU.mult, reads=[pok, "rden"], writes=[ok_])
        store_cols(P, E, 64, 64, (lambda a, b, o=o, q0=q0: o[:, a - q0:b - q0]), q0, n, ok_)
    P.release(m0)


def b_consts(P):
    C = {}
    C["banks"] = [P.ps([128, 512], F32, f"bank{i}") for i in range(8)]
    ones_b = P.sb([128, 128], BF16, "ones_b")
    ones_f = P.sb([128, 128], F32, "ones_f")
    ident = P.sb([128, 128], F32, "ident")
    triF = P.sb([128, 128], F32, "triF")
    triB = P.sb([128, 128], F32, "triB")
    mTF = P.sb([128, 64], F32, "mTF")
    mTB = P.sb([128, 64], F32, "mTB")
    mRF = P.sb([128, 512], F32, "mRF")
    mRB = P.sb([128, 512], F32, "mRB")
    P.v("pool", "memset", ones_b[:], 1.0, writes=["ones"])
    P.v("pool", "memset", ones_f[:], 1.0, writes=["onesf"])
    P.v("pool", "memset", ident[:], 0.0, writes=["ident"])
    P.add("pool", lambda e: e.affine_select(out=ident[:], in_=ident[:], pattern=[[-1, 128]], compare_op=ALU.not_equal,
                                            fill=1.0, base=0, channel_multiplier=1), reads=["ident"], writes=["ident"])
    for t, key, cm, st in ((triF, "triF", -1, 1), (triB, "triB", 1, -1)):
        P.v("pool", "memset", t[:], 0.0, writes=[key])
        for h in range(2):
            blk = t[64 * h:64 * h + 64, 64 * h:64 * h + 64]
            P.v("pool", "memset", blk, 1.0, reads=[key], writes=[key])
            P.add("pool", (lambda blk, cm, st: (lambda e: e.affine_select(out=blk, in_=blk, pattern=[[st, 64]],
                  compare_op=ALU.is_ge, fill=0.0, base=0, channel_multiplier=cm)))(blk, cm, st), reads=[key], writes=[key])
    for t, key, cm, st in ((mTF, "mTF", -1, 1), (mTB, "mTB", 1, -1)):
        P.v("pool", "memset", t[:], 0.0, writes=[key])
        for h in range(2):
            blk = t[64 * h:64 * h + 64, :]
            P.add("pool", (lambda blk, cm, st: (lambda e: e.affine_select(out=blk, in_=blk, pattern=[[st, 64]],
                  compare_op=ALU.is_ge, fill=-30000.0, base=0, channel_multiplier=cm)))(blk, cm, st), reads=[key], writes=[key])
    P.v("pool", "memset", mRF[:], 1.0, writes=["mRF"])
    P.v("pool", "memset", mRF[:, 0::64], 0.0, reads=["mRF"], writes=["mRF"])
    P.v("pool", "memset", mRB[:], 1.0, writes=["mRB"])
    P.v("pool", "memset", mRB[:, 63::64], 0.0, reads=["mRB"], writes=["mRB"])
    C.update(ones_b=ones_b, ones_f=ones_f, ident=ident, triF=triF, triB=triB, mTF=mTF, mTB=mTB, mRF=mRF, mRB=mRB)
    return C


def dla(P, C, dv1, qT, kT, ktok, get_vb, make_gates, finish_dir, tag, NUM, kNUM, accum=False):
    B = C["banks"]
    dk = 128
    brow = P.al([128, TALL], F32)
    lftok = P.al([128, NJ], F32)
    igtok = P.al([128, NJ], F32)
    negb = P.al([128, NJ], F32)
    wtok = P.al([128, NJ], F32)
    dec = P.al([128, NCH], F32)
    qTb = P.al([dk, TALL], BF16)
    Crun = [P.al([dk, dv1], F32) for _ in range(4)]
    Dt = [P.al([128, 64], F32) for _ in range(4)]
    Dm = [P.al([128, 64], F32) for _ in range(4)]
    pTt = [P.al([128, 64], BF16) for _ in range(4)]
    etmp = [P.al([128, 512], F32) for _ in range(2)]
    K = lambda s: f"{tag}_{s}"
    mdir = P.mark()
    for d in range(2):
        rev = d == 1
        tri = C["triB"] if rev else C["triF"]
        mT = C["mTB"] if rev else C["mTF"]
        mR = C["mRB"] if rev else C["mRF"]
        lfrow = P.al([128, TALL], F32)
        has_ig = make_gates(d, lfrow, lftok, igtok, K("lfrow"), K("lftok"), K("igtok"))
        vb, vbk = get_vb(d)
        for ti, (c0, n) in enumerate(_col_tiles(0, TALL)):
            o_, a_, b_ = brow[:, c0:c0 + n], mR[:, :n], lfrow[:, c0:c0 + n]
            if rev:
                o_, a_, b_ = o_[:, ::-1], a_[:, ::-1], b_[:, ::-1]
            P.v("dve", "tensor_tensor_scan", o_, a_, b_, 0.0, ALU.mult, ALU.add,
                reads=[K("lfrow"), "mR"], writes=[K("brow")])
            et, ek = etmp[ti % 2], K(f"etmp{ti % 2}")
            P.actf(et[:, :n], brow[:, c0:c0 + n], AF.Exp, reads=[K("brow")], writes=[ek])
            P.v("dve", "tensor_tensor", qTb[:, c0:c0 + n], qT[:, c0:c0 + n], et[:, :n], ALU.mult,
                reads=[K("qT"), ek], writes=[K("qTb")])
        P.release(mdir)
        kw = P.al([128, NJ, dk], BF16)
        Call = P.al([dk, NCH, dv1], BF16)
        P.actf(dec[:, :], brow[:, (0 if rev else 63)::64], AF.Exp, reads=[K("brow")], writes=[K("dec")])
        pm, pmk = B[6], "bank6"
        P.mm(pm[:, 0:NJ], tri[:, :], lftok[:, :], reads=["tri", K("lftok")], writes=[pmk])
        if has_ig:
            P.v("dve", "tensor_tensor", negb[:, :], igtok[:, :], pm[:, 0:NJ], ALU.subtract, reads=[pmk, K("igtok")], writes=[K("negb")])
        else:
            P.v("dve", "tensor_scalar_mul", negb[:, :], pm[:, 0:NJ], -1.0, reads=[pmk], writes=[K("negb")])
        for h in range(2):
            off = 64 * h + (0 if rev else 63)
            P.v("dve", "tensor_tensor", wtok[64 * h:64 * h + 64, :], negb[64 * h:64 * h + 64, :],
                brow[64 * h:64 * h + 64, off::128], ALU.add, reads=[K("negb"), K("brow")], writes=[K("wtok")])
        P.actf(wtok[:, :], wtok[:, :], AF.Exp, reads=[K("wtok")], writes=[K("wtok")])
        P.v("dve", "tensor_tensor", kw[:, :, :], ktok[:, :, :], wtok[:, :].unsqueeze(2).to_broadcast([128, NJ, dk]), ALU.mult,
            reads=[K("ktok"), K("wtok")], writes=[K("kw")])
        P.v("pool", "memset", Crun[0][:, :], 0.0, writes=[K("Crun0")])
        order = list(range(NCH)) if not rev else [3, 2, 1, 0] + list(range(NCH - 1, 3, -1))

        def slots(it, c):
            pb = 64 * (c % 2)
            sl = (it // 2) % 7
            s8 = (it // 2) % 8
            par = 3 * (c % 2)
            psU = B[par + 2][0:dk, 65 * sl:65 * sl + dv1]
            psN = B[par + 1][pb:pb + 64, 65 * sl:65 * sl + dv1]
            psS = B[par + 0][pb:pb + 64, 64 * s8:64 * s8 + 64]
            return pb, c // 2, slice(64 * c, 64 * c + 64), it % 4, psU, psN, psS, f"psU{par}_{sl}", f"psN{par}_{sl}", f"psS{par}_{s8}"

        def stage1(it, c):
            pb, j, cols, r4, psU, psN, psS, kU, kN, kS = slots(it, c)
            P.mm(psU, kw[pb:pb + 64, j, :], vb[pb:pb + 64, j, :], reads=[K("kw"), vbk], writes=[kU])
            P.mm(psS, kT[:, cols], qT[:, cols], reads=[K("kT"), K("qT")], writes=[kS])
            P.v("pool", "tensor_tensor", Dm[r4][pb:pb + 64, :], brow[pb:pb + 64, cols], mT[pb:pb + 64, :], ALU.add,
                reads=[K("brow"), "mT"], writes=[K(f"Dm{r4}")])
            P.actf(Dt[r4][pb:pb + 64, :], Dm[r4][pb:pb + 64, :], AF.Exp, bias=negb[pb:pb + 64, j:j + 1],
                   reads=[K(f"Dm{r4}"), K("negb")], writes=[K(f"Dt{r4}")])
            P.v("dve", "tensor_tensor", pTt[r4][pb:pb + 64, :], psS, Dt[r4][pb:pb + 64, :], ALU.mult,
                reads=[kS, K(f"Dt{r4}")], writes=[K(f"pTt{r4}")])

        def stage2(it, c):
            pb, j, cols, r4, psU, psN, psS, kU, kN, kS = slots(it, c)
            a, b = it % 4, (it + 1) % 4
            P.actf(Call[:, c, :], Crun[a][:, :], AF.Identity, reads=[K(f"Crun{a}")], writes=[K(f"Call{c % 16}")])
            P.v("dve", "scalar_tensor_tensor", Crun[b][:, :], Crun[a][:, :], dec[:, c:c + 1], psU, ALU.mult, ALU.add,
                reads=[K(f"Crun{a}"), K("dec"), kU], writes=[K(f"Crun{b}")])

        def stage3(it, c):
            pb, j, cols, r4, psU, psN, psS, kU, kN, kS = slots(it, c)
            P.mm(psN, pTt[r4][pb:pb + 64, :], vb[pb:pb + 64, j, :], start=True, stop=False,
                 reads=[K(f"pTt{r4}"), vbk], writes=[kN])
            P.mm(psN, qTb[:, cols], Call[:, c, :], start=False, stop=True,
                 reads=[K("qTb"), K(f"Call{c % 16}")], writes=[kN])
            if accum and d == 1:
                P.v("dve", "tensor_tensor", NUM[pb:pb + 64, j, :], NUM[pb:pb + 64, j, :], psN, ALU.add, reads=[kN, kNUM], writes=[kNUM])
            else:
                P.evac(NUM[pb:pb + 64, j, :], psN, reads=[kN], writes=[kNUM])

        LOOK = 2
        for it in range(min(LOOK, len(order))):
            stage1(it, order[it])
        for it, c in enumerate(order):
            if it + LOOK < len(order):
                stage1(it + LOOK, order[it + LOOK])
            stage2(it, c)
            stage3(it, c)
        finish_dir(d, NUM, kNUM)
        P.release(mdir)


def _softplus(P, ap, bias_ap, key, deps=()):
    P.actf(ap, ap, AF.Exp, bias=bias_ap, reads=[key] + list(deps), writes=[key])
    P.actf(ap, ap, AF.Ln, bias=1.0, reads=[key], writes=[key])


def b_mlstm(P, E, l, C):
    m0 = P.mark()
    qT = P.al([128, TALL], BF16)
    kT = P.al([128, TALL], BF16)
    ktok = P.al([128, NJ, 128], BF16)
    vb = P.al([128, NJ, 65], BF16)
    gtok = P.al([128, NJ, 4], F32)
    gb = P.al([128, 4], F32)
    ngb = P.al([128, 4], F32)
    H = P.al([128, NJ, 64], F32)
    rd = P.al([128, NJ], F32)
    m1 = P.mark()
    stg = P.al([128, TALL], F32)
    P.dma(gb[:, :], E["l_gbias"][l].partition_broadcast(128), writes=["l_gb"])
    P.v("dve", "tensor_scalar_mul", ngb[:, :], gb[:, :], -1.0, reads=["l_gb"], writes=["l_ngb"])
    for op, dst, key in (("l_q", qT, "l_qT"), ("l_k", kT, "l_kT")):
        P.v("pool", "memset", dst[64:128, :], 0.0, writes=[key])
        gather_fm(P, E, stg, op, 64, "l_stg")
        P.v("dve", "tensor_copy", dst[0:64, :], stg[0:64, :], reads=["l_stg"], writes=[key])
        if op == "l_k":
            P.v("pool", "memset", ktok[:, :, 64:128], 0.0, writes=["l_ktok"])
            fm_to_tok(P, E, stg, 64, ktok, "l_stg", "l_ktok")
    gather_fm(P, E, stg, "l_v", 64, "l_stg")
    fm_to_tok(P, E, stg, 64, vb, "l_stg", "l_vb")
    P.v("pool", "memset", vb[:, :, 64:65], 1.0, writes=["l_vb"])
    gather_fm(P, E, stg, "l_g", 4, "l_stg")
    fm_to_tok(P, E, stg, 4, gtok, "l_stg", "l_gtok")

    def make_gates(d, lfrow, lftok, igtok, klr, klt, kit):
        gather_fm(P, E, lfrow, f"l_grow{2 * d + 1}", 128, klr)
        P.actf(lfrow[:, :], lfrow[:, :], AF.Exp, bias=ngb[:, 2 * d + 1:2 * d + 2], scale=-1.0, reads=[klr, "l_ngb"], writes=[klr])
        P.actf(lfrow[:, :], lfrow[:, :], AF.Ln, bias=1.0, reads=[klr], writes=[klr])
        P.v("dve", "tensor_scalar_mul", lfrow[:, :], lfrow[:, :], -1.0, reads=[klr], writes=[klr])
        P.actf(lftok[:, :], gtok[:, :, 2 * d + 1], AF.Exp, bias=ngb[:, 2 * d + 1:2 * d + 2], scale=-1.0, reads=["l_gtok", "l_ngb"], writes=[klt])
        P.actf(lftok[:, :], lftok[:, :], AF.Ln, bias=1.0, reads=[klt], writes=[klt])
        P.v("dve", "tensor_scalar_mul", lftok[:, :], lftok[:, :], -1.0, reads=[klt], writes=[klt])
        P.v("dve", "tensor_scalar", igtok[:, :], gtok[:, :, 2 * d], gb[:, 2 * d:2 * d + 1], None, ALU.add, reads=["l_gtok", "l_gb"], writes=[kit])
        return True

    def finish_dir(d, NUM, kn):
        P.actf(rd[:, :], NUM[:, :, 64], AF.Abs, reads=[kn], writes=["l_rd"])
        P.v("dve", "tensor_scalar_max", rd[:, :], rd[:, :], 1.0, reads=["l_rd"], writes=["l_rd"])
        P.v("dve", "reciprocal", rd[:, :], rd[:, :], reads=["l_rd"], writes=["l_rd"])
        rb = rd[:, :].unsqueeze(2).to_broadcast([128, NJ, 64])
        if d == 0:
            P.v("dve", "tensor_tensor", H[:, :, :], NUM[:, :, 0:64], rb, ALU.mult, reads=[kn, "l_rd"], writes=["l_H"])
        else:
            P.v("dve", "tensor_tensor", NUM[:, :, 0:64], NUM[:, :, 0:64], rb, ALU.mult, reads=[kn, "l_rd"], writes=[kn])
            P.v("dve", "tensor_tensor", H[:, :, :], H[:, :, :], NUM[:, :, 0:64], ALU.add, reads=[kn, "l_H"], writes=["l_H"])

    P.release(m1)
    NUM = P.al([128, NJ, 65], F32)
    dla(P, C, 65, qT, kT, ktok, lambda d: (vb, "l_vb"), make_gates, finish_dir, "l", NUM, "l_NUM")
    P.release(m1)
    sqh = P.al([128, NJ, 64], F32)
    ss = P.al([128, NJ], F32)
    otok = P.al([128, NJ, 64], F32)
    gn = P.al([128, 64], F32)
    ostg = P.al([64, TALL], F32)
    P.dma(gn[:, :], E["l_norm"][l].partition_broadcast(128), writes=["l_gn"])
    gather_fm(P, E, ostg, "l_o", 64, "l_ostg")
    fm_to_tok(P, E, ostg, 64, otok, "l_ostg", "l_otok")
    P.v("dve", "tensor_tensor", sqh[:, :, :], H[:, :, :], H[:, :, :], ALU.mult, reads=["l_H"], writes=["l_sqh"])
    P.v("dve", "reduce_sum", ss[:, :], sqh[:, :, :], AX.X, reads=["l_sqh"], writes=["l_ss"])
    P.v("dve", "tensor_scalar", ss[:, :], ss[:, :], 1.0 / 64, EPS, ALU.mult, ALU.add, reads=["l_ss"], writes=["l_ss"])
    P.actf(ss[:, :], ss[:, :], AF.Sqrt, reads=["l_ss"], writes=["l_ss"])
    P.v("dve", "reciprocal", ss[:, :], ss[:, :], reads=["l_ss"], writes=["l_ss"])
    P.v("dve", "tensor_tensor", H[:, :, :], H[:, :, :], ss[:, :].unsqueeze(2).to_broadcast([128, NJ, 64]), ALU.mult,
        reads=["l_H", "l_ss"], writes=["l_H"])
    P.v("dve", "tensor_tensor", H[:, :, :], H[:, :, :], gn[:, :].unsqueeze(1).to_broadcast([128, NJ, 64]), ALU.mult,
        reads=["l_H", "l_gn"], writes=["l_H"])
    P.actf(otok[:, :, :], otok[:, :, :], AF.Sigmoid, reads=["l_otok"], writes=["l_otok"])
    P.v("dve", "tensor_tensor", H[:, :, :], H[:, :, :], otok[:, :, :], ALU.mult, reads=["l_H", "l_otok"], writes=["l_H"])
    tok_to_fm(P, E, H, ostg, "l_H", "l_ostg")
    store_cols(P, E, 0, 64, (lambda a, b: ostg[0:64, a:b]), 0, TALL, "l_ostg")
    P.release(m0)


def b_ssd(P, E, l, C):
    B = C["banks"]
    ident = C["ident"]
    m0 = P.mark()
    qT = P.al([128, TALL], BF16)
    kT = P.al([128, TALL], BF16)
    ktok = P.al([128, NJ, 128], BF16)
    vtok = P.al([128, NJ, 64], F32)
    vbd = P.al([128, NJ, 64], BF16)
    dttok = P.al([128, NJ, 2], F32)
    Y = P.al([128, NJ, 64], F32)
    cw = P.al([128, 3, 4], F32)
    dtb = P.al([128, 2], F32)
    Ad = P.al([128, 2], F32)
    dsk = P.al([128, 1], F32)
    P.dma(cw[:, :, :], E["s_convw"][l], writes=["s_cw"])
    P.dma(dtb[:, :], E["s_dtbias"][l].partition_broadcast(128), writes=["s_dtb"])
    P.dma(Ad[:, :], E["s_alog"][l].partition_broadcast(128), writes=["s_Ad"])
    P.actf(Ad[:, :], Ad[:, :], AF.Exp, reads=["s_Ad"], writes=["s_Ad"])
    P.v("dve", "tensor_scalar_mul", Ad[:, :], Ad[:, :], -1.0, reads=["s_Ad"], writes=["s_Ad"])
    P.dma(dsk[:, :], E["s_dskip"][l].partition_broadcast(128), writes=["s_dsk"])
    m1 = P.mark()
    raw = P.al([128, TALL], F32)
    acc = P.al([128, TALL], F32)
    gather_fm(P, E, raw, "s_dt", 2, "s_raw")
    fm_to_tok(P, E, raw, 2, dttok, "s_raw", "s_dttok")
    for d in range(2):
        _softplus(P, dttok[:, :, d], dtb[:, d:d + 1], "s_dttok", deps=["s_dtb"])
    for blk, (r0, nr) in enumerate(((0, 64), (64, 128), (192, 128))):
        gather_fm(P, E, raw, ("s_x", "s_B", "s_C")[blk], nr, "s_raw")
        P.actf(acc[0:nr, :], raw[0:nr, :], AF.Identity, bias=cw[0:nr, blk, 3:4], scale=cw[0:nr, blk, 1:2],
               reads=["s_raw", "s_cw"], writes=["s_acc"])
        for (a, b) in ((0, CTX), (CTX, TALL)):
            P.v("dve", "scalar_tensor_tensor", acc[0:nr, a + 1:b], raw[0:nr, a:b - 1], cw[0:nr, blk, 0:1], acc[0:nr, a + 1:b],
                ALU.mult, ALU.add, reads=["s_raw", "s_cw", "s_acc"], writes=["s_acc"])
            P.v("dve", "scalar_tensor_tensor", acc[0:nr, a:b - 1], raw[0:nr, a + 1:b], cw[0:nr, blk, 2:3], acc[0:nr, a:b - 1],
                ALU.mult, ALU.add, reads=["s_raw", "s_cw", "s_acc"], writes=["s_acc"])
        P.actf(acc[0:nr, :], acc[0:nr, :], AF.Silu, reads=["s_acc"], writes=["s_acc"])
        if blk == 2:
            P.v("dve", "tensor_copy", qT[:, :], acc[:, :], reads=["s_acc"], writes=["s_qT"])
            continue
        if blk == 1:
            P.v("dve", "tensor_copy", kT[:, :], acc[:, :], reads=["s_acc"], writes=["s_kT"])
        for g in range(0, NJ, 4):
            ps, pk = B[6 + (g // 4) % 2], f"bank{6 + (g // 4) % 2}"
            nj = min(4, NJ - g)
            for jj in range(nj):
                j = g + jj
                P.tr(ps[:, 128 * jj:128 * jj + nr], acc[0:nr, 128 * j:128 * j + 128], ident[0:nr, 0:nr],
                     reads=["s_acc", "ident"], writes=[pk])
            src = ps[:, 0:128 * nj].rearrange("p (j e) -> p j e", e=128)[:, :, 0:nr]
            if blk == 0:
                P.evac(vtok[:, g:g + nj, :], src, reads=[pk], writes=["s_vtok"])
            else:
                P.evac(ktok[:, g:g + nj, :], src, reads=[pk], writes=["s_ktok"])
    P.release(m1)

    def make_gates(d, lfrow, lftok, igtok, klr, klt, kit):
        gather_fm(P, E, lfrow, f"s_dtrow{d}", 128, klr)
        _softplus(P, lfrow[:, :], dtb[:, d:d + 1], klr, deps=["s_dtb"])
        P.actf(lfrow[:, :], lfrow[:, :], AF.Identity, scale=Ad[:, d:d + 1], reads=[klr, "s_Ad"], writes=[klr])
        P.v("dve", "tensor_scalar", lftok[:, :], dttok[:, :, d], Ad[:, d:d + 1], None, ALU.mult, reads=["s_dttok", "s_Ad"], writes=[klt])
        return False

    def get_vb(d):
        P.v("dve", "tensor_tensor", vbd[:, :, :], vtok[:, :, :], dttok[:, :, d:d + 1].to_broadcast([128, NJ, 64]), ALU.mult,
            reads=["s_vtok", "s_dttok"], writes=["s_vbd"])
        return vbd, "s_vbd"

    def finish_dir(d, NUM, kn):
        pass

    dla(P, C, 64, qT, kT, ktok, get_vb, make_gates, finish_dir, "s", Y, "s_Y", accum=True)
    P.release(m1)
    ztok = P.al([128, NJ, 64], F32)
    zst = P.al([64, TALL], F32)
    gather_fm(P, E, zst, "s_z", 64, "s_zst")
    fm_to_tok(P, E, zst, 64, ztok, "s_zst", "s_ztok")
    P.v("dve", "scalar_tensor_tensor", Y[:, :, :], vtok[:, :, :], dsk[:, 0:1], Y[:, :, :], ALU.mult, ALU.add,
        reads=["s_vtok", "s_dsk", "s_Y"], writes=["s_Y"])
    P.actf(ztok[:, :, :], ztok[:, :, :], AF.Silu, reads=["s_ztok"], writes=["s_ztok"])
    P.v("dve", "tensor_tensor", Y[:, :, :], Y[:, :, :], ztok[:, :, :], ALU.mult, reads=["s_Y", "s_ztok"], writes=["s_Y"])
    tok_to_fm(P, E, Y, zst, "s_Y", "s_zst")
    store_cols(P, E, 128, 64, (lambda a, b: zst[0:64, a:b]), 0, TALL, "s_zst")
    P.release(m0)


MAGIC = 12582912.0
TWO_PI = 6.283185307179586


def _sin_rr(P, out, in_, tmp, key_in, key_out, key_tmp, shift=0.0):
    P.v("dve", "tensor_scalar", tmp, in_, 1.0 / TWO_PI, shift / TWO_PI + MAGIC, ALU.mult, ALU.add, reads=[key_in], writes=[key_tmp])
    P.v("dve", "tensor_scalar", tmp, tmp, MAGIC, -TWO_PI, ALU.subtract, ALU.mult, reads=[key_tmp], writes=[key_tmp])
    P.v("dve", "scalar_tensor_tensor", tmp, in_, 1.0, tmp, ALU.mult, ALU.add, reads=[key_in, key_tmp], writes=[key_tmp])
    P.actf(out, tmp, AF.Sin, bias=shift, reads=[key_tmp], writes=[key_out])


def b_s5(P, E, l, C, after_first_gather=None):
    B = C["banks"]
    ident = C["ident"]
    m0 = P.mark()
    pr = {k: P.al([128, 4], F32) for k in ("are", "aim", "ldt", "dt", "lr", "mag", "th", "sn", "cs", "abr", "abi",
                                            "den", "fre", "fim", "t1", "t2", "t3")}
    bre = P.al([128, 2, 32], F32)
    bim = P.al([128, 2, 32], F32)
    cre = P.al([128, 2, 32], F32)
    cim = P.al([128, 2, 32], F32)
    cTr = P.al([128, 2, 32], BF16)
    cTi = P.al([128, 2, 32], BF16)
    bbT = P.al([32, 8, 128], BF16)
    bbtmp = P.al([128, 2, 32], F32)
    dsk = P.al([32, 2], F32)
    pw = P.al([128, 15, 2], F32)
    npw = P.al([128, 15], F32)
    for k, nm in (("are", "s5_are"), ("aim", "s5_aim"), ("ldt", "s5_ldt")):
        P.dma(pr[k][:, :], E[nm][l], writes=["5p_" + k])
    for t, nm in ((bre, "s5_bre"), (bim, "s5_bim"), (cre, "s5_cre"), (cim, "s5_cim")):
        P.dma(t[:, :, :], E[nm][l], writes=["5p_" + nm])
    P.dma(dsk[:, :], E["s5_d"][l], writes=["5p_dsk"])
    P.v("dve", "tensor_copy", cTr[:, :, :], cre[:, :, :], reads=["5p_s5_cre"], writes=["5p_cT"])
    P.v("dve", "tensor_scalar_mul", cTi[:, :, :], cim[:, :, :], -1.0, reads=["5p_s5_cim"], writes=["5p_cT"])
    kk = "5p_small"
    a = lambda k: pr[k][:, :]
    P.actf(a("dt"), a("ldt"), AF.Exp, reads=["5p_ldt"], writes=[kk])
    P.v("dve", "tensor_scalar_min", a("lr"), a("are"), -1e-4, reads=["5p_are"], writes=[kk])
    P.v("dve", "tensor_tensor", a("t1"), a("lr"), a("dt"), ALU.mult, reads=[kk], writes=[kk])
    P.actf(a("mag"), a("t1"), AF.Exp, reads=[kk], writes=[kk])
    P.v("dve", "tensor_tensor", a("th"), a("aim"), a("dt"), ALU.mult, reads=[kk, "5p_aim"], writes=[kk])
    _sin_rr(P, a("sn"), a("th"), a("t2"), kk, kk, kk)
    P.v("dve", "tensor_scalar_add", a("t3"), a("th"), TWO_PI / 4, reads=[kk], writes=[kk])
    _sin_rr(P, a("cs"), a("t3"), a("t2"), kk, kk, kk)
    P.v("dve", "tensor_tensor", a("abr"), a("mag"), a("cs"), ALU.mult, reads=[kk], writes=[kk])
    P.v("dve", "tensor_tensor", a("abi"), a("mag"), a("sn"), ALU.mult, reads=[kk], writes=[kk])
    P.v("dve", "tensor_tensor", a("den"), a("lr"), a("lr"), ALU.mult, reads=[kk], writes=[kk])
    P.v("dve", "tensor_tensor", a("t1"), a("aim"), a("aim"), ALU.mult, reads=[kk, "5p_aim"], writes=[kk])
    P.v("dve", "tensor_tensor", a("den"), a("den"), a("t1"), ALU.add, reads=[kk], writes=[kk])
    P.v("dve", "reciprocal", a("den"), a("den"), reads=[kk], writes=[kk])
    P.v("dve", "tensor_scalar_add", a("t1"), a("abr"), -1.0, reads=[kk], writes=[kk])
    P.v("dve", "tensor_tensor", a("t2"), a("t1"), a("lr"), ALU.mult, reads=[kk], writes=[kk])
    P.v("dve", "tensor_tensor", a("t3"), a("abi"), a("aim"), ALU.mult, reads=[kk, "5p_aim"], writes=[kk])
    P.v("dve", "tensor_tensor", a("t2"), a("t2"), a("t3"), ALU.add, reads=[kk], writes=[kk])
    P.v("dve", "tensor_tensor", a("fre"), a("t2"), a("den"), ALU.mult, reads=[kk], writes=[kk])
    P.v("dve", "tensor_tensor", a("t2"), a("abi"), a("lr"), ALU.mult, reads=[kk], writes=[kk])
    P.v("dve", "tensor_tensor", a("t3"), a("t1"), a("aim"), ALU.mult, reads=[kk, "5p_aim"], writes=[kk])
    P.v("dve", "tensor_tensor", a("t2"), a("t2"), a("t3"), ALU.subtract, reads=[kk], writes=[kk])
    P.v("dve", "tensor_tensor", a("fim"), a("t2"), a("den"), ALU.mult, reads=[kk], writes=[kk])
    P.v("dve", "tensor_scalar_mul", a("t1"), a("fim"), -1.0, reads=[kk], writes=[kk])
    for gp in range(2):
        for d in range(2):
            ix = gp * 2 + d
            for comp in range(2):
                if comp == 0:
                    P.v("dve", "tensor_scalar", bbtmp[:, 0, :], bre[:, gp, :], pr["fre"][:, ix:ix + 1], None, ALU.mult,
                        reads=[kk, "5p_s5_bre"], writes=["5p_bbtmp"])
                    P.v("dve", "scalar_tensor_tensor", bbtmp[:, 0, :], bim[:, gp, :], pr["t1"][:, ix:ix + 1], bbtmp[:, 0, :],
                        ALU.mult, ALU.add, reads=[kk, "5p_s5_bim", "5p_bbtmp"], writes=["5p_bbtmp"])
                else:
                    P.v("dve", "tensor_scalar", bbtmp[:, 0, :], bim[:, gp, :], pr["fre"][:, ix:ix + 1], None, ALU.mult,
                        reads=[kk, "5p_s5_bim"], writes=["5p_bbtmp"])
                    P.v("dve", "scalar_tensor_tensor", bbtmp[:, 0, :], bre[:, gp, :], pr["fim"][:, ix:ix + 1], bbtmp[:, 0, :],
                        ALU.mult, ALU.add, reads=[kk, "5p_s5_bre", "5p_bbtmp"], writes=["5p_bbtmp"])
                P.tr(B[6][0:32, 0:128], bbtmp[:, 0, :], ident[:, :], reads=["5p_bbtmp", "ident"], writes=["bank6"])
                P.v("dve", "tensor_copy", bbT[:, gp * 4 + d * 2 + comp, :], B[6][0:32, 0:128], reads=["bank6"], writes=["5p_bbT"])
    Er = P.al([128, TALL], F32)
    Ei = P.al([128, TALL], F32)
    DEC = P.al([128, 512], F32)
    ust = P.al([32, TALL], F32)
    ub = P.al([32, TALL], BF16)
    Y = P.al([32, TALL], F32)
    tt = {k: [P.al([128, 512], F32) for _ in range(2)] for k in ("m1", "m2", "m3", "m4", "vr", "vi", "zr", "zi")}
    for k in ("n1", "n2"):
        nbuf = P.al([128, 512], F32)
        tt[k] = [nbuf, nbuf]
    xr = [P.al([128, 512], BF16) for _ in range(2)]
    xi = [P.al([128, 512], BF16) for _ in range(2)]
    for gp in range(2):
        gather_fm(P, E, ust, f"s5_u{gp}", 32, "5_ust")
        if gp == 0 and after_first_gather is not None:
            after_first_gather()
        P.cast(ub[:, :], ust[:, :], reads=["5_ust"], writes=["5_ub"])
        for d in range(2):
            ix = gp * 2 + d
            rev = d == 1
            P.v("dve", "tensor_copy", pw[:, 0, 0:1], pr["cs"][:, ix:ix + 1], reads=[kk], writes=["5_pw"])
            P.v("dve", "tensor_copy", pw[:, 0, 1:2], pr["sn"][:, ix:ix + 1], reads=[kk], writes=["5_pw"])
            for k in range(14):
                c_, s_ = pw[:, k, 0:1], pw[:, k, 1:2]
                P.v("dve", "tensor_tensor", pr["t2"][:, 0:1], c_, c_, ALU.mult, reads=["5_pw"], writes=["5_t"])
                P.v("dve", "tensor_tensor", pr["t2"][:, 1:2], s_, s_, ALU.mult, reads=["5_pw"], writes=["5_t"])
                P.v("dve", "tensor_tensor", pw[:, k + 1, 0:1], pr["t2"][:, 0:1], pr["t2"][:, 1:2], ALU.subtract, reads=["5_t"], writes=["5_pw"])
                P.v("dve", "scalar_tensor_tensor", pw[:, k + 1, 1:2], c_, 2.0, s_, ALU.mult, ALU.mult, reads=["5_pw"], writes=["5_pw"])
            P.v("dve", "tensor_scalar_mul", npw[:, :], pw[:, :, 1], -1.0, reads=["5_pw"], writes=["5_npw"])
            P.v("pool", "memset", Er[:, 0:1], 1.0, writes=["5_E"])
            P.v("pool", "memset", Ei[:, 0:1], 0.0, writes=["5_E"])
            for k in range(14):
                L = 1 << k
                n = min(L, TALL - L)
                c_, s_, ns_ = pw[:, k, 0:1], pw[:, k, 1:2], npw[:, k:k + 1]
                P.v("dve", "tensor_scalar", Er[:, L:L + n], Er[:, 0:n], c_, None, ALU.mult, reads=["5_E", "5_pw"], writes=["5_E"])
                P.v("dve", "scalar_tensor_tensor", Er[:, L:L + n], Ei[:, 0:n], ns_, Er[:, L:L + n], ALU.mult, ALU.add,
                    reads=["5_E", "5_npw"], writes=["5_E"])
                P.v("dve", "tensor_scalar", Ei[:, L:L + n], Ei[:, 0:n], c_, None, ALU.mult, reads=["5_E", "5_pw"], writes=["5_E"])
                P.v("dve", "scalar_tensor_tensor", Ei[:, L:L + n], Er[:, 0:n], s_, Ei[:, L:L + n], ALU.mult, ALU.add,
                    reads=["5_E", "5_pw"], writes=["5_E"])
            P.v("pool", "memset", DEC[:, :], 1.0, writes=["5_DEC"])
            P.v("dve", "tensor_scalar", DEC[:, :], DEC[:, :], pr["mag"][:, ix:ix + 1], None, ALU.mult, reads=["5_DEC", kk], writes=["5_DEC"])
            lat = _col_tiles(CTX, TALL)
            tiles = [(0, CTX)] + (lat if not rev else lat[::-1])
            prev = [None]

            def views(ti):
                c0, n = tiles[ti]
                r = ti % 2
                if not rev:
                    EV = lambda E_, c0=c0, n=n: E_[:, c0:c0 + n]
                else:
                    seg_hi = (CTX - 1) if c0 < CTX else (TALL - 1 + CTX)
                    lo, hi = seg_hi - (c0 + n - 1), seg_hi - c0 + 1
                    EV = lambda E_, lo=lo, hi=hi: E_[:, lo:hi][:, ::-1]
                T_ = lambda k, r=r, n=n: tt[k][r][:, :n]
                Kk = lambda k, r=r: (f"5_{k}" if k in ("n1", "n2") else f"5_{k}{r}")
                return c0, n, r, slice(c0, c0 + n), EV, T_, Kk, B[r], B[2 + r], f"bank{r}", f"bank{2 + r}"

            def pre(ti):
                c0, n, r, cols, EV, T_, Kk, pR, pI, kR, kI = views(ti)
                P.mm(pR[:, :n], bbT[:, gp * 4 + d * 2 + 0, :], ub[:, cols], reads=["5p_bbT", "5_ub"], writes=[kR])
                P.mm(pI[:, :n], bbT[:, gp * 4 + d * 2 + 1, :], ub[:, cols], reads=["5p_bbT", "5_ub"], writes=[kI])
                P.v("dve", "tensor_tensor", T_("m1"), pR[:, :n], EV(Er), ALU.mult, reads=[kR, "5_E"], writes=[Kk("m1")])
                P.v("dve", "tensor_tensor", T_("m2"), pI[:, :n], EV(Ei), ALU.mult, reads=[kI, "5_E"], writes=[Kk("m2")])
                P.v("pool", "tensor_tensor", T_("vr"), T_("m1"), T_("m2"), ALU.add, reads=[Kk("m1"), Kk("m2")], writes=[Kk("vr")])
                P.v("dve", "tensor_tensor", T_("m3"), pI[:, :n], EV(Er), ALU.mult, reads=[kI, "5_E"], writes=[Kk("m3")])
                P.v("dve", "tensor_tensor", T_("m4"), pR[:, :n], EV(Ei), ALU.mult, reads=[kR, "5_E"], writes=[Kk("m4")])
                P.v("pool", "tensor_tensor", T_("vi"), T_("m3"), T_("m4"), ALU.subtract, reads=[Kk("m3"), Kk("m4")], writes=[Kk("vi")])

            def scanpost(ti):
                c0, n, r, cols, EV, T_, Kk, pR, pI, kR, kI = views(ti)
                for comp, vk in (("zr", "vr"), ("zi", "vi")):
                    o_, d1 = T_(comp), T_(vk)
                    if rev:
                        o_, d1 = o_[:, ::-1], d1[:, ::-1]
                    if prev[0] is None:
                        init, ik = 0.0, []
                    else:
                        pr_r, pn = prev[0]
                        init = tt[comp][pr_r][:, 0:1] if rev else tt[comp][pr_r][:, pn - 1:pn]
                        ik = [f"5_{comp}{pr_r}"]
                    P.v("dve", "tensor_tensor_scan", o_, DEC[:, :n], d1, init, ALU.mult, ALU.add,
                        reads=["5_DEC", Kk(vk)] + ik, writes=[Kk(comp)])
                prev[0] = (r, n)
                P.v("dve", "tensor_tensor", T_("n1"), T_("zr"), EV(Er), ALU.mult, reads=[Kk("zr"), "5_E"], writes=[Kk("n1")])
                P.v("pool", "tensor_tensor", T_("n2"), T_("zi"), EV(Ei), ALU.mult, reads=[Kk("zi"), "5_E"], writes=[Kk("n2")])
                P.v("dve", "tensor_tensor", xr[r][:, :n], T_("n1"), T_("n2"), ALU.subtract, reads=[Kk("n1"), Kk("n2")], writes=[f"5_xr{r}"])
                P.v("dve", "tensor_tensor", T_("n1"), T_("zr"), EV(Ei), ALU.mult, reads=[Kk("zr"), "5_E", f"5_xr{r}"], writes=[Kk("n1")])
                P.v("dve", "tensor_tensor", T_("n2"), T_("zi"), EV(Er), ALU.mult, reads=[Kk("zi"), "5_E", f"5_xr{r}"], writes=[Kk("n2")])
                P.v("dve", "tensor_tensor", xi[r][:, :n], T_("n1"), T_("n2"), ALU.add, reads=[Kk("n1"), Kk("n2")], writes=[f"5_xi{r}"])
                pY, kY = B[4 + r], f"bank{4 + r}"
                P.mm(pY[0:32, :n], cTr[:, gp, :], xr[r][:, :n], start=True, stop=False, reads=["5p_cT", f"5_xr{r}"], writes=[kY])
                P.mm(pY[0:32, :n], cTi[:, gp, :], xi[r][:, :n], start=False, stop=True, reads=["5p_cT", f"5_xi{r}"], writes=[kY])
                if d == 0:
                    P.v("dve", "scalar_tensor_tensor", Y[:, cols], ust[:, cols], dsk[:, gp:gp + 1], pY[0:32, :n], ALU.mult, ALU.add,
                        reads=["5_ust", "5p_dsk", kY], writes=["5_Y"])
                else:
                    P.v("dve", "tensor_tensor", Y[:, cols], Y[:, cols], pY[0:32, :n], ALU.add, reads=["5_Y", kY], writes=["5_Y"])

            pre(0)
            for ti in range(len(tiles)):
                if ti + 1 < len(tiles):
                    pre(ti + 1)
                scanpost(ti)
        store_cols(P, E, 192 + 32 * gp, 32, (lambda a, b: Y[0:32, a:b]), 0, TALL, "5_Y")
    P.release(m0)


def phase_C(P, E, l, final=False):
    banks = E["banks"]
    ones_b, cs = E["ones_b"], E["cs"]
    wmod, bmod, norm2 = E["wmodC"][l], E["bmodC"][l], E["norm2"][l]
    snorm, wglu, wout, wup, convw, wdown = E["snorm"][l], E["wglu"][l], E["wout"][l], E["wup"][l], E["convw"][l], E["wdown"][l]
    XS, XSo, GXB, xTo = E["XS"][l % 2], E["XS"][(l + 1) % 2], E["GXB"], E["xTo"]
    itC, itX = E["idxC"], E["idxX"]
    m_top0 = P.mark()
    bmod_sb = P.al([128, 32], F32)
    P.dma(bmod_sb[:], bmod, writes=["bmod"])
    n2 = P.al([128, KC], F32)
    P.dma(n2[:], norm2, writes=["n2"])
    fn = P.al([128, KC], F32)
    P.dma(fn[:], E["fnorm"], writes=["fn"])
    sn = P.al([128, 2], F32)
    P.dma(sn[:], snorm, writes=["sn"])
    hm = P.al([128, 4], F32)
    P.dma(hm[:], E["hmask"], writes=["hm"])
    cwf = P.al([128, NFF, 3], F32)
    P.dma(cwf[:], convw, writes=["cwf"])
    modv = P.al([128, 4, KC, 2], F32)
    A2 = P.al([128, KC, 2], F32)
    nb = P.al([128, KC, 2, 4], F32)
    P.v("pool", "memset", nb[:], 0.0, writes=["nb"])
    for k in range(KC):
        for side in range(2):
            c = k * 2 + side
            P.add("pool", (lambda k, side, c: (lambda e: e.indirect_dma_start(
                out=nb[:, k, side, :], out_offset=None, in_=GXB[:, :],
                in_offset=bass.IndirectOffsetOnAxis(ap=itX[:, c:c + 1], axis=0))))(k, side, c), reads=["GXB", "idxX", "nb"], writes=["nb"], dma=True)
    m_top = P.mark()
    wst = [P.al([128, 1024], F32) for _ in range(2)]
    _mod_vectors(P, wmod, bmod_sb, cs, [0, 1, 2, 3], wst, ["wst0", "wst1"], banks[7], "bank7", modv)
    P.v("dve", "tensor_scalar_add", A2[:], modv[:, 2, :, :], 1.0, reads=["modv"], writes=["A2"])
    for s_ in range(2):
        P.v("dve", "tensor_tensor", A2[:, :, s_], A2[:, :, s_], n2[:], ALU.mult, reads=["A2", "n2"], writes=["A2"])
    P.release(m_top)

    g0 = [(0, 0, 1026, 1, 1025, (0, None))]
    g1 = [(0, 1024, 2050, 1025, 2049, (None, 1))]
    if not final:
        g1.append((1, CX0, CX1, CX0 + 1, CX1 - 1, (2, 3)))
    evi = [0]
    for gi, segs in enumerate((g0, g1)):
        m_g = P.mark()
        W = sum(e1 - e0 for (_, e0, e1, _, _, _) in segs)
        WO = sum(o1 - o0 for (_, _, _, o0, o1, _) in segs)
        xres = P.al([128, KC, W], F32)
        h2 = P.al([128, KC, W], BF16)
        aT = P.al([128, NFF, WO], BF16)
        loc = []
        off = 0
        ooff = 0
        for (kind, e0, e1, o0, o1, hf) in segs:
            loc.append((off, ooff))
            off += e1 - e0
            ooff += o1 - o0
        kx = f"xres{gi}"
        m_s1 = P.mark()
        wout_b = P.al([128, KC, D], BF16)
        wglu_b = P.al([128, 2, 256], BF16)
        wstg = [P.al([128, D], F32) for _ in range(2)]
        for k in range(KC):
            P.dma(wstg[k % 2][:, :], wout[k * 128:(k + 1) * 128, :], writes=[f"wstg{k % 2}"])
            P.cast(wout_b[:, k, :], wstg[k % 2][:, :], reads=[f"wstg{k % 2}"], writes=["woutb"])
        for k in range(2):
            P.dma(wstg[k][:, 0:256], wglu[k * 128:(k + 1) * 128, :], writes=[f"wstg{k}"])
            P.v("pool", "tensor_copy", wglu_b[:, k, :], wstg[k][:, 0:256], reads=[f"wstg{k}"], writes=["wglub"])
        yst = [P.al([128, KC, 512], F32) for _ in range(2)]
        ycb = P.al([128, KC, 512], BF16)
        sqb = P.al([128, KC, 512], BF16)
        gel = P.al([128, 2, 512], F32)
        gelb = P.al([128, 2, 512], BF16)
        sig = P.al([128, 512], F32)
        rstd = P.al([128, 512], F32)
        tmpn = P.al([128, 512], F32)
        tmpx = [P.al([128, 512], F32) for _ in range(2)]
        ti = 0
        for si, (kind, e0, e1, o0, o1, hf) in enumerate(segs):
            lo = loc[si][0]
            halo = {LAT0: (0, 1), LAT1 - 1: (1, 0), CX0: (0, 3), CX1 - 1: (1, 2)}
            oa = e0 + 1 if e0 in halo else e0
            ob = e1 - 1 if (e1 - 1) in halo else e1
            xs0 = (oa - 1) if kind == 0 else (NLAT + oa - CX0 - 1)
            P.dma(xres[:, :, lo + (oa - e0):lo + (ob - e0)], XS[:, xs0:xs0 + (ob - oa)].rearrange("(k p) n -> p k n", p=128),
                  reads=[f"XS{l % 2}"], writes=[kx])
            for ecol in (e0, e1 - 1):
                if ecol in halo:
                    side, bc = halo[ecol]
                    P.v("dve", "tensor_copy", xres[:, :, lo + (ecol - e0)], nb[:, :, side, bc], reads=["nb", kx], writes=[kx])
            for (c0, n) in _col_tiles(e0, e1):
                l0 = lo + (c0 - e0)
                ys, yk = yst[ti % 2], f"yst{ti % 2}"
                ti += 1
                pi = Y_PIECES.index((c0, n))
                srcp = _flat(E["GEXY"]).rearrange("(r w) -> r w", w=n)
                for k in range(KC):
                    cix = pi * KC + k
                    P.add("pool", (lambda k, ys, n, srcp, cix, eoff: (lambda e: e.indirect_dma_start(
                        out=ys[:, k, :n], out_offset=None, in_=srcp,
                        in_offset=bass.IndirectOffsetOnAxis(ap=itC[:, cix:cix + 1], axis=0), element_offset=eoff)))(k, ys, n, srcp, cix, 0), reads=["GEXY", "idxC"], writes=[yk], dma=True, nowaw=True)
                P.cast(ycb[:, 0:4, :n], ys[:, 0:4, :n], reads=[yk], writes=["ycb"])
                P.actf(sqb[:, 0:2, :n], ys[:, 4:6, :n], AF.Square, reads=[yk], writes=["sqb"])
                pss = banks[7]
                for j in range(2):
                    P.mm(pss[:, :n], ones_b[:], sqb[:, j, :n], start=(j == 0), stop=(j == 1), reads=["ones", "sqb"], writes=["bank7"])
                _rms_rstd(P, pss, n, 256, rstd, "bank7", "rstd", tmpn, "tmpn")
                for j in range(2):
                    P.v("dve", "tensor_tensor", tmpx[j][:, :n], ys[:, 4 + j, :n], rstd[:, :n], ALU.mult, reads=[yk, "rstd"], writes=[f"tmpx{j}"])
                    P.actf(ycb[:, 4 + j, :n], tmpx[j][:, :n], AF.Identity, scale=sn[:, j:j + 1], reads=[f"tmpx{j}", "sn"], writes=["ycb"])
                P.actf(gel[:, :, :n], ys[:, 6:8, :n], AF.Gelu, reads=[yk], writes=["gel"])
                P.v("dve", "tensor_copy", gelb[:, :, :n], gel[:, :, :n], reads=["gel"], writes=["gelb"])
                for mc in range(2):
                    pg_, pgk = banks[6], "bank6"
                    for j in range(2):
                        P.mm(pg_[:, :n], wglu_b[:, j, mc * 128:(mc + 1) * 128], gelb[:, j, :n], start=(j == 0), stop=(j == 1),
                             reads=["wglub", "gelb"], writes=[pgk])
                    P.actf(sig[:, :n], pg_[:, :n], AF.Sigmoid, reads=[pgk], writes=["sig"])
                    P.v("dve", "tensor_tensor", ycb[:, 6 + mc, :n], gel[:, mc, :n], sig[:, :n], ALU.mult, reads=["gel", "sig"], writes=["ycb"])
                for dc in range(KC):
                    po, pok = banks[dc % 4], f"bank{dc % 4}"
                    for k in range(KC):
                        P.mm(po[:, :n], wout_b[:, k, dc * 128:(dc + 1) * 128], ycb[:, k, :n], start=(k == 0), stop=(k == KC - 1),
                             reads=["woutb", "ycb"], writes=[pok])
                    P.v("dve", "scalar_tensor_tensor", xres[:, dc, l0:l0 + n], po[:, :n], modv[:, 0, dc, kind:kind + 1],
                        xres[:, dc, l0:l0 + n], ALU.mult, ALU.add, reads=[pok, "modv", kx], writes=[kx])
                P.actf(sqb[:, :, :n], xres[:, :, l0:l0 + n], AF.Square, reads=[kx], writes=["sqb"])
                for k in range(KC):
                    P.mm(pss[:, :n], ones_b[:], sqb[:, k, :n], start=(k == 0), stop=(k == KC - 1), reads=["ones", "sqb"], writes=["bank7"])
                _rms_rstd(P, pss, n, D, rstd, "bank7", "rstd", tmpn, "tmpn")
                for k in range(KC):
                    P.v("dve", "tensor_tensor", tmpx[k % 2][:, :n], xres[:, k, l0:l0 + n], rstd[:, :n], ALU.mult,
                        reads=[kx, "rstd"], writes=[f"tmpx{k % 2}"])
                    P.actf(h2[:, k, l0:l0 + n], tmpx[k % 2][:, :n], AF.Identity, bias=modv[:, 1, k, kind:kind + 1],
                           scale=A2[:, k, kind:kind + 1], reads=[f"tmpx{k % 2}", "modv", "A2"], writes=["h2"])
            for hidx, col in ((hf[0], lo), (hf[1], lo + (e1 - e0) - 1)):
                if hidx is not None:
                    P.v("dve", "tensor_scalar", h2[:, :, col], h2[:, :, col], hm[:, hidx:hidx + 1], None, ALU.mult,
                        reads=["h2", "hm"], writes=["h2"])
        P.release(m_s1)
        wus = [P.al([128, KC, 256], F32) for _ in range(2)]
        wub = [P.al([128, KC, 256], BF16) for _ in range(2)]
        wds = [P.al([128, NFF, 128], F32) for _ in range(2)]
        wdb = [P.al([128, NFF, 128], BF16) for _ in range(2)]
        tcv = [P.al([128, 512], F32) for _ in range(2)]
        tsl = [P.al([128, 512], F32) for _ in range(2)]
        ftiles = []
        for si, (kind, e0, e1, o0, o1, hf) in enumerate(segs):
            lo, oo = loc[si]
            for (c0, n) in _col_tiles(o0, o1, 410):
                ftiles.append((kind, lo + (c0 - e0), oo + (c0 - o0), n))
        it = 0
        for f in range(NFF):
            r = f % 2
            P.dma(wus[r][:, :, 0:128], wup[:, f * 128:(f + 1) * 128].rearrange("(k p) c -> p k c", p=128), writes=[f"wus{r}"])
            P.dma(wus[r][:, :, 128:256], wup[:, DFF + f * 128:DFF + (f + 1) * 128].rearrange("(k p) c -> p k c", p=128), writes=[f"wus{r}"])
            P.cast(wub[r][:, :, :], wus[r][:, :, :], reads=[f"wus{r}"], writes=[f"wub{r}"])
            for (kind, lc, oc, n) in ftiles:
                q = it % 2
                it += 1
                pu, puk = banks[q], f"bank{q}"
                pg_, pgk = banks[2 + q], f"bank{2 + q}"
                for k in range(KC):
                    P.mm(pu[:, :n], wub[r][:, k, 0:128], h2[:, k, lc:lc + n], start=(k == 0), stop=(k == KC - 1),
                         reads=[f"wub{r}", "h2"], writes=[puk])
                for k in range(KC):
                    P.mm(pg_[:, :n + 2], wub[r][:, k, 128:256], h2[:, k, lc - 1:lc + n + 1], start=(k == 0), stop=(k == KC - 1),
                         reads=[f"wub{r}", "h2"], writes=[pgk])
                tc_, tck = tcv[q], f"tcv{q}"
                ts_, tsk = tsl[q], f"tsl{q}"
                P.actf(tc_[:, :n], pg_[:, 1:n + 1], AF.Identity, scale=cwf[:, f, 1:2], reads=[pgk, "cwf"], writes=[tck])
                P.v("dve", "scalar_tensor_tensor", tc_[:, :n], pg_[:, 0:n], cwf[:, f, 0:1], tc_[:, :n], ALU.mult, ALU.add,
                    reads=[pgk, "cwf", tck], writes=[tck])
                P.v("dve", "scalar_tensor_tensor", tc_[:, :n], pg_[:, 2:n + 2], cwf[:, f, 2:3], tc_[:, :n], ALU.mult, ALU.add,
                    reads=[pgk, "cwf", tck], writes=[tck])
                P.actf(ts_[:, :n], tc_[:, :n], AF.Silu, reads=[tck], writes=[tsk])
                P.v("dve", "tensor_tensor", aT[:, f, oc:oc + n], ts_[:, :n], pu[:, :n], ALU.mult, reads=[tsk, puk], writes=[f"aT{gi}"])
        for dc in range(KC):
            r = dc % 2
            P.dma(wds[r][:, :, :], wdown[:, dc * 128:(dc + 1) * 128].rearrange("(f p) c -> p f c", p=128), writes=[f"wds{r}"])
            P.cast(wdb[r][:, :, :], wds[r][:, :, :], reads=[f"wds{r}"], writes=[f"wdb{r}"])
            for (kind, lc, oc, n) in ftiles:
                q = it % 2
                it += 1
                pd, pdk = banks[4 + q], f"bank{4 + q}"
                for f in range(NFF):
                    P.mm(pd[:, :n], wdb[r][:, f, :], aT[:, f, oc:oc + n], start=(f == 0), stop=(f == NFF - 1),
                         reads=[f"wdb{r}", f"aT{gi}"], writes=[pdk])
                P.v("dve", "scalar_tensor_tensor", xres[:, dc, lc:lc + n], pd[:, :n], modv[:, 3, dc, kind:kind + 1],
                    xres[:, dc, lc:lc + n], ALU.mult, ALU.add, reads=[pdk, "modv", kx], writes=[kx])
        if final:
            sqf = P.al([128, KC, 512], BF16)
            rstd = P.al([128, 512], F32)
            tmpn = P.al([128, 512], F32)
            tmpx = [P.al([128, 512], F32) for _ in range(2)]
            ost = [P.al([128, 512], F32) for _ in range(2)]
            (kind, e0, e1, o0, o1, hf) = segs[0]
            lo = loc[0][0]
            oi = 0
            for (c0, n) in _col_tiles(o0, o1):
                l0 = lo + (c0 - e0)
                P.actf(sqf[:, :, :n], xres[:, :, l0:l0 + n], AF.Square, reads=[kx], writes=["sqf"])
                for k in range(KC):
                    P.mm(banks[7][:, :n], ones_b[:], sqf[:, k, :n], start=(k == 0), stop=(k == KC - 1), reads=["ones", "sqf"], writes=["bank7"])
                _rms_rstd(P, banks[7], n, D, rstd, "bank7", "rstdf", tmpn, "tmpnf")
                for k in range(KC):
                    P.v("dve", "tensor_tensor", tmpx[k % 2][:, :n], xres[:, k, l0:l0 + n], rstd[:, :n], ALU.mult,
                        reads=[kx, "rstdf"], writes=[f"tmpxf{k % 2}"])
                    o_, ok_ = ost[oi % 2], f"ostf{oi % 2}"
                    oi += 1
                    P.actf(o_[:, :n], tmpx[k % 2][:, :n], AF.Identity, scale=fn[:, k:k + 1], reads=[f"tmpxf{k % 2}", "fn"], writes=[ok_])
                    P.dma(xTo[k * 128:(k + 1) * 128, c0 - 1:c0 - 1 + n], o_[:, :n], reads=[ok_], writes=["xTo"])
        else:
            for si, (kind, e0, e1, o0, o1, hf) in enumerate(segs):
                lo = loc[si][0]
                l0 = lo + (o0 - e0)
                dst0 = (o0 - 1) if kind == 0 else (NLAT + (o0 - CX0 - 1))
                P.dma(XSo[:, dst0:dst0 + (o1 - o0)].rearrange("(k p) n -> p k n", p=128), xres[:, :, l0:l0 + (o1 - o0)],
                      reads=[kx], writes=[f"XS{(l + 1) % 2}"])
                for (ecol, bc) in ((1, 0), (NLAT, 1), (CX0 + 1, 2), (CX1 - 2, 3)):
                    if o0 <= ecol < o1 and ((kind == 0) == (ecol < CX0)):
                        P.dma(E["XBND"][:, bc:bc + 1].rearrange("(k p) o -> p k o", p=128), xres[:, :, lo + (ecol - e0):lo + (ecol - e0) + 1],
                              reads=[kx], writes=["XBND"], allow_slow_non_contiguous=True)
        P.release(m_g)
    P.release(m_top0)


LAYER_W = [
    ("wmodA", [D, 2 * D]), ("bmodA", [128, 16]), ("wmodC", [D, 4 * D]), ("bmodC", [128, 32]),
    ("norm1", [128, KC]), ("norm2", [128, KC]), ("win", [D, P_IN]), ("winsw", [D, 32]), ("qn", [128, 2]), ("kvn", [128, 1]),
    ("wuq", [256, 384]), ("wuqsw", [256, 384]), ("wukv", [128, 512]), ("snorm", [128, 2]), ("wglu", [256, 256]),
    ("wout", [D, D]), ("wup", [D, 2 * DFF]), ("convw", [128, NFF, 3]), ("wdown", [DFF, D]),
    ("l_gbias", [1, 4]), ("l_norm", [1, 64]), ("s_convw", [128, 3, 4]), ("s_dtbias", [1, 2]), ("s_alog", [1, 2]), ("s_dskip", [1, 1]),
    ("s5_are", [128, 4]), ("s5_aim", [128, 4]), ("s5_ldt", [128, 4]), ("s5_bre", [128, 2, 32]), ("s5_bim", [128, 2, 32]),
    ("s5_cre", [128, 2, 32]), ("s5_cim", [128, 2, 32]), ("s5_d", [32, 2]),
]
GLOBAL_IN = [("x0", [D, NT], F32), ("cvec", [128, KC, 2], F32), ("ropeC", [32, NT], F32), ("ropeS", [32, NT], F32),
             ("fnorm", [128, KC], F32), ("hmask", [128, 4], F32), ("idxB", [128, NIB], I32), ("idxC", [128, 7 * KC], I32),
             ("idxX", [128, 2 * KC], I32)]
RG = [[0, 1, 2, 3], [4, 5, 6, 7]]


def _allgather(P, src, dst, ksrc, kdst, rows=None, chunks=None, chunk_keys=False):
    R = src.shape[0]
    rows = rows or R
    assert R % rows == 0
    for c in (range(R // rows) if chunks is None else chunks):
        a, b = src[c * rows:(c + 1) * rows, :], dst[4 * c * rows:4 * (c + 1) * rows, :]
        P.add("pool", (lambda a, b: (lambda e: e.collective_compute("AllGather", ALU.bypass, replica_groups=RG,
                                                                      ins=[a.opt()], outs=[b.opt()])))(a, b),
              reads=[ksrc], writes=[(f"{kdst}{c}" if chunk_keys else kdst)], cc=True)


def build_fused(NL=4):
    nc = bass.Bass("TRN2", target_bir_lowering=False)
    E = {}
    for name, shape, dt_ in GLOBAL_IN:
        E[name + "_d"] = _dram(nc, name, shape, dtype=dt_)
    for name, shape in LAYER_W:
        E[name] = _dram(nc, name, [NL] + shape)
    E["xTo"] = _dram(nc, "xTo", [D, NLAT], kind="ExternalOutput")
    for name, shape in (("EXA", [RA_PAD // 512, 512]), ("GEXA", [4 * RA_PAD // 512, 512]), ("EXY", [RY_PAD // 512, 512]),
                        ("GEXY", [4 * RY_PAD // 512, 512]), ("XS0", [D, NT]), ("XS1", [D, NT]), ("XBND", [D, 4]), ("GXB", [4 * D, 4])):
        E[name] = nc.dram_tensor(name, shape, F32).ap()
    E["ropeC"], E["ropeS"], E["fnorm"], E["hmask"] = E["ropeC_d"], E["ropeS_d"], E["fnorm_d"], E["hmask_d"]
    P = Prog(nc)
    C = b_consts(P)
    E["banks"], E["ident"], E["ones_b"] = C["banks"], C["ident"], C["ones_b"]
    cs = P.sb([128, KC, 2], F32, "cs")
    P.dma(cs[:], E["cvec_d"], writes=["cs"])
    P.actf(cs[:], cs[:], AF.Silu, reads=["cs"], writes=["cs"])
    E["cs"] = cs
    for nm, w in (("idxB", NIB), ("idxC", 7 * KC), ("idxX", 2 * KC)):
        t = P.sb([128, w], I32, nm)
        P.dma(t[:], E[nm + "_d"], writes=[nm])
        E[nm] = t
    P.arena_init(196 * 1024)
    mz = P.mark()
    zt = P.al([128, 8192], F32)
    P.v("pool", "memset", zt[:], 0.0, writes=["zeros"])
    nrow = RY_PAD // 512
    for j0 in range(0, nrow, 128 * 16):
        nr_ = min(128 * 16, nrow - j0)
        if nr_ % 128 == 0:
            P.dma(E["EXY"][j0:j0 + nr_, :].rearrange("(p a) w -> p (a w)", p=128), zt[:, 0:(nr_ // 128) * 512], reads=["zeros"], writes=["EXY"])
        else:
            for j1 in range(j0, j0 + nr_, 128):
                n1 = min(128, j0 + nr_ - j1)
                P.dma(E["EXY"][j1:j1 + n1, :], zt[0:n1, 0:512], reads=["zeros"], writes=["EXY"])
    P.release(mz)
    E["XS"] = [E["XS0"], E["XS1"]]
    P.dma(E["XS"][0], E["x0_d"], writes=["XS0"])
    for bc, col in ((0, 0), (1, NLAT - 1), (2, NLAT), (3, NT - 1)):
        P.dma(E["XBND"][:, bc:bc + 1], E["x0_d"][:, col:col + 1], writes=["XBND"], allow_slow_non_contiguous=True)
    for l in range(NL):
        phase_A(P, E, l)
        order = []
        for ops_ in (("s5_u0", "s5_u1"), ("l_q", "l_k", "l_v", "l_g", "l_grow1", "l_grow3", "l_o"),
                     ("s_dt", "s_x", "s_B", "s_C", "s_dtrow0", "s_dtrow1", "s_z"), ("m_KT", "m_QT", "m_V")):
            for op_ in ops_:
                for c_ in _op_chunks(op_):
                    if c_ not in order:
                        order.append(c_)
        order += [c_ for c_ in range(NCHA) if c_ not in order]
        first = [c_ for c_ in order if c_ in set(_op_chunks("s5_u0") + _op_chunks("s5_u1"))]
        rest = [c_ for c_ in order if c_ not in first]
        _allgather(P, E["EXA"], E["GEXA"], "PT", "GEXA", rows=CHA // 512, chunks=first, chunk_keys=True)
        b_s5(P, E, l, C, after_first_gather=lambda: _allgather(P, E["EXA"], E["GEXA"], "PT", "GEXA", rows=CHA // 512,
                                                               chunks=rest, chunk_keys=True))
        b_mlstm(P, E, l, C)
        b_ssd(P, E, l, C)
        b_mla(P, E, l, C)
        _allgather(P, E["EXY"], E["GEXY"], "EXY", "GEXY", rows=CHY // 512)
        _allgather(P, E["XBND"], E["GXB"], "XBND", "GXB")
        phase_C(P, E, l, final=(l == NL - 1))
    P.emit()
    return nc


def _rope_tables():
    rows = SEQ // 64
    row = np.broadcast_to(np.arange(rows)[:, None], (rows, 64)).reshape(-1).astype(np.float32)
    col = np.broadcast_to(np.arange(64)[None, :], (rows, 64)).reshape(-1).astype(np.float32)
    inv = (10000.0 ** (-np.arange(8, dtype=np.float32) / 8)).astype(np.float32)
    ang = np.concatenate([row[:, None] * inv, col[:, None] * inv], axis=-1)
    return np.cos(ang).astype(np.float32), np.sin(ang).astype(np.float32)


def _c32(a):
    return np.ascontiguousarray(a, dtype=np.float32)


def _index_tables(h, q):
    OOB = -1
    g = h // 2
    SS = R_SS
    p = np.arange(128)
    rows = {
        "m_QT": np.where(p < 96, R_Q + 96 * h + p, OOB),
        "m_KT": np.where(p < 64, R_KN + 64 * h + p, np.where(p < 96, R_KR + (p - 64), OOB)),
        "m_V": np.where(p < 64, R_V + 64 * h + p, OOB),
        "l_q": np.where(p < 64, 0 + 64 * h + p, OOB), "l_k": np.where(p < 64, 256 + 64 * h + p, OOB),
        "l_v": np.where(p < 64, 512 + 64 * h + p, OOB), "l_o": np.where(p < 64, 768 + 64 * h + p, OOB),
        "l_g": np.where(p < 4, 1024 + 4 * p + h, OOB),
        "l_grow1": np.full(128, 1024 + 4 * 1 + h), "l_grow3": np.full(128, 1024 + 4 * 3 + h),
        "s_x": np.where(p < 64, SS + 256 + 64 * h + p, OOB), "s_B": SS + 512 + 128 * g + p, "s_C": SS + 768 + 128 * g + p,
        "s_z": np.where(p < 64, SS + 64 * h + p, OOB), "s_dt": np.where(p < 2, SS + 1024 + 4 * p + h, OOB),
        "s_dtrow0": np.full(128, SS + 1024 + h), "s_dtrow1": np.full(128, SS + 1024 + 4 + h),
        "s5_u0": np.where(p < 32, SS + 1032 + 64 * h + p, OOB), "s5_u1": np.where(p < 32, SS + 1032 + 64 * h + 32 + p, OOB),
    }
    idxB = np.zeros((128, NIB), np.int64)
    for name, r in rows.items():
        for qp in range(4):
            for pc, w in enumerate((NLAT, NCTX)):
                off = (A_LAT_OFF, A_CTX_OFF)[pc]
                rr = np.maximum(r, 0)
                idxB[:, (OPIDX[name] * 4 + qp) * 2 + pc] = _gaddr(off + rr * w, qp, CHA) // w
    idxC = np.zeros((128, 7 * KC), np.int64)
    for pi, (pc0, pw) in enumerate(Y_PIECES):
        for k in range(KC):
            m, hp = k // 2, 2 * (k % 2) + p // 64
            idxC[:, pi * KC + k] = _gaddr(Y_OFF[pi] + (q * 256 + m * 64 + (p % 64)) * pw, hp, CHY) // pw
    idxX = np.zeros((128, 2 * KC), np.int64)
    for k in range(KC):
        for side, qn_ in ((0, q - 1), (1, q + 1)):
            idxX[:, 2 * k + side] = ((qn_ if 0 <= qn_ <= 3 else q) * D + k * 128 + p)
    return idxB.astype(np.int32), idxC.astype(np.int32), idxX.astype(np.int32)


def _layer_weights(inp, NL, h):
    g = h // 2
    W = {name: [] for name, _ in LAYER_W}
    for l in range(NL):
        win = inp['w_in'][l]
        W["wmodA"].append(inp['w_mod'][l][:, 0:2 * D]); W["bmodA"].append(inp['b_mod'][l][0:2 * D].reshape(16, 128).T)
        W["wmodC"].append(inp['w_mod'][l][:, 2 * D:6 * D]); W["bmodC"].append(inp['b_mod'][l][2 * D:6 * D].reshape(32, 128).T)
        W["norm1"].append(inp['norm1'][l].reshape(8, 128).T); W["norm2"].append(inp['norm2'][l].reshape(8, 128).T)
        W["win"].append(win); W["winsw"].append(np.concatenate([win[:, 1440:1456], win[:, 1424:1440]], axis=1))
        W["qn"].append(inp['mla_q_norm'][l].reshape(2, 128).T); W["kvn"].append(inp['mla_kv_norm'][l].reshape(1, 128).T)
        wuq = inp['mla_w_uq'][l]; w4 = wuq.reshape(256, 4, 96)
        W["wuq"].append(wuq); W["wuqsw"].append(np.concatenate([w4[:, :, :64], w4[:, :, 80:96], w4[:, :, 64:80]], axis=2).reshape(256, 384))
        W["wukv"].append(inp['mla_w_ukv'][l].reshape(128, 4, 2, 64).transpose(0, 2, 1, 3).reshape(128, 512))
        W["snorm"].append(inp['ssd_norm'][l].reshape(2, 128).T); W["wglu"].append(inp['s5_w_glu'][l]); W["wout"].append(inp['w_out'][l])
        W["wup"].append(inp['ffn_w_up'][l]); W["convw"].append(inp['ffn_conv_w'][l].reshape(3, NFF, 128).transpose(2, 1, 0))
        W["wdown"].append(inp['ffn_w_down'][l])
        W["l_gbias"].append(inp['ml_gate_bias'][l][:, h].reshape(1, 4)); W["l_norm"].append(inp['ml_norm'][l][64 * h:64 * h + 64].reshape(1, 64))
        cw = np.zeros((128, 3, 4), np.float32)
        for blk, (ch0, n) in enumerate(((64 * h, 64), (256 + 128 * g, 128), (512 + 128 * g, 128))):
            cw[:n, blk, 0:3] = inp['ssd_conv_w'][l][:, ch0:ch0 + n].T
            cw[:n, blk, 3] = inp['ssd_conv_b'][l][ch0:ch0 + n]
        W["s_convw"].append(cw)
        W["s_dtbias"].append(inp['ssd_dt_bias'][l][:, h].reshape(1, 2)); W["s_alog"].append(inp['ssd_a_log'][l][:, h].reshape(1, 2))
        W["s_dskip"].append(inp['ssd_d'][l][h].reshape(1, 1))
        are = np.zeros((128, 4), np.float32); aim = np.zeros((128, 4), np.float32); ldt = np.zeros((128, 4), np.float32)
        bre = np.zeros((128, 2, 32), np.float32); bim = np.zeros((128, 2, 32), np.float32)
        cre = np.zeros((128, 2, 32), np.float32); cim = np.zeros((128, 2, 32), np.float32)
        dsk = np.zeros((32, 2), np.float32)
        for gp in range(2):
            for g2 in range(2):
                gg = 4 * h + 2 * gp + g2
                ps = slice(64 * g2, 64 * g2 + 64)
                for d in range(2):
                    are[ps, gp * 2 + d] = inp['s5_a_re'][l][d, gg]; aim[ps, gp * 2 + d] = inp['s5_a_im'][l][d, gg]
                    ldt[ps, gp * 2 + d] = inp['s5_log_dt'][l][d, gg]
                bre[ps, gp, 16 * g2:16 * g2 + 16] = inp['s5_b_re'][l][gg]; bim[ps, gp, 16 * g2:16 * g2 + 16] = inp['s5_b_im'][l][gg]
                cre[ps, gp, 16 * g2:16 * g2 + 16] = inp['s5_c_re'][l][gg].T; cim[ps, gp, 16 * g2:16 * g2 + 16] = inp['s5_c_im'][l][gg].T
                dsk[16 * g2:16 * g2 + 16, gp] = inp['s5_d'][l][16 * gg:16 * gg + 16]
        for nm, v in (("s5_are", are), ("s5_aim", aim), ("s5_ldt", ldt), ("s5_bre", bre), ("s5_bim", bim), ("s5_cre", cre), ("s5_cim", cim), ("s5_d", dsk)):
            W[nm].append(v)
    return {k: _c32(np.stack(v)) for k, v in W.items()}


def _fused_inputs(inp, NL=4):
    cos, sin = _rope_tables()
    maps = []
    shared = {}
    for k in range(8):
        b, q = k // 4, k % 4
        if q not in shared:
            shared[q] = _layer_weights(inp, NL, q)
        m = dict(shared[q])
        m["x0"] = _c32(np.concatenate([inp['x'][b, NLAT * q:NLAT * (q + 1)].T, inp['ctx'][b, NCTX * q:NCTX * (q + 1)].T], axis=1))
        m["cvec"] = _c32(np.stack([inp['c'][b].reshape(8, 128).T, inp['c_ctx'].reshape(8, 128).T], axis=-1))
        c_l = cos[NLAT * q:NLAT * (q + 1)].T
        s_l = sin[NLAT * q:NLAT * (q + 1)].T
        m["ropeC"] = _c32(np.concatenate([np.concatenate([c_l, c_l], 0), np.ones((32, NCTX), np.float32)], axis=1))
        m["ropeS"] = _c32(np.concatenate([np.concatenate([-s_l, s_l], 0), np.zeros((32, NCTX), np.float32)], axis=1))
        m["fnorm"] = _c32(inp['final_norm'].reshape(8, 128).T)
        m["hmask"] = _c32(np.broadcast_to(np.array([q > 0, q < 3, q > 0, q < 3], np.float32), (128, 4)))
        m["idxB"], m["idxC"], m["idxX"] = _index_tables(q, q)
        maps.append(m)
    return maps


_PROG = {}


def kernel(**inputs):
    inp = {k: np.asarray(v) for k, v in inputs.items()}
    if "f" not in _PROG:
        _PROG["f"] = build_fused(4)
    res = run_bass_kernel_spmd(_PROG["f"], _fused_inputs(inp, 4), core_ids=list(range(8)))
    out = np.stack([np.concatenate([res.results[4 * b + q]["xTo"].T for q in range(4)], axis=0) for b in range(2)])
    return np.ascontiguousarray(out, dtype=np.float32)
```
